# Optimizing a Trainium2 kernel written in Bass

```python
import math
import jax, jax.numpy as jnp
from jax import lax
import numpy as np

D_MODEL = 1024
BATCH = 8
SEQ = 2048
DEPTH = 1
DEC_BATCH = 128
DEC_SEQ = 4
PAST_LEN = 16384
PAGE_SIZE = 128

CONV_DIM = D_MODEL
CONV_WIDTH = 31
SSM_EXPAND = 2
SSM_DIM = SSM_EXPAND * D_MODEL
SSM_HEAD_DIM = 64
SSM_HEADS = SSM_DIM // SSM_HEAD_DIM
SSM_GROUPS = 8
SSM_STATE = 128
SSM_CONV_WIDTH = 4
SSM_CHUNK = 128
SSM_XBC = SSM_DIM + 2 * SSM_GROUPS * SSM_STATE
IN_SIZES = (CONV_DIM, CONV_DIM, CONV_DIM, SSM_DIM, SSM_XBC, SSM_HEADS, D_MODEL, D_MODEL)
IN_PROJ_DIM = 3 * CONV_DIM + SSM_DIM + SSM_XBC + SSM_HEADS + 2 * D_MODEL
EPS = 1e-6

kernel_name = 'hybrid_conformer_ssd_decode_step'


def split_cols(t, sizes):
    offs = []
    acc = 0
    for s in sizes[:-1]:
        acc += s
        offs.append(acc)
    return jnp.split(t, offs, axis=-1)


def rmsnorm(x, g):
    x32 = x.astype(jnp.float32)
    r = lax.rsqrt(jnp.mean(x32 * x32, axis=-1, keepdims=True) + EPS)
    return (x32 * r).astype(x.dtype) * g


def layernorm(x, g, b):
    x32 = x.astype(jnp.float32)
    mu = jnp.mean(x32, axis=-1, keepdims=True)
    var = jnp.mean(jnp.square(x32 - mu), axis=-1, keepdims=True)
    return ((x32 - mu) * lax.rsqrt(var + EPS)).astype(x.dtype) * g + b


def causal_dwconv(x, buf, w, b):
    width, ch = w.shape
    xp = jnp.concatenate([buf.astype(x.dtype), x], axis=1)
    y = lax.conv_general_dilated(xp, w[:, None, :].astype(x.dtype), window_strides=(1,),
                                 padding='VALID', dimension_numbers=('NWC', 'WIO', 'NWC'),
                                 feature_group_count=ch)
    return y + b, xp[:, -(width - 1):]


def ssd_scan(x, dt, a, bm, cm, h0, chunk):
    bsz, l, nh, p = x.shape
    g, n = bm.shape[2], bm.shape[3]
    r = nh // g
    q = chunk
    c = l // q
    f32 = jnp.float32
    xc = x.reshape(bsz, c, q, g, r, p).astype(f32)
    dtc = dt.reshape(bsz, c, q, g, r).astype(f32)
    bc = bm.reshape(bsz, c, q, g, n).astype(f32)
    cc = cm.reshape(bsz, c, q, g, n).astype(f32)
    cum = jnp.cumsum(dtc * a.reshape(g, r).astype(f32), axis=2)
    xdt = xc * dtc[..., None]
    seg = cum[:, :, :, None] - cum[:, :, None]
    mask = jnp.tril(jnp.ones((q, q), dtype=bool))[:, :, None, None]
    decay = jnp.exp(jnp.where(mask, seg, -jnp.inf))
    cb = jnp.einsum('bcqgn,bckgn->bcqkg', cc, bc)
    y_intra = jnp.einsum('bcqkgr,bckgrp->bcqgrp', cb[..., None] * decay, xdt)
    decay_end = jnp.exp(cum[:, :, -1:] - cum)
    st = jnp.einsum('bcqgn,bcqgrp->bcgrpn', bc, xdt * decay_end[..., None])
    chunk_decay = jnp.exp(cum[:, :, -1])

    def step(h_prev, inp):
        s, d = inp
        return d[..., None, None] * h_prev + s, h_prev

    h_last, h_starts = lax.scan(step, h0.astype(f32).reshape(bsz, g, r, p, n),
                                (jnp.moveaxis(st, 1, 0), jnp.moveaxis(chunk_decay, 1, 0)))
    h_starts = jnp.moveaxis(h_starts, 0, 1)
    y_inter = jnp.einsum('bcqgn,bcgrpn->bcqgrp', cc, h_starts) * jnp.exp(cum)[..., None]
    y = (y_intra + y_inter).reshape(bsz, l, nh, p).astype(x.dtype)
    return y, h_last.reshape(bsz, nh, p, n).astype(x.dtype)


def layer(x, c, conf_buf, ssm_buf, ssm_h, w_ada, b_ada, g_pre, g_post, w_in,
          conf_dw_w, conf_dw_b, conf_ln_g, conf_ln_b, w_conf_out,
          ssm_dw_w, ssm_dw_b, ssm_dt_bias, ssm_a_log, ssm_d, ssm_norm_g, w_ssm_out, w_o):
    bsz, L, _ = x.shape
    shift, scale, gate = jnp.split(jax.nn.silu(c) @ w_ada + b_ada, 3, axis=-1)
    h = rmsnorm(x, g_pre) * (1.0 + scale[:, None]) + shift[:, None]
    proj = h @ w_in
    conv_val, conv_glu, conv_gate, z, xbc, dt_raw, m_conv, m_ssm = split_cols(proj, IN_SIZES)
    u = conv_val * jax.nn.sigmoid(conv_glu)
    u, new_conf_buf = causal_dwconv(u, conf_buf, conf_dw_w, conf_dw_b)
    u = jax.nn.silu(layernorm(u, conf_ln_g, conf_ln_b)) * jax.nn.silu(conv_gate)
    branch_conv = u @ w_conf_out
    xbc, new_ssm_buf = causal_dwconv(xbc, ssm_buf, ssm_dw_w, ssm_dw_b)
    xbc = jax.nn.silu(xbc)
    xs, bm, cm = split_cols(xbc, (SSM_DIM, SSM_GROUPS * SSM_STATE, SSM_GROUPS * SSM_STATE))
    xs = xs.reshape(bsz, L, SSM_HEADS, SSM_HEAD_DIM)
    bm = bm.reshape(bsz, L, SSM_GROUPS, SSM_STATE)
    cm = cm.reshape(bsz, L, SSM_GROUPS, SSM_STATE)
    dt = jax.nn.softplus(dt_raw + ssm_dt_bias)
    a = -jnp.exp(ssm_a_log.astype(jnp.float32))
    chunk = SSM_CHUNK if L % SSM_CHUNK == 0 else L
    y, new_h = ssd_scan(xs, dt, a, bm, cm, ssm_h, chunk)
    y = y + ssm_d[:, None] * xs
    y = y.reshape(bsz, L, SSM_DIM) * jax.nn.silu(z)
    y = rmsnorm(y.reshape(bsz, L, SSM_GROUPS, SSM_DIM // SSM_GROUPS), 1.0).reshape(bsz, L, SSM_DIM) * ssm_norm_g
    branch_ssm = y @ w_ssm_out
    merged = jax.nn.sigmoid(m_conv) * branch_conv + jax.nn.sigmoid(m_ssm) * branch_ssm
    o = merged @ w_o
    return x + gate[:, None] * rmsnorm(o, g_post), new_conf_buf, new_ssm_buf, new_h


def setup_inputs(seed: int = 0) -> dict:
    key = jax.random.key(seed)
    ks = jax.random.split(key, 26)
    f32 = jnp.float32

    def nrm(k, shape, s):
        return jax.random.normal(k, shape, f32) * s

    dt0 = jnp.exp(jax.random.uniform(ks[19], (DEPTH, SSM_HEADS), f32,
                                     minval=math.log(1e-3), maxval=math.log(1e-1)))
    return {
        'x_prompt': nrm(ks[0], (BATCH, SEQ, D_MODEL), 1.0),
        'x_sample': nrm(ks[1], (DEC_BATCH, DEC_SEQ, D_MODEL), 1.0),
        'c_prompt': nrm(ks[2], (BATCH, D_MODEL), 1.0),
        'c_sample': nrm(ks[3], (DEC_BATCH, D_MODEL), 1.0),
        'state_conf_conv': nrm(ks[4], (DEPTH, DEC_BATCH, CONV_WIDTH - 1, CONV_DIM), 1.0),
        'state_ssm_conv': nrm(ks[5], (DEPTH, DEC_BATCH, SSM_CONV_WIDTH - 1, SSM_XBC), 1.0),
        'state_ssm': nrm(ks[6], (DEPTH, DEC_BATCH, SSM_HEADS, SSM_HEAD_DIM, SSM_STATE), 0.1),
        'w_ada': nrm(ks[7], (DEPTH, D_MODEL, 3 * D_MODEL), 0.1 * D_MODEL ** -0.5),
        'b_ada': nrm(ks[8], (DEPTH, 3 * D_MODEL), 0.01),
        'g_pre': 1.0 + nrm(ks[9], (DEPTH, D_MODEL), 0.01),
        'g_post': 1.0 + nrm(ks[10], (DEPTH, D_MODEL), 0.01),
        'w_in': nrm(ks[11], (DEPTH, D_MODEL, IN_PROJ_DIM), D_MODEL ** -0.5),
        'conf_dw_w': nrm(ks[12], (DEPTH, CONV_WIDTH, CONV_DIM), CONV_WIDTH ** -0.5),
        'conf_dw_b': nrm(ks[13], (DEPTH, CONV_DIM), 0.01),
        'conf_ln_g': 1.0 + nrm(ks[14], (DEPTH, CONV_DIM), 0.01),
        'conf_ln_b': nrm(ks[15], (DEPTH, CONV_DIM), 0.01),
        'w_conf_out': nrm(ks[16], (DEPTH, CONV_DIM, D_MODEL), CONV_DIM ** -0.5),
        'ssm_dw_w': nrm(ks[17], (DEPTH, SSM_CONV_WIDTH, SSM_XBC), SSM_CONV_WIDTH ** -0.5),
        'ssm_dw_b': nrm(ks[18], (DEPTH, SSM_XBC), 0.01),
        'ssm_dt_bias': dt0 + jnp.log(-jnp.expm1(-dt0)),
        'ssm_a_log': jnp.log(jax.random.uniform(ks[20], (DEPTH, SSM_HEADS), f32, minval=1.0, maxval=16.0)),
        'ssm_d': 1.0 + nrm(ks[21], (DEPTH, SSM_HEADS), 0.01),
        'ssm_norm_g': 1.0 + nrm(ks[22], (DEPTH, SSM_DIM), 0.01),
        'w_ssm_out': nrm(ks[23], (DEPTH, SSM_DIM, D_MODEL), SSM_DIM ** -0.5),
        'w_o': nrm(ks[24], (DEPTH, D_MODEL, D_MODEL), D_MODEL ** -0.5),
    }


def reference(x_prompt, x_sample, c_prompt, c_sample, state_conf_conv, state_ssm_conv, state_ssm,
              w_ada, b_ada, g_pre, g_post, w_in, conf_dw_w, conf_dw_b, conf_ln_g, conf_ln_b, w_conf_out,
              ssm_dw_w, ssm_dw_b, ssm_dt_bias, ssm_a_log, ssm_d, ssm_norm_g, w_ssm_out, w_o):
    xp = x_prompt
    xs = x_sample
    bp = x_prompt.shape[0]
    pc_conf, pc_ssmc, pc_ssm = [], [], []
    sc_conf, sc_ssmc, sc_ssm = [], [], []
    for l in range(DEPTH):
        wl = (w_ada[l], b_ada[l], g_pre[l], g_post[l], w_in[l], conf_dw_w[l], conf_dw_b[l],
              conf_ln_g[l], conf_ln_b[l], w_conf_out[l], ssm_dw_w[l], ssm_dw_b[l], ssm_dt_bias[l],
              ssm_a_log[l], ssm_d[l], ssm_norm_g[l], w_ssm_out[l], w_o[l])
        zero_conf = jnp.zeros((bp, CONV_WIDTH - 1, CONV_DIM), xp.dtype)
        zero_ssmc = jnp.zeros((bp, SSM_CONV_WIDTH - 1, SSM_XBC), xp.dtype)
        zero_h = jnp.zeros((bp, SSM_HEADS, SSM_HEAD_DIM, SSM_STATE), xp.dtype)
        xp, a1, a2, a3 = layer(xp, c_prompt, zero_conf, zero_ssmc, zero_h, *wl)
        xs, b1, b2, b3 = layer(xs, c_sample, state_conf_conv[l], state_ssm_conv[l], state_ssm[l], *wl)
        pc_conf.append(a1)
        pc_ssmc.append(a2)
        pc_ssm.append(a3)
        sc_conf.append(b1)
        sc_ssmc.append(b2)
        sc_ssm.append(b3)
    return (xp, xs, jnp.stack(pc_conf), jnp.stack(pc_ssmc), jnp.stack(pc_ssm),
            jnp.stack(sc_conf), jnp.stack(sc_ssmc), jnp.stack(sc_ssm))
```

```python
import contextlib
import numpy as np
import concourse.bass as bass
import concourse.mybir as mybir
from concourse.bass_utils import run_bass_kernel_spmd

F32 = mybir.dt.float32
BF16 = mybir.dt.bfloat16
AF = mybir.ActivationFunctionType
ALU = mybir.AluOpType

D = 1024
NCORE = 8
SEQ = 2048
TB = 512
IN_DIM = 11296
OFF_VAL, OFF_GLU, OFF_GATE, OFF_Z, OFF_XS, OFF_B, OFF_C, OFF_DT, OFF_MC, OFF_MS = (
    0, 1024, 2048, 3072, 5120, 7168, 8192, 9216, 9248, 10272)
EPS = 1e-6
NDMA_SEM = 32
NSW_SEM = 16


class View:
    def __init__(self, ap, keys):
        self.ap = ap
        self.keys = keys

    def v(self, ap):
        return View(ap, self.keys)

    def __getitem__(self, idx):
        return View(self.ap[idx], self.keys)


class Tile:
    def __init__(self, ap, keys):
        self.ap = ap
        self.keys = list(keys)

    def __getitem__(self, idx):
        return View(self.ap[idx], self.keys)

    def v(self, ap):
        return View(ap, self.keys)

    def all(self):
        return View(self.ap, self.keys)


class Op:
    __slots__ = ("idx", "eng", "fn", "dma", "raw", "war", "needed", "sig", "prev_dma", "cost", "xfer", "tbl", "tag",
                 "t0", "t1")

    def __init__(self, idx, eng, fn, dma):
        self.idx, self.eng, self.fn, self.dma = idx, eng, fn, dma
        self.cost = 0.1
        self.xfer = 0.0
        self.tbl = None
        self.raw, self.war = set(), set()
        self.needed = False
        self.sig = None
        self.prev_dma = None


def _ap(x):
    return x.ap if isinstance(x, View) else x


def _fsz(x):
    sh = _ap(x).shape
    n = 1
    for d in sh[1:]:
        n *= d
    return n


def _nbytes(x):
    a = _ap(x)
    n = 1
    for d in a.shape:
        n *= d
    return n * (4 if a.dtype == F32 else 2)


_TBL = {}


class Prog:
    def __init__(self, nc, stack):
        self.nc = nc
        self.stack = stack
        self.ops = []
        self.state = {}
        self.pstate = {}
        self.dma_ops = []

    def sbt(self, name, shape, dtype):
        h = self.stack.enter_context(self.nc.sbuf_tensor(name, list(shape), dtype))
        return Tile(h[:], [(name,)])

    def make_arena(self, nslots):
        self.arena = self.stack.enter_context(self.nc.sbuf_tensor("arena", [128, nslots * 512], F32))
        self.nslots = nslots

    def at(self, slot0, nslots, dtype, pattern=None, **kw):
        assert slot0 + nslots <= self.nslots
        ap = self.arena[:, slot0 * 512:(slot0 + nslots) * 512]
        if dtype == BF16:
            ap = ap.bitcast(BF16)
        if pattern is not None:
            ap = ap.rearrange(pattern, **kw)
        return Tile(ap, [("A", s) for s in range(slot0, slot0 + nslots)])

    def make_psum(self):
        self.psum = []
        for i in range(8):
            h = self.stack.enter_context(self.nc.psum_tensor("ps%d" % i, [128, 512], F32))
            self.psum.append(h)

    def ps(self, bank, half=None):
        ap = self.psum[bank][:]
        if half is None:
            return Tile(ap, [("ps", bank)])
        return Tile(ap[:, half * 256:(half + 1) * 256], [("ps", bank)])

    def psb(self, bank, half=None):
        ap = self.psum[bank][:].bitcast(BF16)
        if half is None:
            return Tile(ap, [("ps", bank)])
        return Tile(ap[:, half * 512:(half + 1) * 512], [("ps", bank)])

    def op(self, eng, fn, r=(), w=(), dma=False, grp=None):
        o = Op(len(self.ops), eng, fn, dma)
        o.tag = getattr(self, "tag", "")
        rk, wk = [], []
        for x in r:
            if isinstance(x, (View, Tile)):
                rk.extend(x.keys)
        for x in w:
            if isinstance(x, (View, Tile)):
                wk.extend(x.keys)
        prk = [k for k in rk if k[0] == "ps"]
        pwk = [k for k in wk if k[0] == "ps"]
        rk = [k for k in rk if k[0] != "ps"]
        wk = [k for k in wk if k[0] != "ps"]
        for k in prk:
            st = self.pstate.get(k)
            if st is not None:
                (o.raw if st[1] else o.war).add(st[0])
        for k in pwk:
            st = self.pstate.get(k)
            if st is not None:
                (o.raw if st[1] else o.war).add(st[0])
        for k in prk:
            if k not in pwk:
                self.pstate[k] = [o.idx, False]
        for k in pwk:
            self.pstate[k] = [o.idx, True]
        for k in rk:
            st = self.state.get(k)
            if st is not None:
                o.raw.update(st[0])
        for k in wk:
            st = self.state.get(k)
            if st is not None:
                if grp is not None and st[2] == grp:
                    o.raw.update(st[3])
                    o.war.update(st[4])
                    o.war.update(st[1])
                else:
                    o.raw.update(st[0])
                    o.war.update(st[1])
        for k in rk:
            self.state.setdefault(k, [[], [], None, [], []])[1].append(o.idx)
        for k in wk:
            st = self.state.get(k)
            if st is not None and grp is not None and st[2] == grp:
                st[0].append(o.idx)
            else:
                pw = st[0] if st is not None else []
                pr = st[1] if st is not None else []
                self.state[k] = [[o.idx], [], grp, pw, pr]
        o.raw.discard(o.idx)
        o.war.discard(o.idx)
        self.ops.append(o)
        if dma:
            n = len(self.dma_ops)
            if n >= NDMA_SEM:
                o.prev_dma = self.dma_ops[n - NDMA_SEM]
            self.dma_ops.append(o.idx)
        return o

    def mm(self, out, lhsT, rhs, start=True, stop=True):
        o = self.op("pe", lambda e: e.matmul(_ap(out), lhsT=_ap(lhsT), rhs=_ap(rhs), start=start, stop=stop),
                    r=[lhsT, rhs], w=[out])
        o.cost = max(_fsz(out), 64) / 2050.0 * (4.0 if _ap(lhsT).dtype == F32 else 1.0) + 0.01

    def tr(self, out, in_, ident):
        o = self.op("pe", lambda e: e.transpose(_ap(out), _ap(in_), _ap(ident)), r=[in_, ident], w=[out])
        o.cost = max(_fsz(out), 64) / 2050.0 * (4.0 if _ap(in_).dtype == F32 else 1.0) + 0.01

    def act(self, out, in_, func, bias=None, scale=None):
        kw = {}
        if bias is not None:
            kw["bias"] = _ap(bias)
        if scale is not None:
            kw["scale"] = _ap(scale)
        o = self.op("act", lambda e: e.activation(out=_ap(out), in_=_ap(in_), func=func, **kw),
                    r=[in_, bias, scale], w=[out])
        o.cost = 0.15 + _fsz(out) / 1250.0
        o.tbl = {AF.Exp: "e", AF.Ln: "e", AF.Silu: "s", AF.Sigmoid: "g", AF.Sqrt: "q"}.get(func)

    def _vcost(self, o, eng, out, k=1.0):
        n = _fsz(out)
        o.cost = (0.12 + n / 960.0 * k) if eng == "dve" else (0.15 + n / 500.0)

    def tt(self, eng, out, in0, in1, op):
        o = self.op(eng, lambda e: e.tensor_tensor(out=_ap(out), in0=_ap(in0), in1=_ap(in1), op=op),
                    r=[in0, in1], w=[out])
        self._vcost(o, eng, out)

    def ts(self, eng, out, in0, s1, s2, op0, op1=None):
        if op1 is None:
            o = self.op(eng, lambda e: e.tensor_scalar(out=_ap(out), in0=_ap(in0), scalar1=_ap(s1), scalar2=None,
                                                       op0=op0), r=[in0, s1], w=[out])
        else:
            o = self.op(eng, lambda e: e.tensor_scalar(out=_ap(out), in0=_ap(in0), scalar1=_ap(s1),
                                                       scalar2=_ap(s2), op0=op0, op1=op1),
                        r=[in0, s1, s2], w=[out])
        self._vcost(o, eng, out, 0.5)

    def stt(self, eng, out, in0, scalar, in1, op0, op1, accum=None):
        if accum is None:
            o = self.op(eng, lambda e: e.scalar_tensor_tensor(out=_ap(out), in0=_ap(in0), scalar=_ap(scalar),
                                                              in1=_ap(in1), op0=op0, op1=op1),
                        r=[in0, scalar, in1], w=[out])
        else:
            o = self.op(eng, lambda e: e.scalar_tensor_tensor(out=_ap(out), in0=_ap(in0), scalar=_ap(scalar),
                                                              in1=_ap(in1), op0=op0, op1=op1,
                                                              accum_out=_ap(accum)),
                        r=[in0, scalar, in1], w=[out, accum])
        self._vcost(o, eng, out)

    def copy(self, eng, out, in_):
        if eng == "act":
            o = self.op("act", lambda e: e.activation(out=_ap(out), in_=_ap(in_), func=AF.Copy), r=[in_], w=[out])
            o.cost = 0.2 + _fsz(out) / 1200.0
        else:
            o = self.op(eng, lambda e: e.tensor_copy(out=_ap(out), in_=_ap(in_)), r=[in_], w=[out])
            self._vcost(o, eng, out, 0.5)

    def recip(self, out, in_):
        o = self.op("dve", lambda e: e.reciprocal(out=_ap(out), in_=_ap(in_)), r=[in_], w=[out])
        o.cost = 0.12 + _fsz(out) * 6.3 / 960.0

    def memset(self, eng, out, val):
        self.op(eng, lambda e: e.memset(_ap(out), val), r=[], w=[out])

    def dma(self, eng, out, in_, slow=False, grp=None):
        if grp is None:
            grp = getattr(self, "cur_grp", None)
        if slow:
            o = self.op(eng, lambda e: e.dma_start(out=_ap(out), in_=_ap(in_), allow_slow_non_contiguous=True),
                        r=[in_], w=[out], dma=True, grp=grp)
        else:
            o = self.op(eng, lambda e: e.dma_start(out=_ap(out), in_=_ap(in_)), r=[in_], w=[out], dma=True, grp=grp)
        nb = _nbytes(out)
        if eng == "pool":
            o.cost = 1.05
            o.xfer = 2.0 + nb / 90e3
        else:
            o.cost = 0.5
            o.xfer = 2.0 + nb / 260e3

    def schedule(self):
        import heapq
        ops = self.ops
        n = len(ops)
        preds = [sorted(o.raw | o.war) for o in ops]
        succ = [[] for _ in range(n)]
        indeg = [0] * n
        for i, ps in enumerate(preds):
            indeg[i] = len(ps)
            for p in ps:
                succ[p].append(i)
        prio = [0.0] * n
        for i in range(n - 1, -1, -1):
            o = ops[i]
            m = 0.0
            for s_ in succ[i]:
                if prio[s_] > m:
                    m = prio[s_]
            prio[i] = m + o.cost + o.xfer
        fin = [0.0] * n
        efree = {"pe": 0.0, "act": 0.0, "dve": 0.0, "pool": 0.0, "sp": 0.0}
        swdge_free = [0.0]
        last_tbl = [None]
        ready = {e: [] for e in efree}
        dep_t = [0.0] * n
        for i in range(n):
            if indeg[i] == 0:
                ready[ops[i].eng].append(i)
        order = []
        LAT = 0.35
        while len(order) < n:
            best = None
            for e, lst in ready.items():
                if not lst:
                    continue
                ef = efree[e]
                cb = None
                for i in lst:
                    st = dep_t[i] if dep_t[i] > ef else ef
                    if e == "act" and ops[i].tbl is not None and last_tbl[0] is not None and ops[i].tbl != last_tbl[0]:
                        st += 1.3
                    key = (st, -prio[i], i)
                    if cb is None or key < cb:
                        cb = key
                if best is None or cb < best[0]:
                    best = (cb, e)
            (st, _, i), e = best
            ready[e].remove(i)
            o = ops[i]
            if e == "act" and o.tbl is not None:
                last_tbl[0] = o.tbl
            end_eng = st + o.cost
            efree[e] = end_eng
            if o.dma:
                if e == "pool":
                    t0 = max(end_eng, swdge_free[0])
                    fin[i] = t0 + o.xfer
                    swdge_free[0] = fin[i] - 2.0
                else:
                    fin[i] = end_eng + o.xfer
            else:
                fin[i] = end_eng
            order.append(i)
            o.t0, o.t1 = st, fin[i]
            for s_ in succ[i]:
                t = fin[i] + (LAT if ops[s_].eng != e or o.dma else 0.0)
                if t > dep_t[s_]:
                    dep_t[s_] = t
                indeg[s_] -= 1
                if indeg[s_] == 0:
                    ready[ops[s_].eng].append(s_)
        self.est_time = max(fin) if fin else 0.0
        return order

    def emit(self, reorder=True):
        nc = self.nc
        ops = self.ops
        if reorder:
            order = self.schedule()
        else:
            order = list(range(len(ops)))
        pos = {i: k for k, i in enumerate(order)}
        self.dma_ops = [i for i in order if ops[i].dma]
        self.dma_sw = [i for i in self.dma_ops if ops[i].eng == "pool"]
        self.dma_hw = [i for i in self.dma_ops if ops[i].eng != "pool"]
        for lst, nsem in ((self.dma_sw, NSW_SEM), (self.dma_hw, NDMA_SEM)):
            for n_, i in enumerate(lst):
                ops[i].prev_dma = lst[n_ - nsem] if n_ >= nsem else None
        for o in ops:
            deps = set()
            for d in o.raw:
                p = ops[d]
                if p.eng == o.eng and not p.dma and not o.dma and o.eng == "pe":
                    continue
                deps.add(d)
            for d in o.war:
                p = ops[d]
                if p.eng == o.eng and not p.dma and not o.dma and o.eng == "pe":
                    continue
                deps.add(d)
            if o.prev_dma is not None:
                deps.add(o.prev_dma)
            o.raw = deps
            for d in deps:
                ops[d].needed = True
        for i in self.dma_ops:
            ops[i].needed = True
        engs = ["pe", "act", "dve", "pool", "sp"]
        esem = {e: self.stack.enter_context(nc.semaphore("sem_" + e)) for e in engs}
        dsem = [self.stack.enter_context(nc.semaphore("dsem%d" % i)) for i in range(NDMA_SEM)]
        ssem = [self.stack.enter_context(nc.semaphore("ssem%d" % i)) for i in range(NSW_SEM)]
        cnt = {e: 0 for e in engs}
        for n, i in enumerate(self.dma_hw):
            ops[i].sig = (dsem[n % NDMA_SEM], 16 * (n // NDMA_SEM + 1), ("d", n % NDMA_SEM))
        for n, i in enumerate(self.dma_sw):
            ops[i].sig = (ssem[n % NSW_SEM], 16 * (n // NSW_SEM + 1), ("s", n % NSW_SEM))
        for i in order:
            o = ops[i]
            if o.dma:
                continue
            if o.needed:
                cnt[o.eng] += 1
                o.sig = (esem[o.eng], cnt[o.eng], ("e", o.eng))
        final = {}
        for n, i in enumerate(self.dma_ops):
            s = ops[i].sig
            final[s[2]] = (s[0], max(final.get(s[2], (None, 0))[1], s[1]))
        by_eng = {e: [ops[i] for i in order if ops[i].eng == e] for e in engs}
        self.n_instr = {e: len(by_eng[e]) for e in engs}

        def run(eng_name, e):
            waited = {}
            for o in by_eng[eng_name]:
                need = {}
                for d in o.raw:
                    s = ops[d].sig
                    if s[1] > need.get(s[2], (None, 0))[1]:
                        need[s[2]] = (s[0], s[1])
                for k, (sem, val) in need.items():
                    if waited.get(k, 0) >= val:
                        continue
                    e.wait_ge(sem, val)
                    waited[k] = val
                ins = o.fn(e)
                if o.sig is not None:
                    ins.then_inc(o.sig[0], 16 if o.dma else 1)
            if eng_name == "sp":
                for k, (sem, val) in final.items():
                    e.wait_ge(sem, val)

        with nc.Block() as block:
            @block.tensor
            def _(e):
                run("pe", e)

            @block.scalar
            def _(e):
                run("act", e)

            @block.vector
            def _(e):
                run("dve", e)

            @block.gpsimd
            def _(e):
                run("pool", e)

            @block.sync
            def _(e):
                run("sp", e)


def build_nc(NSB=4, with_sample=True, STOP=99, DBG=False):
    nc = bass.Bass("TRN2", target_bir_lowering=False)
    stack = contextlib.ExitStack()
    P = Prog(nc, stack)
    L = NSB * TB

    def din(name, shape):
        return nc.dram_tensor(name, list(shape), F32, kind="ExternalInput").ap()

    def dout(name, shape):
        return nc.dram_tensor(name, list(shape), F32, kind="ExternalOutput").ap()

    xp = din("xp", [L, D])
    xs_in = din("xs", [64, D])
    call = din("call", [17, D])
    st_conf = din("st_conf", [16, 30, D])
    st_ssmc = din("st_ssmc", [16, 3, 4096])
    st_ssm = din("st_ssm", [16, 32, 64, 128])
    w_ada = din("w_ada", [D, 3 * D])
    b_ada = din("b_ada", [3 * D])
    g_pre = din("g_pre", [D])
    g_post = din("g_post", [D])
    w_in = din("w_in", [D, IN_DIM])
    conf_dw_w = din("conf_dw_w", [31, D])
    conf_dw_b = din("conf_dw_b", [D])
    conf_ln_g = din("conf_ln_g", [D])
    conf_ln_b = din("conf_ln_b", [D])
    w_conf_out = din("w_conf_out", [D, D])
    ssm_dw_w = din("ssm_dw_w", [4, 4096])
    ssm_dw_b = din("ssm_dw_b", [4096])
    ssm_dt_bias = din("ssm_dt_bias", [32])
    ssm_a_log = din("ssm_a_log", [32])
    ssm_d = din("ssm_d", [32])
    ssm_norm_g = din("ssm_norm_g", [2048])
    w_ssm_out = din("w_ssm_out", [2048, D])
    w_o = din("w_o", [D, D])
    c_ident = din("c_ident", [128, 128])
    c_tri = din("c_tri", [128, 128])
    c_sut = din("c_sut", [128, 128])
    c_smp = din("c_smp", [128, 1168])

    y_prompt = dout("y_prompt", [L, D])
    y_sample = dout("y_sample", [64, D])
    o_conf_p = dout("o_conf_p", [30, D])
    o_ssmc_p = dout("o_ssmc_p", [3, 4096])
    o_ssm_p = dout("o_ssm_p", [32 * 64, 128])
    o_conf_s = dout("o_conf_s", [16, 30, D])
    o_ssmc_s = dout("o_ssmc_s", [16, 3, 4096])
    o_ssm_s = dout("o_ssm_s", [16, 32 * 64, 128])

    win_v = w_in.rearrange("(kc kp) n -> kp kc n", kp=128)
    dbg = {}
    if DBG:
        for nm, shp in (("d_hT", [128, 8 * 512]), ("d_dt", [128, 512]), ("d_dt2", [128, 512]), ("d_xact", [128, 4 * 512]),
                        ("d_yn", [128, 16 * 512]), ("d_gbs", [128, 8 * 512]), ("d_vt", [128, 8 * 512]),
                        ("d_merged", [128, 8 * 512]), ("d_prmT", [128, 488]), ("d_modT", [128, 24 * 17]),
                        ("d_ytok", [128, 1024]), ("d_ycv", [128, 8 * 512]), ("d_GGp", [128, 1024])):
            dbg[nm] = dout(nm, shp)
    dbf = {}

    def dump(nm, view, bf=False):
        if not DBG:
            return
        if bf:
            P.dma("pool", dbg[nm], View(view.ap if len(view.ap.shape) == 2 else view.ap.rearrange("p a b -> p (a b)"), view.keys))
        else:
            P.dma("sp", dbg[nm], View(view.ap if len(view.ap.shape) == 2 else view.ap.rearrange("p a b -> p (a b)"), view.keys))

    P.make_psum()
    identf = P.sbt("identf", [128, 128], F32)
    identb = P.sbt("identb", [128, 128], BF16)
    tri = P.sbt("tri", [128, 128], F32)
    sut = P.sbt("sut", [128, 128], F32)
    sutb = P.sbt("sutb", [128, 128], BF16)
    onesf = P.sbt("onesf", [128, 128], F32)
    onesb = P.sbt("onesb", [128, 128], BF16)
    prmT = P.sbt("prmT", [128, 488], F32)
    modT = P.sbt("modT", [128, 24, 17], F32)
    G_all = P.sbt("G_all", [128, 8, 17], F32)
    dtb_bc = P.sbt("dtb_bc", [128, 32], F32)
    a_bc = P.sbt("a_bc", [128, 32], F32)
    D_fm = P.sbt("D_fm", [128, 16], F32)
    wdt = P.sbt("wdt", [128, 8, 32], BF16)
    GGp = P.sbt("GGp", [128, D], F32)
    ss = P.sbt("ss", [128, 8], F32)
    epsb = P.sbt("epsb", [128, 1], F32)

    tailst = [P.sbt("tailst%d" % i, [32, 512], F32) for i in range(2)]
    tailu = P.sbt("tailu", [32, 1024], F32)
    NSLOT = 92
    P.make_arena(NSLOT)
    hT = P.at(0, 4, BF16, "p (a b) -> p a b", a=8)
    gbs = P.at(4, 4, BF16, "p (a b) -> p a b", a=8)
    hT2 = P.at(8, 4, BF16, "p (a b) -> p a b", a=8)
    HT = P.at(12, 4, F32, "p (a b) -> p a b", a=8)
    uhalo = P.at(16, 1, BF16)
    xhalo = P.at(17, 1, BF16)
    dt_t = P.at(18, 1, F32)
    dt2_t = P.at(19, 1, F32)
    R0 = 20

    def prm(c0, n):
        return prmT[:, c0:c0 + n]

    P.dma("sp", identf.all(), c_ident)
    P.dma("pool", identb.all(), c_ident)
    P.dma("sp", tri.all(), c_tri)
    P.dma("sp", sut.all(), c_sut)
    P.dma("pool", sutb.all(), c_sut)
    P.memset("pool", onesf.all(), 1.0)
    P.memset("pool", onesb.all(), 1.0)
    P.memset("pool", epsb.all(), EPS)
    P.memset("pool", HT.all(), 0.0)
    P.memset("pool", uhalo.all(), 0.0)
    P.memset("pool", xhalo.all(), 0.0)
    P.dma("sp", dtb_bc.all(), ssm_dt_bias.rearrange("(o n) -> o n", o=1).partition_broadcast(128)[:, 0, :])
    P.dma("sp", a_bc.all(), ssm_a_log.rearrange("(o n) -> o n", o=1).partition_broadcast(128)[:, 0, :])
    P.dma("sp", D_fm[0:64, :], bass.AP(tensor=ssm_d.tensor, offset=0, ap=[[0, 64], [2, 16]]), slow=True, grp="dfm")
    P.dma("sp", D_fm[64:128, :], bass.AP(tensor=ssm_d.tensor, offset=1, ap=[[0, 64], [2, 16]]), slow=True, grp="dfm")
    P.dma("pool", wdt.all(), win_v[:, :, OFF_DT:OFF_DT + 32])
    P.act(a_bc.all(), a_bc.all(), AF.Exp)
    P.ts("dve", a_bc.all(), a_bc.all(), -1.0, None, ALU.mult)

    stg = P.at(R0, 1, F32)
    cw = conf_dw_w.rearrange("j (c p) -> (j c) p", p=128)
    sw = ssm_dw_w.rearrange("j (c p) -> (j c) p", p=128)

    def v128(x):
        return x.rearrange("(c p) -> c p", p=128)
    groups = [
        (0, 128, [(cw[0:128, :], 128)]),
        (128, 120, [(cw[128:248, :], 120)]),
        (248, 128, [(sw, 128)]),
        (376, 112, [(v128(g_pre), 8), (v128(g_post), 8), (v128(b_ada), 24), (v128(conf_dw_b), 8),
                    (v128(conf_ln_g), 8), (v128(conf_ln_b), 8), (v128(ssm_dw_b), 32), (v128(ssm_norm_g), 16)]),
    ]
    for gi, (c0, nrow, srcs) in enumerate(groups):
        r0 = 0
        for (src, n) in srcs:
            P.dma("sp", stg[r0:r0 + n, 0:128], src, grp=("stg", gi))
            r0 += n
        pt = P.ps(gi % 2)
        P.tr(pt[:, 0:nrow], stg[0:nrow, 0:128], identf[0:nrow, 0:nrow])
        P.copy("dve", prmT[:, c0:c0 + nrow], pt[:, 0:nrow])
    PC_CDW, PC_SDW, PC_GPRE, PC_GPOST, PC_BADA, PC_CDWB, PC_LNG, PC_LNB, PC_SDWB, PC_NG = (
        0, 248, 376, 384, 392, 416, 424, 432, 440, 472)

    c_sb = P.at(R0 + 1, 2, F32)
    csl = P.at(R0 + 3, 2, F32)
    scT = P.at(R0 + 5, 1, F32)
    scT3 = scT.v(scT.ap[:, 0:256].rearrange("p (a b) -> p a b", a=8))
    P.dma("sp", c_sb[0:17, :], call)
    P.act(csl[0:17, :], c_sb[0:17, :], AF.Silu)
    pt = P.ps(2)
    for kc in range(8):
        P.tr(pt[:, kc * 32:kc * 32 + 17], csl[0:17, kc * 128:(kc + 1) * 128], identf[0:17, 0:17])
        P.copy("dve", scT[:, kc * 32:kc * 32 + 17], pt[:, kc * 32:kc * 32 + 17])
    wada_v = w_ada.rearrange("(kc kp) n -> kp kc n", kp=128)
    pm = P.ps(3)
    for c6 in range(6):
        wa = P.at(R0 + 24 + 8 * (c6 % 2), 8, F32, "p (a b) -> p a b", a=8)
        P.dma("sp", wa.all(), wada_v[:, :, c6 * 512:(c6 + 1) * 512])
        for q in range(4):
            dc = c6 * 4 + q
            for kc in range(8):
                P.mm(pm[:, dc * 17:(dc + 1) * 17], wa[:, kc, q * 128:(q + 1) * 128],
                     View(scT3.ap[:, kc, 0:17], scT.keys), start=(kc == 0), stop=(kc == 7))
    pm3 = pm.v(pm.ap[:, 0:408].rearrange("p (a b) -> p a b", a=24))
    P.tt("dve", modT.all(), pm3, prmT.v(prmT.ap[:, PC_BADA:PC_BADA + 24].unsqueeze(2).broadcast_to([128, 24, 17])),
         ALU.add)
    P.stt("dve", G_all.all(), modT[:, 8:16, :], 1.0,
          prmT.v(prmT.ap[:, PC_GPRE:PC_GPRE + 8].unsqueeze(2).broadcast_to([128, 8, 17])), ALU.add, ALU.mult)
    ggrep = P.at(R0 + 13, 8, F32, "p (a b) -> p a b", a=8)
    P.tt("dve", ggrep[:, :, 0:128], modT.v(modT.ap[:, 16:24, 0:1].broadcast_to([128, 8, 128])),
         prmT.v(prmT.ap[:, PC_GPOST:PC_GPOST + 8].unsqueeze(2).broadcast_to([128, 8, 128])), ALU.mult)
    for half in range(2):
        pg = P.ps(4 + half)
        for q in range(4):
            dc = half * 4 + q
            P.tr(pg[:, q * 128:(q + 1) * 128], ggrep[:, dc, 0:128], identf.all())
        P.copy("act", GGp[:, half * 512:(half + 1) * 512], pg.all())

    dump("d_prmT", prmT.all())
    dump("d_modT", modT.all())
    dump("d_GGp", GGp.all())
    if STOP <= 0:
        P.emit()
        return nc, P, stack
    def A(off, n, dtype, pattern=None, **kw):
        return P.at(R0 + off, n, dtype, pattern, **kw)

    RB = [A(6 * i, 6, BF16) for i in range(4)]
    S0 = 24
    yn = A(S0 + 0, 8, BF16, "p (a b) -> p a b", a=16)
    xpad = [A(S0 + 8 + 3 * i, 3, BF16) for i in range(2)]
    xact = [A(S0 + 14 + 2 * i, 2, BF16, "p (a b) -> p a b", a=4) for i in range(2)]
    zs = [A(S0 + 18 + i, 1, BF16, "p (a b) -> p a b", a=2) for i in range(2)]
    Dg = [A(S0 + 20 + 2 * i, 2, BF16, "p (a b c) -> p a b c", a=4, b=4) for i in range(2)]
    xdt = A(S0 + 24, 1, BF16, "p (a b) -> p a b", a=4)
    xdtd = A(S0 + 25, 1, BF16, "p (a b) -> p a b", a=4)
    Btok = A(S0 + 26, 1, BF16)
    cbm = A(S0 + 27, 1, F32, "p (a b) -> p a b", a=4)
    R1 = [A(S0 + 28 + i, 1, F32, "p (a b) -> p a b", a=4) for i in range(2)]
    Et = [A(S0 + 30 + i, 1, F32, "p (a b) -> p a b", a=4) for i in range(2)]
    MT = [A(S0 + 32 + i, 1, BF16) for i in range(2)]
    ytmp = A(S0 + 34, 1, F32)
    ytok = A(S0 + 35, 1, BF16, "p (a b) -> p a b", a=4)
    HTb = A(S0 + 36, 1, BF16)
    t1 = A(S0 + 37, 2, F32, "p (a b) -> p a b", a=2)
    yg = A(S0 + 39, 2, F32, "p (a b) -> p a b", a=2)
    sq = A(S0 + 41, 1, BF16, "p (a b) -> p a b", a=2)
    rstd = A(S0 + 42, 1, F32)
    sgm = A(S0 + 43, 1, F32)
    xdt2 = A(S0 + 44, 1, BF16, "p (a b) -> p a b", a=4)
    cbm2 = A(S0 + 45, 1, F32, "p (a b) -> p a b", a=4)
    ytok2 = A(S0 + 46, 1, BF16, "p (a b) -> p a b", a=4)
    xdtd2 = A(S0 + 47, 1, BF16, "p (a b) -> p a b", a=4)
    upad = [A(S0 + i, 1, BF16) for i in range(2)]
    sgl = A(S0 + 2, 1, F32)
    ycv = A(S0 + 3, 8, F32, "p (a b) -> p a b", a=8)
    sgt = A(S0 + 11, 4, BF16, "p (a b) -> p a b", a=8)
    Dc = [A(S0 + 15 + 4 * i, 4, BF16) for i in range(2)]
    ysq = A(S0 + 23, 1, BF16)
    mean_t = A(S0 + 24, 1, F32)
    var_t = A(S0 + 25, 1, F32)
    vt = A(S0 + 26, 4, BF16, "p (a b) -> p a b", a=8)
    sgc = A(S0 + 30, 1, F32)
    merged = A(S0 + 31, 4, BF16, "p (a b) -> p a b", a=8)
    xt = [A(S0 + 35 + 2 * i, 2, F32) for i in range(2)]
    ot = [A(S0 + 39 + 2 * i, 2, F32) for i in range(2)]
    junk = A(S0 + 43, 2, BF16)
    xn = A(S0 + 45, 1, BF16)
    tk = A(S0 + 46, 2, F32)

    wso_v = w_ssm_out.rearrange("(cc cp) n -> cp cc n", cp=128)
    wco_v = w_conf_out.rearrange("(cc cp) n -> cp cc n", cp=128)
    wo_v = w_o.rearrange("(kc kp) n -> kp kc n", kp=128)

    def rbv(rb, c0, a, n):
        return View(rb.ap[:, c0:c0 + a * n].rearrange("p (a b) -> p a b", a=a), rb.keys)

    NJ_SB = 34
    scr = nc.dram_tensor("scr", [NJ_SB, 128, 6144], BF16, kind="Internal").ap()

    def job_wg(g):
        def f(rb):
            w = rbv(rb, 0, 8, 768)
            P.dma("pool", w[:, :, 0:256], win_v[:, :, OFF_XS + 256 * g:OFF_XS + 256 * g + 256])
            P.dma("pool", w[:, :, 256:384], win_v[:, :, OFF_B + 128 * g:OFF_B + 128 * g + 128])
            P.dma("pool", w[:, :, 384:512], win_v[:, :, OFF_C + 128 * g:OFF_C + 128 * g + 128])
            P.dma("pool", w[:, :, 512:768], win_v[:, :, OFF_Z + 256 * g:OFF_Z + 256 * g + 256])
        return f, 6144

    def job_s3(dc):
        def f(rb):
            P.dma("pool", rbv(rb, 0, 16, 128), wso_v[:, :, dc * 128:(dc + 1) * 128])
            P.dma("pool", rbv(rb, 2048, 8, 128), win_v[:, :, OFF_MS + dc * 128:OFF_MS + (dc + 1) * 128])
        return f, 3072

    def job_wc(cc):
        def f(rb):
            w = rbv(rb, 0, 8, 384)
            for q, off in enumerate((OFF_VAL, OFF_GLU, OFF_GATE)):
                P.dma("pool", w[:, :, q * 128:(q + 1) * 128], win_v[:, :, off + cc * 128:off + (cc + 1) * 128])
        return f, 3072

    def job_s5(dc):
        def f(rb):
            P.dma("pool", rbv(rb, 0, 8, 128), wco_v[:, :, dc * 128:(dc + 1) * 128])
            P.dma("pool", rbv(rb, 1024, 8, 128), win_v[:, :, OFF_MC + dc * 128:OFF_MC + (dc + 1) * 128])
        return f, 2048

    def job_wo(half):
        def f(rb):
            P.dma("pool", rbv(rb, 0, 8, 512), wo_v[:, :, half * 512:(half + 1) * 512])
        return f, 4096

    def job_win(col0):
        def f(rb):
            P.dma("pool", rbv(rb, 0, 8, 512), win_v[:, :, col0:col0 + 512])
        return f, 4096

    def cached(jf, slot, first):
        f, ncol = jf
        sv = View(scr[slot, :, 0:ncol], [("scr", slot)])

        def g_(rb):
            if first:
                f(rb)
                P.dma("sp", sv, rb[:, 0:ncol])
            else:
                P.dma("sp", rb[:, 0:ncol], sv)
        return g_

    jobs = []
    jid = {}
    for sb_ in range(NSB):
        sl = 0
        for kind, n_, mk in (("wg", 8, job_wg), ("s3", 8, job_s3), ("wc", 8, job_wc), ("s5", 8, job_s5), ("wo", 2, job_wo)):
            for i_ in range(n_):
                jid[(sb_, kind, i_)] = len(jobs)
                jobs.append(cached(mk(i_), sl, sb_ == 0))
                sl += 1
    job_issued = [0]
    PF = 3

    def w_get(k, hold=None):
        lim = k + 1 + PF if hold is None else min(k + 1 + PF, hold + 4)
        while job_issued[0] < min(len(jobs), lim):
            j = job_issued[0]
            P.cur_grp = ("job", j)
            jobs[j](RB[j % 4])
            P.cur_grp = None
            job_issued[0] += 1
        return RB[k % 4]

    dtv = dt_t.v(dt_t.ap[:, 0:128].rearrange("p (a b) -> p a b", a=4))
    dtav = dt_t.v(dt_t.ap[:, 128:256].rearrange("p (a b) -> p a b", a=4))
    ecv = dt_t.v(dt_t.ap[:, 256:384].rearrange("p (a b) -> p a b", a=4))
    cdv = dt_t.v(dt_t.ap[:, 384:512].rearrange("p (a b) -> p a b", a=4))
    dtmp = dt2_t.v(dt2_t.ap[:, 0:128].rearrange("p (a b) -> p a b", a=4))
    cqs = dt2_t.v(dt2_t.ap[:, 128:256])
    dendv = dt2_t.v(dt2_t.ap[:, 256:384].rearrange("p (a b) -> p a b", a=4))
    dtalv = dt2_t.v(dt2_t.ap[:, 384:512].rearrange("p (a b) -> p a b", a=4))

    def bc_heads(v3, c, g, width):
        ap = v3.ap[:, c, 4 * g:4 * g + 4].unsqueeze(2).broadcast_to([128, 4, width])
        return View(ap, v3.keys)

    pp_ctr = [0]

    def next_pp():
        b = pp_ctr[0] % 2
        pp_ctr[0] += 1
        return b

    def load_x_tile(row0, buf):
        P.dma("sp", xt[buf].all(), xp[row0:row0 + 128, :])

    hT_bufs = [hT, hT2]
    xt1 = [A(S0 + 0, 2, F32), A(S0 + 2, 2, F32)]
    xn1 = A(S0 + 4, 1, BF16)
    junk1 = A(S0 + 5, 2, BF16)
    ss1 = P.sbt("ss1", [128, 4], F32)
    for sb in range(NSB):
        hT = hT_bufs[sb % 2]
        P.tag = "sb%d.st1" % sb
        for i in range(4):
            row0 = sb * TB + i * 128
            buf = i % 2
            P.dma("sp", xt1[buf].all(), xp[row0:row0 + 128, :])
            P.stt("dve", junk1[:, 0:1024], xt1[buf].all(), 1.0, xt1[buf].all(), ALU.mult, ALU.mult,
                  accum=ss1[:, 0:1])
            P.act(ss1[:, 1:2], ss1[:, 0:1], AF.Ln, bias=epsb[:, 0:1], scale=1.0 / D)
            P.act(ss1[:, 2:3], ss1[:, 1:2], AF.Exp, scale=-0.5)
            P.ts("dve", xn1.all(), xt1[buf].all(), ss1[:, 2:3], None, ALU.mult)
            pt = P.psb(2 + (i % 2))
            pt3 = pt.v(pt.ap.rearrange("p (a b) -> p a b", a=8))
            for kc in range(8):
                P.tr(View(pt3.ap[:, kc, :], pt.keys), xn1[:, kc * 128:(kc + 1) * 128], identb.all())
            for kc in range(8):
                P.act(hT[:, kc, i * 128:(i + 1) * 128], View(pt3.ap[:, kc, :], pt.keys), AF.Identity,
                      bias=modT[:, kc, 0:1], scale=G_all[:, kc, 0:1])

        if sb == 0:
            dump("d_hT", hT.all(), bf=True)
        if STOP <= 1:
            break
        P.tag = "sb%d.st2" % sb
        pdt = P.ps(3, 0)
        for i in range(4):
            for kc in range(8):
                P.mm(pdt[:, i * 32:(i + 1) * 32], hT[:, kc, i * 128:(i + 1) * 128], wdt[:, kc, :],
                     start=(kc == 0), stop=(kc == 7))
        P.tt("dve", dtmp, pdt.v(pdt.ap[:, 0:128].rearrange("p (a b) -> p a b", a=4)),
             dtb_bc.v(dtb_bc.ap.unsqueeze(1).broadcast_to([128, 4, 32])), ALU.add)
        P.act(dtmp, dtmp, AF.Exp)
        P.act(dtv, dtmp, AF.Ln, bias=1.0)
        P.tt("dve", dtav, dtv, a_bc.v(a_bc.ap.unsqueeze(1).broadcast_to([128, 4, 32])), ALU.mult)
        pcq = P.ps(4, 0)
        pct = P.ps(4, 1)
        dta2 = dt_t[:, 128:256]
        P.mm(pcq[:, 0:128], tri.all(), dta2)
        P.mm(pct[:, 0:128], onesf.all(), dta2)
        P.act(dt_t[:, 256:384], pcq[:, 0:128], AF.Exp)
        P.act(dt_t[:, 384:512], pct[:, 0:128], AF.Exp)
        P.copy("dve", cqs, pcq[:, 0:128])
        P.tt("dve", dt2_t[:, 256:384], pct[:, 0:128], cqs, ALU.subtract)
        P.act(dt2_t[:, 256:384], dt2_t[:, 256:384], AF.Exp)
        P.copy("dve", xn1[:, 0:128], dta2)
        P.copy("dve", dt2_t[:, 0:128], xn1[:, 0:128])
        P.tt("dve", dt2_t[:, 384:512], dta2, dt2_t[:, 0:128], ALU.subtract)

        if sb == 0:
            dump("d_dt", dt_t.all())
            dump("d_dt2", dt2_t.all())
        if STOP <= 2:
            break
        xh3 = xhalo.v(xhalo.ap[:, 0:96].rearrange("p (a b) -> p a b", a=32))
        CH = lambda g, j: (2 * g, 2 * g + 1, 16 + g, 24 + g)[j]

        def phaseA(g):
            w = rbv(w_get(jid[(sb, "wg", g)]), 0, 8, 768)
            xp_t = xpad[g % 2]
            xp3 = xp_t.v(xp_t.ap[:, 0:2060].rearrange("p (a b) -> p a b", a=4))
            xa = xact[g % 2]
            zz = zs[g % 2]
            dg = Dg[g % 2]
            P.copy("pool", View(xp3.ap[:, :, 0:3], xp_t.keys), View(xh3.ap[:, 4 * g:4 * g + 4, :], xhalo.keys))
            for j in range(4):
                c0 = PC_SDW + CH(g, j)
                P.tt("dve" if sb == 0 else "pool", dg[:, j, :, :], identb.v(identb.ap.unsqueeze(1).broadcast_to([128, 4, 128])),
                     prmT.v(prmT.ap[:, c0:c0 + 97:32].unsqueeze(2).broadcast_to([128, 4, 128])), ALU.mult)
            yield
            if sb == NSB - 1:
                pq = P.ps(next_pp())
                for kc in range(8):
                    P.mm(pq[0:32, :], hT[:, kc, 480:512], w[:, kc, 0:512], start=(kc == 0), stop=(kc == 7))
                tst = tailst[g % 2]
                P.copy("act", tst[0:32, :], pq[0:32, :])
                P.dma("sp", o_ssmc_p[:, 256 * g:256 * g + 256], tst[29:32, 0:256])
                P.dma("sp", o_ssmc_p[:, 2048 + 128 * g:2048 + 128 * g + 128], tst[29:32, 256:384])
                P.dma("sp", o_ssmc_p[:, 3072 + 128 * g:3072 + 128 * g + 128], tst[29:32, 384:512])
            for j in range(4):
                pj = P.ps(next_pp())
                for kc in range(8):
                    P.mm(pj.all(), w[:, kc, j * 128:(j + 1) * 128], hT[:, kc, :], start=(kc == 0), stop=(kc == 7))
                P.copy("act", View(xp3.ap[:, j, 3:515], xp_t.keys), pj.all())
                yield
            P.copy("pool", View(xh3.ap[:, 4 * g:4 * g + 4, :], xhalo.keys), View(xp3.ap[:, :, 512:515], xp_t.keys))
            for j in range(4):
                pj = P.ps(next_pp())
                for tap in range(4):
                    P.mm(pj.all(), dg[:, j, tap, :], View(xp3.ap[:, j, tap:tap + 512], xp_t.keys),
                         start=(tap == 0), stop=(tap == 3))
                col = PC_SDWB + CH(g, j)
                P.act(xa[:, j, :], pj.all(), AF.Silu, bias=prmT[:, col:col + 1])
                yield
            for j in range(2):
                pj = P.ps(next_pp())
                for kc in range(8):
                    P.mm(pj.all(), w[:, kc, 512 + j * 128:512 + (j + 1) * 128], hT[:, kc, :],
                         start=(kc == 0), stop=(kc == 7))
                P.act(zz[:, j, :], pj.all(), AF.Silu)
                yield

        aux = "dve" if sb == 0 else "pool"
        xdt_b = [xdt, xdt2]
        xdtd_b = [xdtd, xdtd2]
        cbm_b = [cbm, cbm2]
        ytok_b = [ytok, ytok2]

        def pre(g):
            xa = xact[g % 2]
            xdt_ = xdt_b[g % 2]
            xdtd_ = xdtd_b[g % 2]
            cbm_ = cbm_b[g % 2]
            bo = (g % 2) * 512
            ptx = P.psb(2)
            ptx3 = ptx.v(ptx.ap.rearrange("p (a b) -> p a b", a=4))
            for c in range(4):
                for j in range(2):
                    P.tr(View(ptx3.ap[:, c, j * 128:(j + 1) * 128], ptx.keys), xa[:, j, c * 128:(c + 1) * 128],
                         identb.all())
            dt_b = View(dtv.ap[:, :, 4 * g:4 * g + 4].unsqueeze(3).broadcast_to([128, 4, 4, 64]), dt_t.keys)
            P.tt("dve", xdt_.v(xdt_.ap.rearrange("p a (h q) -> p a h q", h=4)),
                 ptx.v(ptx.ap.rearrange("p (a h q) -> p a h q", a=4, h=4)), dt_b, ALU.mult)
            yield
            ptb = P.psb(4)
            for c in range(4):
                P.tr(ptb[:, c * 128:(c + 1) * 128], xa[:, 2, c * 128:(c + 1) * 128], identb.all())
            P.copy("act", Btok[:, bo:bo + 512], ptb[:, 0:512])
            pcb = P.ps(4)
            for c in range(4):
                P.mm(pcb[:, c * 128:(c + 1) * 128], xa[:, 2, c * 128:(c + 1) * 128], xa[:, 3, c * 128:(c + 1) * 128])
            P.tt("dve", cbm_.all(), pcb.v(pcb.ap.rearrange("p (a b) -> p a b", a=4)),
                 tri.v(tri.ap.unsqueeze(1).broadcast_to([128, 4, 128])), ALU.mult)
            dend_b = View(dendv.ap[:, :, 4 * g:4 * g + 4].unsqueeze(3).broadcast_to([128, 4, 4, 64]), dt2_t.keys)
            P.tt(aux, xdtd_.v(xdtd_.ap.rearrange("p a (h q) -> p a h q", h=4)),
                 xdt_.v(xdt_.ap.rearrange("p a (h q) -> p a h q", h=4)), dend_b, ALU.mult)
            yield

        def chunks(g):
            xa = xact[g % 2]
            xdt_ = xdt_b[g % 2]
            xdtd_ = xdtd_b[g % 2]
            cbm_ = cbm_b[g % 2]
            ytok_ = ytok_b[g % 2]
            bo = (g % 2) * 512
            P.copy("act", HTb[:, 0:256], HT[:, g, :])

            def front(c):
                r1 = R1[c % 2]
                et = Et[c % 2]
                mt = MT[c % 2]
                mt3 = mt.v(mt.ap[:, 0:512].rearrange("p (a b) -> p a b", a=4))
                r1b = r1.ap.rearrange("p a b -> p (a b)").bitcast(BF16)
                r1h = View(r1b[:, 0:512].rearrange("p (a b) -> p a b", a=4), r1.keys)
                r1l = View(r1b[:, 512:1024].rearrange("p (a b) -> p a b", a=4), r1.keys)
                P.tt("dve", r1h, tri.v(tri.ap.unsqueeze(1).broadcast_to([128, 4, 128])),
                     View(dtav.ap[:, c, 4 * g:4 * g + 4].unsqueeze(2).broadcast_to([128, 4, 128]), dt_t.keys),
                     ALU.mult)
                P.tt(aux, r1l, tri.v(tri.ap.unsqueeze(1).broadcast_to([128, 4, 128])),
                     View(dtalv.ap[:, c, 4 * g:4 * g + 4].unsqueeze(2).broadcast_to([128, 4, 128]), dt2_t.keys),
                     ALU.mult)
                pseg = P.ps(5 + (c % 2))
                P.mm(pseg.all(), sutb.all(), View(r1b[:, 0:512], r1.keys), start=True, stop=False)
                P.mm(pseg.all(), sutb.all(), View(r1b[:, 512:1024], r1.keys), start=False, stop=True)
                P.act(et.v(et.ap.rearrange("p a b -> p (a b)")), pseg.all(), AF.Exp)
                P.tt("dve", mt3, et.all(), View(cbm_.ap[:, c:c + 1, :].broadcast_to([128, 4, 128]), cbm_.keys), ALU.mult)

            def back(c):
                mt = MT[c % 2]
                mt3 = mt.v(mt.ap[:, 0:512].rearrange("p (a b) -> p a b", a=4))
                py1 = P.ps(7, 0)
                py2 = P.ps(7, 1)
                for h in range(4):
                    P.mm(py1[:, h * 64:(h + 1) * 64], View(mt3.ap[:, h, :], mt.keys), xdt_[:, c, h * 64:(h + 1) * 64])
                P.mm(py2.all(), xa[:, 3, c * 128:(c + 1) * 128], HTb[:, 0:256])
                P.tt("dve", ytmp.v(ytmp.ap[:, 0:256].rearrange("p (h q) -> p h q", h=4)),
                     py2.v(py2.ap.rearrange("p (h q) -> p h q", h=4)), bc_heads(ecv, c, g, 64), ALU.mult)
                P.tt("dve", ytok_[:, c, :], ytmp[:, 0:256], py1.all(), ALU.add)
                pst = P.ps(3, 0)
                P.mm(pst.all(), Btok[:, bo + c * 128:bo + (c + 1) * 128], xdtd_[:, c, :])
                hg = View(HT.ap[:, g, :].rearrange("p (h q) -> p h q", h=4), HT.keys)
                P.tt("dve", hg, hg, bc_heads(cdv, c, g, 64), ALU.mult)
                P.tt("dve", HT[:, g, :], HT[:, g, :], pst.all(), ALU.add)
                if c < 3:
                    P.copy("act", HTb[:, 0:256], HT[:, g, :])

            front(0)
            for c in range(4):
                if c + 1 < 4:
                    front(c + 1)
                yield
                back(c)
                yield
            if sb == 0 and g == 0:
                dump("d_xact", xa.all(), bf=True)
                dump("d_ytok", ytok_.all(), bf=True)

        def epi(g):
            xa = xact[g % 2]
            zz = zs[g % 2]
            ytok_ = ytok_b[g % 2]
            pyt = P.psb(next_pp())
            pyt3 = pyt.v(pyt.ap.rearrange("p (a b) -> p a b", a=2))
            for j in range(2):
                for c in range(4):
                    P.tr(View(pyt3.ap[:, j, c * 128:(c + 1) * 128], pyt.keys), ytok_[:, c, j * 128:(j + 1) * 128],
                         identb.all())
            for j in range(2):
                P.stt("dve", t1[:, j, :], xa[:, j, :], D_fm[:, 2 * g + j:2 * g + j + 1],
                      View(pyt3.ap[:, j, :], pyt.keys), ALU.mult, ALU.add)
            yield
            for j in range(2):
                P.tt(aux, yg[:, j, :], t1[:, j, :], zz[:, j, :], ALU.mult)
                P.act(sq[:, j, :], yg[:, j, :], AF.Square)
            yield
            pms = P.ps(next_pp())
            for j in range(2):
                P.mm(pms.all(), onesb.all(), sq[:, j, :], start=(j == 0), stop=(j == 1))
            P.act(rstd.all(), pms.all(), AF.Ln, bias=epsb[:, 0:1], scale=1.0 / 256.0)
            yield
            P.act(rstd.all(), rstd.all(), AF.Exp, scale=-0.5)
            for j in range(2):
                col = PC_NG + 2 * g + j
                P.stt("dve", yn[:, 2 * g + j, :], yg[:, j, :], prmT[:, col:col + 1], rstd.all(), ALU.mult, ALU.mult)
            yield

        def chain(*gens):
            for g_ in gens:
                yield from g_

        def interleave(*gens):
            live = [g_ for g_ in gens if g_ is not None]
            while live:
                for g_ in list(live):
                    try:
                        next(g_)
                    except StopIteration:
                        live.remove(g_)

        interleave(chain(phaseA(0), pre(0)))
        for g in range(8):
            interleave(chunks(g),
                       chain(phaseA(g + 1), pre(g + 1)) if g + 1 < 8 else None,
                       epi(g - 1) if g >= 1 else None)
        interleave(epi(7))

        if sb == 0:
            dump("d_yn", yn.all(), bf=True)
        if STOP <= 3:
            break
        P.tag = "sb%d.st3" % sb
        for dc in range(8):
            rb = w_get(jid[(sb, "s3", dc)])
            wso_t = rbv(rb, 0, 16, 128)
            wm_t = rbv(rb, 2048, 8, 128)
            pa = P.ps(next_pp())
            for cc in range(16):
                P.mm(pa.all(), wso_t[:, cc, :], yn[:, cc, :], start=(cc == 0), stop=(cc == 15))
            pb = P.ps(2 + (dc % 2))
            for kc in range(8):
                P.mm(pb.all(), wm_t[:, kc, :], hT[:, kc, :], start=(kc == 0), stop=(kc == 7))
            P.act(sgm.all(), pb.all(), AF.Sigmoid)
            P.tt("dve", gbs[:, dc, :], pa.all(), sgm.all(), ALU.mult)

        if sb == 0:
            dump("d_gbs", gbs.all(), bf=True)
        if STOP <= 4:
            break
        P.tag = "sb%d.st4" % sb
        uh3 = uhalo.v(uhalo.ap[:, 0:240].rearrange("p (a b) -> p a b", a=8))

        def conv_proj(cc):
            w = rbv(w_get(jid[(sb, "wc", cc)]), 0, 8, 384)
            up = upad[cc % 2]
            dcw = Dc[cc % 2]
            dcw3 = dcw.v(dcw.ap[:, 0:3968].rearrange("p (a b) -> p a b", a=31))
            P.copy("pool", up[:, 0:30], View(uh3.ap[:, cc, :], uhalo.keys))
            c0 = PC_CDW + cc
            P.tt("dve", dcw3, identb.v(identb.ap.unsqueeze(1).broadcast_to([128, 31, 128])),
                 prmT.v(prmT.ap[:, c0:c0 + 241:8].unsqueeze(2).broadcast_to([128, 31, 128])), ALU.mult)
            banks = (0, 1, 2) if cc % 2 == 0 else (5, 6, 7)
            pv, pgl, pgt = (P.ps(bk) for bk in banks)
            for q, pq in enumerate((pv, pgl, pgt)):
                for kc in range(8):
                    P.mm(pq.all(), w[:, kc, q * 128:(q + 1) * 128], hT[:, kc, :], start=(kc == 0), stop=(kc == 7))
            P.act(sgl.all(), pgl.all(), AF.Sigmoid)
            P.act(sgt[:, cc, :], pgt.all(), AF.Silu)
            P.tt("dve", up[:, 30:542], pv.all(), sgl.all(), ALU.mult)
            P.copy("pool", View(uh3.ap[:, cc, :], uhalo.keys), up[:, 512:542])
            if sb == NSB - 1:
                pq = P.ps(3 + (cc % 2))
                for kc in range(8):
                    P.mm(pq[0:32, 0:256], hT[:, kc, 480:512], w[:, kc, 0:256], start=(kc == 0), stop=(kc == 7))
                tst = tailst[cc % 2]
                P.act(tst[0:32, 0:128], pq[0:32, 128:256], AF.Sigmoid)
                P.tt("dve", tailu[0:32, cc * 128:(cc + 1) * 128], pq[0:32, 0:128], tst[0:32, 0:128], ALU.mult)

        def conv_taps(cc):
            up = upad[cc % 2]
            dcw = Dc[cc % 2]
            dcw3 = dcw.v(dcw.ap[:, 0:3968].rearrange("p (a b) -> p a b", a=31))
            pcv = P.ps(3 + (cc % 2))
            for tap in range(31):
                P.mm(pcv.all(), View(dcw3.ap[:, tap, :], dcw.keys), up[:, tap:tap + 512],
                     start=(tap == 0), stop=(tap == 30))
            col = PC_CDWB + cc
            P.act(ycv[:, cc, :], pcv.all(), AF.Identity, bias=prmT[:, col:col + 1])

        conv_proj(0)
        for cc in range(8):
            if cc + 1 < 8:
                conv_proj(cc + 1)
            conv_taps(cc)
        if sb == 0:
            dump("d_ycv", ycv.all())
        pmean = P.ps(5)
        pmsq = P.ps(6)
        for cc in range(8):
            P.copy("act", ysq[:, 0:512], ycv[:, cc, :])
            P.mm(pmean.all(), onesb.all(), ysq[:, 0:512], start=(cc == 0), stop=(cc == 7))
            P.act(ysq[:, 512:1024], ycv[:, cc, :], AF.Square)
            P.mm(pmsq.all(), onesb.all(), ysq[:, 512:1024], start=(cc == 0), stop=(cc == 7))
        P.ts("dve", mean_t.all(), pmean.all(), 1.0 / D, None, ALU.mult)
        P.tt("dve", var_t.all(), mean_t.all(), mean_t.all(), ALU.mult)
        P.stt("dve", var_t.all(), pmsq.all(), 1.0 / D, var_t.all(), ALU.mult, ALU.subtract)
        P.act(var_t.all(), var_t.all(), AF.Ln, bias=epsb[:, 0:1])
        P.act(var_t.all(), var_t.all(), AF.Exp, scale=-0.5)
        for cc in range(8):
            P.tt("dve", ycv[:, cc, :], ycv[:, cc, :], mean_t.all(), ALU.subtract)
            P.tt("dve", ycv[:, cc, :], ycv[:, cc, :], var_t.all(), ALU.mult)
            P.act(ycv[:, cc, :], ycv[:, cc, :], AF.Silu, bias=prmT[:, PC_LNB + cc:PC_LNB + cc + 1],
                  scale=prmT[:, PC_LNG + cc:PC_LNG + cc + 1])
            P.tt("dve", vt[:, cc, :], ycv[:, cc, :], sgt[:, cc, :], ALU.mult)

        if sb == 0:
            dump("d_vt", vt.all(), bf=True)
        if STOP <= 5:
            break
        P.tag = "sb%d.st5" % sb
        for dc in range(8):
            rb = w_get(jid[(sb, "s5", dc)])
            wco_t = rbv(rb, 0, 8, 128)
            wmc_t = rbv(rb, 1024, 8, 128)
            pa = P.ps(next_pp())
            for cc in range(8):
                P.mm(pa.all(), wco_t[:, cc, :], vt[:, cc, :], start=(cc == 0), stop=(cc == 7))
            pb = P.ps(2 + (dc % 2))
            for kc in range(8):
                P.mm(pb.all(), wmc_t[:, kc, :], hT[:, kc, :], start=(kc == 0), stop=(kc == 7))
            P.act(sgc.all(), pb.all(), AF.Sigmoid)
            P.tt("dve", sgc.all(), pa.all(), sgc.all(), ALU.mult)
            P.tt("dve", merged[:, dc, :], sgc.all(), gbs[:, dc, :], ALU.add)

        if sb == 0:
            dump("d_merged", merged.all(), bf=True)
        if STOP <= 6:
            break
        P.tag = "sb%d.st6" % sb
        wo_t = [rbv(w_get(jid[(sb, "wo", half)], hold=jid[(sb, "wo", 0)]), 0, 8, 512) for half in range(2)]
        for i in range(4):
            row0 = sb * TB + i * 128
            buf = i % 2
            load_x_tile(row0, buf)
            po = [P.ps(next_pp()), None]
            po[1] = P.ps(2 + (i % 2))
            for half in range(2):
                for kc in range(8):
                    P.mm(po[half].all(), merged[:, kc, i * 128:(i + 1) * 128], wo_t[half][:, kc, :],
                         start=(kc == 0), stop=(kc == 7))
            o_t = ot[buf]
            for half in range(2):
                P.copy("act", o_t[:, half * 512:(half + 1) * 512], po[half].all())
            P.stt("dve", junk[:, 1024:2048], o_t.all(), 1.0, o_t.all(), ALU.mult, ALU.mult, accum=ss[:, 4:5])
            P.act(ss[:, 6:7], ss[:, 4:5], AF.Ln, bias=epsb[:, 0:1], scale=1.0 / D)
            P.act(ss[:, 7:8], ss[:, 6:7], AF.Exp, scale=-0.5)
            P.tt("dve", o_t.all(), o_t.all(), GGp.all(), ALU.mult)
            P.stt("dve", o_t.all(), o_t.all(), ss[:, 7:8], xt[buf].all(), ALU.mult, ALU.add)
            P.dma("sp", y_prompt[row0:row0 + 128, :], o_t.all())

        P.tag = "tail"
        if sb == NSB - 1:
            P.dma("sp", o_conf_p, tailu[2:32, 0:1024])
            for g in range(8):
                pq = P.ps(g % 2)
                for j in range(2):
                    P.tr(pq[:, j * 128:(j + 1) * 128], HT[:, g, j * 128:(j + 1) * 128], identf.all())
                P.copy("act", xt[g % 2][:, 0:256], pq[:, 0:256])
                for j in range(2):
                    r0 = (2 * g + j) * 128
                    P.dma("sp", o_ssm_p[r0:r0 + 128, :], xt[g % 2][:, j * 128:(j + 1) * 128])

    if with_sample and STOP > 6:
        smp = P.sbt("smp", [128, 144], F32)
        P.dma("sp", smp.all(), c_smp[:, 0:144])
        bmk = P.sbt("bmk", [128, 1024], BF16)
        P.dma("pool", bmk.all(), c_smp[:, 144:1168])
        bdtri = smp[0:64, 0:64]
        samem = smp[0:64, 64:128]
        seqm = smp[0:64, 128:144]
        regs = {"a": [0, 20], "t": [20, 32], "b": [32, 44], "e": [44, 69], "c": [69, 92]}

        def SA(n, dtype, pattern=None, reg="c", **kw):
            lo, hi = regs[reg]
            assert lo + n <= hi, (reg, lo, n, hi)
            t_ = P.at(lo, n, dtype, pattern, **kw)
            regs[reg][0] = lo + n
            return t_
        xts = SA(2, F32, reg="a")
        xns = SA(1, BF16, reg="a")
        sm1 = SA(1, F32, reg="a")
        hTs = SA(1, BF16, reg="e")
        hTs3 = hTs.v(hTs.ap[:, 0:512].rearrange("p (a b) -> p a b", a=8))
        projS = SA(12, F32, reg="e")
        prS = projS.v(projS.ap[:, 0:5696].rearrange("p (a b) -> p a b", a=89))
        tokS_slot0 = regs["t"][0]
        tokS = SA(12, F32, reg="t")
        wts = [SA(6, BF16, reg="e") for _ in range(2)]
        rowsA = SA(8, F32, reg="a")
        rowsB_slot0 = regs["a"][0]
        rowsB = SA(8, F32, reg="a")
        upad_s = SA(1, BF16, reg="b")
        ups3 = upad_s.v(upad_s.ap[:, 0:544].rearrange("p (a b) -> p a b", a=16))
        xpad_s = SA(4, BF16, reg="c")
        xps4 = xpad_s.v(xpad_s.ap[:, 0:3584].rearrange("p (c a b) -> p c a b", c=32, a=16))
        xact_s = SA(2, BF16, "p (a b) -> p a b", reg="b", a=32)
        zs_s = SA(1, BF16, "p (a b) -> p a b", reg="b", a=16)
        ycv_s = SA(1, F32, "p (a b) -> p a b", reg="b", a=8)
        sgt_s = SA(1, BF16, "p (a b) -> p a b", reg="b", a=16)
        vt_s = SA(1, BF16, "p (a b) -> p a b", reg="b", a=16)
        yn_s = SA(1, BF16, "p (a b) -> p a b", reg="b", a=16)
        gbs_s = SA(1, F32, "p (a b) -> p a b", reg="b", a=8)
        mrg_s = SA(1, BF16, "p (a b) -> p a b", reg="b", a=16)
        dgs = SA(1, BF16, "p (a b) -> p a b", reg="b", a=8)
        Dcs = SA(4, BF16)
        Dcs3 = Dcs.v(Dcs.ap[:, 0:3968].rearrange("p (a b) -> p a b", a=31))
        GGs = SA(2, F32)
        sm2 = SA(1, F32)
        sm3 = SA(1, F32)
        sm4 = SA(1, F32)
        sb1 = SA(1, BF16)
        sb2 = SA(1, BF16)
        sb3 = SA(1, BF16)
        sb4 = SA(1, BF16)
        sb5 = SA(1, BF16)
        ots = SA(2, F32)
        dts = SA(1, F32)

        P.tag = "S1"
        P.dma("sp", xts[0:64, :], xs_in)
        P.stt("dve", wts[1][0:64, 0:1024], xts[0:64, :], 1.0, xts[0:64, :], ALU.mult, ALU.mult, accum=ss1[0:64, 0:1])
        P.act(ss1[0:64, 1:2], ss1[0:64, 0:1], AF.Ln, bias=epsb[0:64, 0:1], scale=1.0 / D)
        P.act(ss1[0:64, 2:3], ss1[0:64, 1:2], AF.Exp, scale=-0.5)
        P.ts("dve", xns[0:64, :], xts[0:64, :], ss1[0:64, 2:3], None, ALU.mult)
        pt = P.psb(2)
        for kc in range(8):
            P.tr(pt[:, kc * 64:(kc + 1) * 64], xns[0:64, kc * 128:(kc + 1) * 128], identb[0:64, 0:64])
        pt4 = pt.v(pt.ap[:, 0:512].rearrange("p (a b c) -> p a b c", a=8, b=16))
        Gs = View(G_all.ap[:, :, 1:17].unsqueeze(3).broadcast_to([128, 8, 16, 4]), G_all.keys)
        Ss = View(modT.ap[:, 0:8, 1:17].unsqueeze(3).broadcast_to([128, 8, 16, 4]), modT.keys)
        t4 = sm1.v(sm1.ap[:, 0:512].rearrange("p (a b c) -> p a b c", a=8, b=16))
        P.tt("dve", t4, pt4, Gs, ALU.mult)
        P.tt("dve", hTs.v(hTs.ap[:, 0:512].rearrange("p (a b c) -> p a b c", a=8, b=16)), t4, Ss, ALU.add)
        ggr = sm2.v(sm2.ap[:, 0:512].rearrange("p (a b c) -> p a b c", a=8, b=16))
        P.tt("dve", ggr, View(modT.ap[:, 16:24, 1:17].unsqueeze(3).broadcast_to([128, 8, 16, 4]), modT.keys),
             View(prmT.ap[:, PC_GPOST:PC_GPOST + 8].unsqueeze(2).unsqueeze(3).broadcast_to([128, 8, 16, 4]), prmT.keys),
             ALU.mult)
        for half in range(2):
            pg = P.ps(4 + half)
            for q in range(4):
                dc = half * 4 + q
                P.tr(pg[0:64, q * 128:(q + 1) * 128], sm2[:, dc * 64:(dc + 1) * 64], identf.all())
            P.copy("act", GGs[0:64, half * 512:(half + 1) * 512], pg[0:64, :])

        P.tag = "S2"
        tile_ctr = [0]

        def s2_load(slot, c_lo, c_hi):
            w = wts[tile_ctr[0] % 2]
            tile_ctr[0] += 1
            P.dma("sp", w[:, c_lo:c_hi], View(scr[slot, :, c_lo:c_hi], [("scr", slot)]))
            return w

        def fm_block(w, base, W, c0, pslot):
            wv = w.ap[:, base:base + 8 * W].rearrange("p (a b) -> p a b", a=8)
            pj = P.ps(next_pp())
            for kc in range(8):
                P.mm(pj[:, 0:64], View(wv[:, kc, c0:c0 + 128], w.keys), View(hTs3.ap[:, kc, :], hTs.keys),
                     start=(kc == 0), stop=(kc == 7))
            P.copy("act", View(prS.ap[:, pslot, :], projS.keys), pj[:, 0:64])

        def tm_block(w, base, W, c0, wid, tok_off):
            wv = w.ap[:, base:base + 8 * W].rearrange("p (a b) -> p a b", a=8)
            pj = P.ps(2 + (tile_ctr[0] % 2))
            for kc in range(8):
                P.mm(pj[0:64, 0:wid], View(hTs3.ap[:, kc, :], hTs.keys), View(wv[:, kc, c0:c0 + wid], w.keys),
                     start=(kc == 0), stop=(kc == 7))
            P.copy("act", tokS[0:64, tok_off:tok_off + wid], pj[0:64, 0:wid])

        for g in range(8):
            w = s2_load(g, 0, 6144)
            for c0, ps_ in ((0, 40 + 2 * g), (128, 41 + 2 * g), (256, 56 + g), (384, 64 + g),
                            (512, 24 + 2 * g), (640, 25 + 2 * g)):
                fm_block(w, 0, 768, c0, ps_)
            tm_block(w, 0, 768, 0, 256, 2048 + 256 * g)
            tm_block(w, 0, 768, 256, 128, 2048 + 2048 + 128 * g)
            tm_block(w, 0, 768, 384, 128, 2048 + 3072 + 128 * g)
        for dc in range(8):
            w = s2_load(8 + dc, 2048, 3072)
            fm_block(w, 2048, 128, 0, 81 + dc)
        for cc in range(8):
            w = s2_load(16 + cc, 0, 3072)
            fm_block(w, 0, 384, 0, cc)
            fm_block(w, 0, 384, 128, 8 + cc)
            fm_block(w, 0, 384, 256, 16 + cc)
            tm_block(w, 0, 384, 0, 128, cc * 128)
            tm_block(w, 0, 384, 128, 128, 1024 + cc * 128)
        for dc in range(8):
            w = s2_load(24 + dc, 1024, 2048)
            fm_block(w, 1024, 128, 0, 73 + dc)
        pdt = P.ps(3, 0)
        for kc in range(8):
            P.mm(pdt[0:64, 0:32], View(hTs3.ap[:, kc, :], hTs.keys), wdt[:, kc, :], start=(kc == 0), stop=(kc == 7))
        dt_s = dts[0:64, 0:32]
        dta_s = dts[0:64, 32:64]
        e_s = dts[0:64, 64:96]
        dend_s = dts[0:64, 96:128]
        P.tt("dve", dt_s, pdt[0:64, 0:32], dtb_bc[0:64, :], ALU.add)
        P.act(dt_s, dt_s, AF.Exp)
        P.act(dt_s, dt_s, AF.Ln, bias=1.0)
        P.tt("dve", dta_s, dt_s, a_bc[0:64, :], ALU.mult)
        pcq = P.ps(4, 0)
        pct = P.ps(5, 0)
        P.mm(pcq[0:64, 0:32], bdtri, dta_s)
        P.mm(pct[0:64, 0:32], samem, dta_s)
        P.act(e_s, pcq[0:64, 0:32], AF.Exp)
        P.copy("dve", sm3[0:64, 0:32], pcq[0:64, 0:32])
        P.tt("dve", dend_s, pct[0:64, 0:32], sm3[0:64, 0:32], ALU.subtract)
        P.act(dend_s, dend_s, AF.Exp)

        P.tag = "S3"
        P.act(sm4[0:64, :], tokS[0:64, 1024:1536], AF.Sigmoid)
        P.tt("dve", ots[0:64, 0:512], tokS[0:64, 0:512], sm4[0:64, :], ALU.mult)
        P.act(sm4[0:64, :], tokS[0:64, 1536:2048], AF.Sigmoid)
        P.tt("dve", ots[0:64, 512:1024], tokS[0:64, 512:1024], sm4[0:64, :], ALU.mult)
        for b in range(16):
            P.dma("sp", o_conf_s[b, 0:26, :], st_conf[b, 4:30, :])
            P.dma("sp", o_conf_s[b, 26:30, :], ots[4 * b:4 * b + 4, :])
            P.dma("sp", o_ssmc_s[b, :, :], tokS[4 * b + 1:4 * b + 4, 2048:6144])
        stc = st_conf.rearrange("b j d -> (b j) d")
        rA = rowsA.v(rowsA.ap.rearrange("p (a b) -> p a b", a=4))
        for rt in range(4):
            nr = 128 if rt < 3 else 96
            P.dma("sp", View(rA.ap[0:nr, rt, :], rowsA.keys), stc[rt * 128:rt * 128 + nr, :], grp="rowsA")
        for cc in range(8):
            ph = P.ps(6)
            for rt in range(4):
                nr = 128 if rt < 3 else 96
                P.tr(ph[:, rt * 128:rt * 128 + nr], View(rA.ap[0:nr, rt, cc * 128:(cc + 1) * 128], rowsA.keys),
                     identf[0:nr, 0:nr])
            P.copy("act", View(ups3.ap[:, :, 0:30], upad_s.keys),
                   ph.v(ph.ap[:, 0:480].rearrange("p (a b) -> p a b", a=16)))
            P.act(sm4[:, 0:64], View(prS.ap[:, 8 + cc, :], projS.keys), AF.Sigmoid)
            P.tt("dve", View(ups3.ap[:, :, 30:34], upad_s.keys),
                 View(prS.ap[:, cc, :].rearrange("p (a b) -> p a b", a=16), projS.keys),
                 sm4.v(sm4.ap[:, 0:64].rearrange("p (a b) -> p a b", a=16)), ALU.mult)
            P.act(sgt_s[:, cc, :], View(prS.ap[:, 16 + cc, :], projS.keys), AF.Silu)
            for tap in range(31):
                col = PC_CDW + tap * 8 + cc
                P.ts("dve", View(Dcs3.ap[:, tap, :], Dcs.keys), identb.all(), prmT[:, col:col + 1], None, ALU.mult)
            pcv = P.ps(7)
            pcv3 = pcv.v(pcv.ap[:, 0:64].rearrange("p (a b) -> p a b", a=16))
            for tap in range(31):
                P.mm(pcv3, View(Dcs3.ap[:, tap, :], Dcs.keys), View(ups3.ap[:, :, tap:tap + 4], upad_s.keys),
                     start=(tap == 0), stop=(tap == 30))
            P.act(ycv_s[:, cc, :], pcv[:, 0:64], AF.Identity, bias=prmT[:, PC_CDWB + cc:PC_CDWB + cc + 1])
        pmean = P.ps(5)
        pmsq = P.ps(6)
        for cc in range(8):
            P.copy("act", sb1[:, 0:64], ycv_s[:, cc, :])
            P.mm(pmean[:, 0:64], onesb.all(), sb1[:, 0:64], start=(cc == 0), stop=(cc == 7))
            P.act(sb1[:, 64:128], ycv_s[:, cc, :], AF.Square)
            P.mm(pmsq[:, 0:64], onesb.all(), sb1[:, 64:128], start=(cc == 0), stop=(cc == 7))
        mean_s = sm3[:, 64:128]
        var_s = sm3[:, 128:192]
        P.ts("dve", mean_s, pmean[:, 0:64], 1.0 / D, None, ALU.mult)
        P.tt("dve", var_s, mean_s, mean_s, ALU.mult)
        P.stt("dve", var_s, pmsq[:, 0:64], 1.0 / D, var_s, ALU.mult, ALU.subtract)
        P.act(var_s, var_s, AF.Ln, bias=epsb[:, 0:1])
        P.act(var_s, var_s, AF.Exp, scale=-0.5)
        for cc in range(8):
            P.tt("dve", ycv_s[:, cc, :], ycv_s[:, cc, :], mean_s, ALU.subtract)
            P.tt("dve", ycv_s[:, cc, :], ycv_s[:, cc, :], var_s, ALU.mult)
            P.act(ycv_s[:, cc, :], ycv_s[:, cc, :], AF.Silu, bias=prmT[:, PC_LNB + cc:PC_LNB + cc + 1],
                  scale=prmT[:, PC_LNG + cc:PC_LNG + cc + 1])
            P.tt("dve", vt_s[:, cc, :], ycv_s[:, cc, :], sgt_s[:, cc, :], ALU.mult)

        P.tag = "S4"
        stm = st_ssmc.rearrange("b j d -> (b j) d")
        rB = rowsB.v(rowsB.ap)
        P.dma("sp", View(rB.ap[0:48, :], rowsB.keys), stm)
        for ch in range(32):
            ph = P.ps(6)
            P.tr(ph[:, 0:48], View(rB.ap[0:48, ch * 128:(ch + 1) * 128], rowsB.keys), identf[0:48, 0:48])
            P.copy("act", View(xps4.ap[:, ch, :, 0:3], xpad_s.keys), ph.v(ph.ap[:, 0:48].rearrange("p (a b) -> p a b", a=16)))
            P.copy("dve", View(xps4.ap[:, ch, :, 3:7], xpad_s.keys),
                   View(prS.ap[:, 40 + ch, :].rearrange("p (a b) -> p a b", a=16), projS.keys))
            for tap in range(4):
                col = PC_SDW + tap * 32 + ch
                P.ts("dve", dgs[:, tap, :], identb.all(), prmT[:, col:col + 1], None, ALU.mult)
            pj = P.ps(7)
            pj3 = pj.v(pj.ap[:, 0:64].rearrange("p (a b) -> p a b", a=16))
            for tap in range(4):
                P.mm(pj3, dgs[:, tap, :], View(xps4.ap[:, ch, :, tap:tap + 4], xpad_s.keys),
                     start=(tap == 0), stop=(tap == 3))
            P.act(xact_s[:, ch, :], pj[:, 0:64], AF.Silu, bias=prmT[:, PC_SDWB + ch:PC_SDWB + ch + 1])
        for zc in range(16):
            P.act(zs_s[:, zc, :], View(prS.ap[:, 24 + zc, :], projS.keys), AF.Silu)

        P.tag = "S5"
        tokS8 = P.at(tokS_slot0, 8, F32)
        h0gs = [rowsA.v(rowsA.ap.rearrange("p (b r n) -> p b r n", b=16, r=2)),
                tokS8.v(tokS8.ap.rearrange("p (b r n) -> p b r n", b=16, r=2))]
        h0keys = [rowsA.keys, tokS8.keys]
        newhs = []
        for i in range(8):
            t_ = P.at(rowsB_slot0 + i, 1, F32)
            newhs.append(Tile(t_.ap[:, 0:256].rearrange("p (r n) -> p r n", r=2), t_.keys))
        nh_ctr = [0]

        def load_h0(g_):
            for r_ in range(2):
                src = st_v[:, 256 * g_ + 128 * r_:256 * g_ + 128 * r_ + 128, :].rearrange("b p n -> p b n")
                P.dma(("sp", "act")[r_], View(h0gs[g_ % 2].ap[:, :, r_, :], h0keys[g_ % 2]), src, grp=("h0", g_))
        h0T = wts[0].v(wts[0].ap[:, 0:4096].rearrange("p (b c) -> p b c", b=16))
        xdm = wts[1].v(wts[1].ap[:, 0:4096].rearrange("p (b c) -> p b c", b=16))
        cmask = sb1.v(sb1.ap.rearrange("p (a b) -> p a b", a=16))
        bmask = bmk.v(bmk.ap.rearrange("p (a b) -> p a b", a=16))
        xdt_s = sb2[0:64, 0:256]
        xdtd_s = sb3[0:64, 0:256]
        Btok_s = sb4[0:64, 0:128]
        MT_s = sb5.v(sb5.ap[0:64, 0:256].rearrange("p (a b) -> p a b", a=4))
        cbm_s = sm1[0:64, 0:64]
        R1_s = sm2.v(sm2.ap[0:64, 0:256].rearrange("p (a b) -> p a b", a=4))
        E_s = sm4.v(sm4.ap[0:64, 0:256].rearrange("p (a b) -> p a b", a=4))
        ytmp_s = sm3[0:64, 256:512]
        ytok_s = Dcs[0:64, 0:256]
        dtaB = ots[0:64, 0:128]
        cdcol = dts[:, 128:160]
        st_v = st_ssm.rearrange("b h p n -> b (h p) n")
        load_h0(0)
        for g in range(8):
            if g + 1 < 8:
                load_h0(g + 1)
            h0g = h0gs[g % 2]
            h0k = h0keys[g % 2]
            for b in range(16):
                ph = P.ps(next_pp())
                for r_ in range(2):
                    P.tr(ph[:, r_ * 128:(r_ + 1) * 128], View(h0g.ap[:, b, r_, :], h0k), identf.all())
                P.copy("act", View(h0T.ap[:, b, :], wts[0].keys), ph[:, 0:256])
            ptx = P.psb(2)
            for j in range(2):
                P.tr(ptx[0:64, j * 128:(j + 1) * 128], xact_s[:, 2 * g + j, :], identb.all())
            P.tt("dve", View(xdt_s.ap.rearrange("p (h q) -> p h q", h=4), sb2.keys),
                 ptx.v(ptx.ap[0:64, 0:256].rearrange("p (h q) -> p h q", h=4)),
                 View(dt_s.ap[:, 4 * g:4 * g + 4].unsqueeze(2).broadcast_to([64, 4, 64]), dts.keys), ALU.mult)
            P.tt("dve", View(xdtd_s.ap.rearrange("p (h q) -> p h q", h=4), sb3.keys),
                 View(xdt_s.ap.rearrange("p (h q) -> p h q", h=4), sb2.keys),
                 View(dend_s.ap[:, 4 * g:4 * g + 4].unsqueeze(2).broadcast_to([64, 4, 64]), dts.keys), ALU.mult)
            ptb = P.psb(3)
            P.tr(ptb[0:64, 0:128], xact_s[:, 16 + g, :], identb.all())
            P.copy("act", Btok_s, ptb[0:64, 0:128])
            pcb = P.ps(4)
            P.mm(pcb[0:64, 0:64], xact_s[:, 16 + g, :], xact_s[:, 24 + g, :])
            P.tt("dve", cbm_s, pcb[0:64, 0:64], bdtri, ALU.mult)
            P.tt("pool", R1_s, View(tri.ap[0:64, 0:64].unsqueeze(1).broadcast_to([64, 4, 64]), tri.keys),
                 View(dta_s.ap[:, 4 * g:4 * g + 4].unsqueeze(2).broadcast_to([64, 4, 64]), dts.keys), ALU.mult)
            pseg = P.ps(5)
            P.mm(pseg[0:64, 0:256], sut[0:64, 0:64], View(R1_s.ap.rearrange("p a b -> p (a b)"), sm2.keys))
            P.act(View(E_s.ap.rearrange("p a b -> p (a b)"), sm4.keys), pseg[0:64, 0:256], AF.Exp)
            P.tt("dve", MT_s, E_s, View(cbm_s.ap.unsqueeze(1).broadcast_to([64, 4, 64]), sm1.keys), ALU.mult)
            py1 = P.ps(7, 0)
            py2 = P.ps(7, 1)
            for h in range(4):
                P.mm(py1[0:64, h * 64:(h + 1) * 64], View(MT_s.ap[:, h, :], sb5.keys),
                     View(xdt_s.ap[:, h * 64:(h + 1) * 64], sb2.keys))
            P.tt("dve", cmask, View(xact_s.ap[:, 24 + g, :].unsqueeze(1).broadcast_to([128, 16, 64]), xact_s.keys),
                 bmask, ALU.mult)
            for b in range(16):
                P.mm(py2[0:64, :], View(cmask.ap[:, b, :], sb1.keys), View(h0T.ap[:, b, :], wts[0].keys),
                     start=(b == 0), stop=(b == 15))
            P.tt("dve", View(ytmp_s.ap.rearrange("p (h q) -> p h q", h=4), sm3.keys),
                 py2.v(py2.ap[0:64, :].rearrange("p (h q) -> p h q", h=4)),
                 View(e_s.ap[:, 4 * g:4 * g + 4].unsqueeze(2).broadcast_to([64, 4, 64]), dts.keys), ALU.mult)
            P.tt("dve", ytok_s, ytmp_s, py1[0:64, :], ALU.add)
            pyt = P.psb(2)
            for j in range(2):
                P.tr(pyt[:, j * 64:(j + 1) * 64], View(ytok_s.ap[:, j * 128:(j + 1) * 128], Dcs.keys), identb[0:64, 0:64])
            t1s = sm2.v(sm2.ap[:, 256:384].rearrange("p (a b) -> p a b", a=2))
            ygs = sm2.v(sm2.ap[:, 384:512].rearrange("p (a b) -> p a b", a=2))
            sqs = sb5.v(sb5.ap[:, 512:640].rearrange("p (a b) -> p a b", a=2))
            for j in range(2):
                P.stt("dve", View(t1s.ap[:, j, :], sm2.keys), xact_s[:, 2 * g + j, :], D_fm[:, 2 * g + j:2 * g + j + 1],
                      pyt[:, j * 64:(j + 1) * 64], ALU.mult, ALU.add)
                P.tt("dve", View(ygs.ap[:, j, :], sm2.keys), View(t1s.ap[:, j, :], sm2.keys), zs_s[:, 2 * g + j, :], ALU.mult)
                P.act(View(sqs.ap[:, j, :], sb5.keys), View(ygs.ap[:, j, :], sm2.keys), AF.Square)
            pms = P.ps(4)
            for j in range(2):
                P.mm(pms[:, 0:64], onesb.all(), View(sqs.ap[:, j, :], sb5.keys), start=(j == 0), stop=(j == 1))
            rs_s = sm1[:, 64:128]
            P.act(rs_s, pms[:, 0:64], AF.Ln, bias=epsb[:, 0:1], scale=1.0 / 256.0)
            P.act(rs_s, rs_s, AF.Exp, scale=-0.5)
            for j in range(2):
                col = PC_NG + 2 * g + j
                P.stt("dve", yn_s[:, 2 * g + j, :], View(ygs.ap[:, j, :], sm2.keys), prmT[:, col:col + 1], rs_s,
                      ALU.mult, ALU.mult)
            P.tt("dve", xdm.v(xdm.ap[0:64, :, :]), View(xdtd_s.ap.unsqueeze(1).broadcast_to([64, 16, 256]), sb3.keys),
                 View(seqm.ap.unsqueeze(2).broadcast_to([64, 16, 256]), smp.keys), ALU.mult)
            pcd = P.ps(6)
            for j in range(2):
                P.copy("dve", View(dtaB.ap.rearrange("p (h q) -> p h q", h=2), ots.keys),
                       View(dta_s.ap[:, 4 * g + 2 * j:4 * g + 2 * j + 2].unsqueeze(2).broadcast_to([64, 2, 64]), dts.keys))
                P.mm(pcd[:, j * 16:(j + 1) * 16], dtaB, seqm)
            P.act(cdcol, pcd[:, 0:32], AF.Exp)
            for b in range(16):
                nh = newhs[nh_ctr[0] % 8]
                nh_ctr[0] += 1
                for j in range(2):
                    pst = P.ps(next_pp())
                    P.mm(pst[:, 0:128], View(xdm.ap[0:64, b, j * 128:(j + 1) * 128], wts[1].keys), Btok_s)
                    P.stt("dve", nh[:, j, :], View(h0g.ap[:, b, j, :], h0k),
                          View(cdcol.ap[:, j * 16 + b:j * 16 + b + 1], dts.keys), pst[:, 0:128], ALU.mult, ALU.add)
                P.dma("sp", o_ssm_s[b, 256 * g:256 * g + 256, :].rearrange("(r p) n -> p r n", p=128), nh.all())

        P.tag = "S6"
        wso_v2 = w_ssm_out.rearrange("(cc cp) n -> cp cc n", cp=128)
        wco_v2 = w_conf_out.rearrange("(cc cp) n -> cp cc n", cp=128)
        s6b = [wts[0], wts[1], rowsA.v(rowsA.ap.bitcast(BF16)), rowsB.v(rowsB.ap.bitcast(BF16))]
        s6k = [wts[0].keys, wts[1].keys, rowsA.keys, rowsB.keys]
        for dc in range(8):
            wb_ = s6b[dc % 4]
            wk_ = s6k[dc % 4]
            wsoS = View(wb_.ap[:, 0:2048].rearrange("p (a b) -> p a b", a=16), wk_)
            wcoS = View(wb_.ap[:, 2048:3072].rearrange("p (a b) -> p a b", a=8), wk_)
            P.dma("sp", View(wb_.ap[:, 0:2048], wk_), View(scr[8 + dc, :, 0:2048], [("scr", 8 + dc)]))
            P.dma("sp", View(wb_.ap[:, 2048:3072], wk_), View(scr[24 + dc, :, 0:1024], [("scr", 24 + dc)]))
            pa = P.ps(next_pp())
            for cc in range(16):
                P.mm(pa[:, 0:64], View(wsoS.ap[:, cc, :], wk_), yn_s[:, cc, :], start=(cc == 0), stop=(cc == 15))
            P.act(sm1[:, 0:64], View(prS.ap[:, 81 + dc, :], projS.keys), AF.Sigmoid)
            P.tt("dve", gbs_s[:, dc, :], pa[:, 0:64], sm1[:, 0:64], ALU.mult)
            pb = P.ps(next_pp())
            for cc in range(8):
                P.mm(pb[:, 0:64], View(wcoS.ap[:, cc, :], wk_), vt_s[:, cc, :], start=(cc == 0), stop=(cc == 7))
            P.act(sm1[:, 64:128], View(prS.ap[:, 73 + dc, :], projS.keys), AF.Sigmoid)
            P.tt("dve", sm1[:, 128:192], pb[:, 0:64], sm1[:, 64:128], ALU.mult)
            P.tt("dve", mrg_s[:, dc, :], sm1[:, 128:192], gbs_s[:, dc, :], ALU.add)
        wo_v2 = w_o.rearrange("(kc kp) n -> kp kc n", kp=128)
        for half in range(2):
            P.dma("sp", wts[half][:, 0:4096], View(scr[32 + half, :, 0:4096], [("scr", 32 + half)]))
        for half in range(2):
            wov = wts[half].ap[:, 0:4096].rearrange("p (a b) -> p a b", a=8)
            po_ = P.ps(next_pp())
            for kc in range(8):
                P.mm(po_[0:64, :], mrg_s[:, kc, :], View(wov[:, kc, :], wts[half].keys), start=(kc == 0), stop=(kc == 7))
            P.copy("act", ots[0:64, half * 512:(half + 1) * 512], po_[0:64, :])
        P.stt("dve", sb1[0:64, 0:1024], ots[0:64, :], 1.0, ots[0:64, :], ALU.mult, ALU.mult, accum=ss[0:64, 4:5])
        P.act(ss[0:64, 6:7], ss[0:64, 4:5], AF.Ln, bias=epsb[0:64, 0:1], scale=1.0 / D)
        P.act(ss[0:64, 7:8], ss[0:64, 6:7], AF.Exp, scale=-0.5)
        P.stt("dve", ots[0:64, :], ots[0:64, :], ss[0:64, 7:8], GGs[0:64, :], ALU.mult, ALU.mult)
        P.tt("pool", ots[0:64, :], ots[0:64, :], xts[0:64, :], ALU.add)
        P.dma("sp", y_sample, ots[0:64, :])

    P.emit()
    return nc, P, stack


_CONSTS = None


def _consts():
    global _CONSTS
    if _CONSTS is None:
        i = np.arange(128)
        ident = np.eye(128, dtype=np.float32)
        tri = (i[:, None] <= i[None, :]).astype(np.float32)
        sut = (i[:, None] > i[None, :]).astype(np.float32)
        smp = np.zeros((128, 1168), np.float32)
        t = np.arange(64)
        same = (t[:, None] // 4 == t[None, :] // 4)
        smp[:64, 0:64] = (same & (t[:, None] <= t[None, :])).astype(np.float32)
        smp[:64, 64:128] = same.astype(np.float32)
        smp[:64, 128:144] = (t[:, None] // 4 == np.arange(16)[None, :]).astype(np.float32)
        bm = (np.arange(16)[:, None] == t[None, :] // 4).astype(np.float32).reshape(1, 1024)
        smp[:, 144:1168] = bm
        _CONSTS = dict(c_ident=ident, c_tri=tri, c_sut=sut, c_smp=smp)
    return _CONSTS


def make_in_maps(inp, ncores=NCORE, NSB=4):
    f = lambda a: np.ascontiguousarray(np.asarray(a, dtype=np.float32))
    L = NSB * TB
    shared = dict(
        w_ada=f(inp["w_ada"][0]), b_ada=f(inp["b_ada"][0]), g_pre=f(inp["g_pre"][0]), g_post=f(inp["g_post"][0]),
        w_in=f(inp["w_in"][0]), conf_dw_w=f(inp["conf_dw_w"][0]), conf_dw_b=f(inp["conf_dw_b"][0]),
        conf_ln_g=f(inp["conf_ln_g"][0]), conf_ln_b=f(inp["conf_ln_b"][0]), w_conf_out=f(inp["w_conf_out"][0]),
        ssm_dw_w=f(inp["ssm_dw_w"][0]), ssm_dw_b=f(inp["ssm_dw_b"][0]), ssm_dt_bias=f(inp["ssm_dt_bias"][0]),
        ssm_a_log=f(inp["ssm_a_log"][0]), ssm_d=f(inp["ssm_d"][0]), ssm_norm_g=f(inp["ssm_norm_g"][0]),
        w_ssm_out=f(inp["w_ssm_out"][0]), w_o=f(inp["w_o"][0]), **_consts())
    maps = []
    for c in range(ncores):
        s = slice(16 * c, 16 * c + 16)
        m = dict(shared)
        m["xp"] = f(inp["x_prompt"][c][:L])
        m["xs"] = f(inp["x_sample"][s]).reshape(64, D)
        m["call"] = f(np.concatenate([np.asarray(inp["c_prompt"])[c:c + 1], np.asarray(inp["c_sample"])[s]], axis=0))
        m["st_conf"] = f(inp["state_conf_conv"][0, s])
        m["st_ssmc"] = f(inp["state_ssm_conv"][0, s])
        m["st_ssm"] = f(inp["state_ssm"][0, s])
        maps.append(m)
    return maps


_NC_CACHE = {}


def kernel(**inputs):
    if "full" not in _NC_CACHE:
        _NC_CACHE["full"] = build_nc(4, True)
    nc = _NC_CACHE["full"][0]
    maps = make_in_maps(inputs)
    res = run_bass_kernel_spmd(nc, maps, core_ids=list(range(NCORE)))
    R = res.results
    y_prompt = np.stack([R[c]["y_prompt"] for c in range(NCORE)], axis=0)
    y_sample = np.concatenate([R[c]["y_sample"].reshape(16, 4, D) for c in range(NCORE)], axis=0)
    conf_p = np.stack([R[c]["o_conf_p"] for c in range(NCORE)], axis=0)[None]
    ssmc_p = np.stack([R[c]["o_ssmc_p"] for c in range(NCORE)], axis=0)[None]
    ssm_p = np.stack([R[c]["o_ssm_p"].reshape(32, 64, 128) for c in range(NCORE)], axis=0)[None]
    conf_s = np.concatenate([R[c]["o_conf_s"] for c in range(NCORE)], axis=0)[None]
    ssmc_s = np.concatenate([R[c]["o_ssmc_s"] for c in range(NCORE)], axis=0)[None]
    ssm_s = np.concatenate([R[c]["o_ssm_s"].reshape(16, 32, 64, 128) for c in range(NCORE)], axis=0)[None]
    return (y_prompt, y_sample, conf_p, ssmc_p, ssm_p, conf_s, ssmc_s, ssm_s)
```

```python
import contextlib
import numpy as np
import concourse.bass as bass
import concourse.mybir as mybir
from concourse.bass_utils import run_bass_kernel_spmd

F32 = mybir.dt.float32
BF16 = mybir.dt.bfloat16
AF = mybir.ActivationFunctionType
ALU = mybir.AluOpType

D = 1024
NCORE = 8
SEQ = 2048
TB = 512
IN_DIM = 11296
OFF_VAL, OFF_GLU, OFF_GATE, OFF_Z, OFF_XS, OFF_B, OFF_C, OFF_DT, OFF_MC, OFF_MS = (
    0, 1024, 2048, 3072, 5120, 7168, 8192, 9216, 9248, 10272)
EPS = 1e-6
NDMA_SEM = 32
NSW_SEM = 16


class View:
    def __init__(self, ap, keys):
        self.ap = ap
        self.keys = keys

    def v(self, ap):
        return View(ap, self.keys)

    def __getitem__(self, idx):
        return View(self.ap[idx], self.keys)


class Tile:
    def __init__(self, ap, keys):
        self.ap = ap
        self.keys = list(keys)

    def __getitem__(self, idx):
        return View(self.ap[idx], self.keys)

    def v(self, ap):
        return View(ap, self.keys)

    def all(self):
        return View(self.ap, self.keys)


class Op:
    __slots__ = ("idx", "eng", "fn", "dma", "raw", "war", "needed", "sig", "prev_dma", "cost", "xfer", "tbl", "tag",
                 "t0", "t1")

    def __init__(self, idx, eng, fn, dma):
        self.idx, self.eng, self.fn, self.dma = idx, eng, fn, dma
        self.cost = 0.1
        self.xfer = 0.0
        self.tbl = None
        self.raw, self.war = set(), set()
        self.needed = False
        self.sig = None
        self.prev_dma = None


def _ap(x):
    return x.ap if isinstance(x, View) else x


def _fsz(x):
    sh = _ap(x).shape
    n = 1
    for d in sh[1:]:
        n *= d
    return n


def _nbytes(x):
    a = _ap(x)
    n = 1
    for d in a.shape:
        n *= d
    return n * (4 if a.dtype == F32 else 2)


_TBL = {}


class Prog:
    def __init__(self, nc, stack):
        self.nc = nc
        self.stack = stack
        self.ops = []
        self.state = {}
        self.pstate = {}
        self.dma_ops = []

    def sbt(self, name, shape, dtype):
        h = self.stack.enter_context(self.nc.sbuf_tensor(name, list(shape), dtype))
        return Tile(h[:], [(name,)])

    def make_arena(self, nslots):
        self.arena = self.stack.enter_context(self.nc.sbuf_tensor("arena", [128, nslots * 512], F32))
        self.nslots = nslots

    def at(self, slot0, nslots, dtype, pattern=None, **kw):
        assert slot0 + nslots <= self.nslots
        ap = self.arena[:, slot0 * 512:(slot0 + nslots) * 512]
        if dtype == BF16:
            ap = ap.bitcast(BF16)
        if pattern is not None:
            ap = ap.rearrange(pattern, **kw)
        return Tile(ap, [("A", s) for s in range(slot0, slot0 + nslots)])

    def make_psum(self):
        self.psum = []
        for i in range(8):
            h = self.stack.enter_context(self.nc.psum_tensor("ps%d" % i, [128, 512], F32))
            self.psum.append(h)

    def ps(self, bank, half=None):
        ap = self.psum[bank][:]
        if half is None:
            return Tile(ap, [("ps", bank)])
        return Tile(ap[:, half * 256:(half + 1) * 256], [("ps", bank)])

    def psb(self, bank, half=None):
        ap = self.psum[bank][:].bitcast(BF16)
        if half is None:
            return Tile(ap, [("ps", bank)])
        return Tile(ap[:, half * 512:(half + 1) * 512], [("ps", bank)])

    def op(self, eng, fn, r=(), w=(), dma=False, grp=None):
        o = Op(len(self.ops), eng, fn, dma)
        o.tag = getattr(self, "tag", "")
        rk, wk = [], []
        for x in r:
            if isinstance(x, (View, Tile)):
                rk.extend(x.keys)
        for x in w:
            if isinstance(x, (View, Tile)):
                wk.extend(x.keys)
        prk = [k for k in rk if k[0] == "ps"]
        pwk = [k for k in wk if k[0] == "ps"]
        rk = [k for k in rk if k[0] != "ps"]
        wk = [k for k in wk if k[0] != "ps"]
        for k in prk:
            st = self.pstate.get(k)
            if st is not None:
                (o.raw if st[1] else o.war).add(st[0])
        for k in pwk:
            st = self.pstate.get(k)
            if st is not None:
                (o.raw if st[1] else o.war).add(st[0])
        for k in prk:
            if k not in pwk:
                self.pstate[k] = [o.idx, False]
        for k in pwk:
            self.pstate[k] = [o.idx, True]
        for k in rk:
            st = self.state.get(k)
            if st is not None:
                o.raw.update(st[0])
        for k in wk:
            st = self.state.get(k)
            if st is not None:
                if grp is not None and st[2] == grp:
                    o.raw.update(st[3])
                    o.war.update(st[4])
                    o.war.update(st[1])
                else:
                    o.raw.update(st[0])
                    o.war.update(st[1])
        for k in rk:
            self.state.setdefault(k, [[], [], None, [], []])[1].append(o.idx)
        for k in wk:
            st = self.state.get(k)
            if st is not None and grp is not None and st[2] == grp:
                st[0].append(o.idx)
            else:
                pw = st[0] if st is not None else []
                pr = st[1] if st is not None else []
                self.state[k] = [[o.idx], [], grp, pw, pr]
        o.raw.discard(o.idx)
        o.war.discard(o.idx)
        self.ops.append(o)
        if dma:
            n = len(self.dma_ops)
            if n >= NDMA_SEM:
                o.prev_dma = self.dma_ops[n - NDMA_SEM]
            self.dma_ops.append(o.idx)
        return o

    def mm(self, out, lhsT, rhs, start=True, stop=True):
        o = self.op("pe", lambda e: e.matmul(_ap(out), lhsT=_ap(lhsT), rhs=_ap(rhs), start=start, stop=stop),
                    r=[lhsT, rhs], w=[out])
        o.cost = max(_fsz(out), 64) / 2050.0 * (4.0 if _ap(lhsT).dtype == F32 else 1.0) + 0.01

    def tr(self, out, in_, ident):
        o = self.op("pe", lambda e: e.transpose(_ap(out), _ap(in_), _ap(ident)), r=[in_, ident], w=[out])
        o.cost = max(_fsz(out), 64) / 2050.0 * (4.0 if _ap(in_).dtype == F32 else 1.0) + 0.01

    def act(self, out, in_, func, bias=None, scale=None):
        kw = {}
        if bias is not None:
            kw["bias"] = _ap(bias)
        if scale is not None:
            kw["scale"] = _ap(scale)
        o = self.op("act", lambda e: e.activation(out=_ap(out), in_=_ap(in_), func=func, **kw),
                    r=[in_, bias, scale], w=[out])
        o.cost = 0.15 + _fsz(out) / 1250.0
        o.tbl = {AF.Exp: "e", AF.Ln: "e", AF.Silu: "s", AF.Tanh: "s", AF.Sigmoid: "g", AF.Sqrt: "q"}.get(func)

    def _vcost(self, o, eng, out, k=1.0):
        n = _fsz(out)
        o.cost = (0.12 + n / 960.0 * k) if eng == "dve" else (0.15 + n / 500.0)

    def tt(self, eng, out, in0, in1, op):
        o = self.op(eng, lambda e: e.tensor_tensor(out=_ap(out), in0=_ap(in0), in1=_ap(in1), op=op),
                    r=[in0, in1], w=[out])
        self._vcost(o, eng, out)

    def ts(self, eng, out, in0, s1, s2, op0, op1=None):
        if op1 is None:
            o = self.op(eng, lambda e: e.tensor_scalar(out=_ap(out), in0=_ap(in0), scalar1=_ap(s1), scalar2=None,
                                                       op0=op0), r=[in0, s1], w=[out])
        else:
            o = self.op(eng, lambda e: e.tensor_scalar(out=_ap(out), in0=_ap(in0), scalar1=_ap(s1),
                                                       scalar2=_ap(s2), op0=op0, op1=op1),
                        r=[in0, s1, s2], w=[out])
        self._vcost(o, eng, out, 0.5)

    def stt(self, eng, out, in0, scalar, in1, op0, op1, accum=None):
        if accum is None:
            o = self.op(eng, lambda e: e.scalar_tensor_tensor(out=_ap(out), in0=_ap(in0), scalar=_ap(scalar),
                                                              in1=_ap(in1), op0=op0, op1=op1),
                        r=[in0, scalar, in1], w=[out])
        else:
            o = self.op(eng, lambda e: e.scalar_tensor_tensor(out=_ap(out), in0=_ap(in0), scalar=_ap(scalar),
                                                              in1=_ap(in1), op0=op0, op1=op1,
                                                              accum_out=_ap(accum)),
                        r=[in0, scalar, in1], w=[out, accum])
        self._vcost(o, eng, out)

    def copy(self, eng, out, in_):
        if eng == "act":
            o = self.op("act", lambda e: e.activation(out=_ap(out), in_=_ap(in_), func=AF.Copy), r=[in_], w=[out])
            o.cost = 0.2 + _fsz(out) / 1200.0
        else:
            o = self.op(eng, lambda e: e.tensor_copy(out=_ap(out), in_=_ap(in_)), r=[in_], w=[out])
            self._vcost(o, eng, out, 0.5)

    def sigmoid(self, out, in_):
        self.act(out, in_, AF.Tanh, scale=0.5)
        self.ts("dve", out, out, 0.5, 0.5, ALU.mult, ALU.add)

    def recip(self, out, in_):
        o = self.op("dve", lambda e: e.reciprocal(out=_ap(out), in_=_ap(in_)), r=[in_], w=[out])
        o.cost = 0.12 + _fsz(out) * 6.3 / 960.0

    def memset(self, eng, out, val):
        self.op(eng, lambda e: e.memset(_ap(out), val), r=[], w=[out])

    def dma(self, eng, out, in_, slow=False, grp=None):
        if grp is None:
            grp = getattr(self, "cur_grp", None)
        if slow:
            o = self.op(eng, lambda e: e.dma_start(out=_ap(out), in_=_ap(in_), allow_slow_non_contiguous=True),
                        r=[in_], w=[out], dma=True, grp=grp)
        else:
            o = self.op(eng, lambda e: e.dma_start(out=_ap(out), in_=_ap(in_)), r=[in_], w=[out], dma=True, grp=grp)
        nb = _nbytes(out)
        if eng == "pool":
            o.cost = 1.05
            o.xfer = 2.0 + nb / 90e3
        else:
            o.cost = 0.5
            o.xfer = 2.0 + nb / 260e3

    def schedule(self):
        import heapq
        ops = self.ops
        n = len(ops)
        preds = [sorted(o.raw | o.war) for o in ops]
        succ = [[] for _ in range(n)]
        indeg = [0] * n
        for i, ps in enumerate(preds):
            indeg[i] = len(ps)
            for p in ps:
                succ[p].append(i)
        prio = [0.0] * n
        for i in range(n - 1, -1, -1):
            o = ops[i]
            m = 0.0
            for s_ in succ[i]:
                if prio[s_] > m:
                    m = prio[s_]
            prio[i] = m + o.cost + o.xfer
        fin = [0.0] * n
        efree = {"pe": 0.0, "act": 0.0, "dve": 0.0, "pool": 0.0, "sp": 0.0}
        swdge_free = [0.0]
        last_tbl = [None]
        ready = {e: [] for e in efree}
        dep_t = [0.0] * n
        for i in range(n):
            if indeg[i] == 0:
                ready[ops[i].eng].append(i)
        order = []
        LAT = 0.35
        while len(order) < n:
            best = None
            for e, lst in ready.items():
                if not lst:
                    continue
                ef = efree[e]
                cb = None
                for i in lst:
                    st = dep_t[i] if dep_t[i] > ef else ef
                    if e == "act" and ops[i].tbl is not None and last_tbl[0] is not None and ops[i].tbl != last_tbl[0]:
                        st += 1.3
                    key = (st, -prio[i], i)
                    if cb is None or key < cb:
                        cb = key
                if best is None or cb < best[0]:
                    best = (cb, e)
            (st, _, i), e = best
            ready[e].remove(i)
            o = ops[i]
            if e == "act" and o.tbl is not None:
                last_tbl[0] = o.tbl
            end_eng = st + o.cost
            efree[e] = end_eng
            if o.dma:
                if e == "pool":
                    t0 = max(end_eng, swdge_free[0])
                    fin[i] = t0 + o.xfer
                    swdge_free[0] = fin[i] - 2.0
                else:
                    fin[i] = end_eng + o.xfer
            else:
                fin[i] = end_eng
            order.append(i)
            o.t0, o.t1 = st, fin[i]
            for s_ in succ[i]:
                t = fin[i] + (LAT if ops[s_].eng != e or o.dma else 0.0)
                if t > dep_t[s_]:
                    dep_t[s_] = t
                indeg[s_] -= 1
                if indeg[s_] == 0:
                    ready[ops[s_].eng].append(s_)
        self.est_time = max(fin) if fin else 0.0
        return order

    def emit(self, reorder=True):
        nc = self.nc
        ops = self.ops
        if reorder:
            order = self.schedule()
        else:
            order = list(range(len(ops)))
        pos = {i: k for k, i in enumerate(order)}
        self.dma_ops = [i for i in order if ops[i].dma]
        self.dma_sw = [i for i in self.dma_ops if ops[i].eng == "pool"]
        self.dma_hw = [i for i in self.dma_ops if ops[i].eng != "pool"]
        for lst, nsem in ((self.dma_sw, NSW_SEM), (self.dma_hw, NDMA_SEM)):
            for n_, i in enumerate(lst):
                ops[i].prev_dma = lst[n_ - nsem] if n_ >= nsem else None
        for o in ops:
            deps = set()
            for d in o.raw:
                p = ops[d]
                if p.eng == o.eng and not p.dma and not o.dma and o.eng == "pe":
                    continue
                deps.add(d)
            for d in o.war:
                p = ops[d]
                if p.eng == o.eng and not p.dma and not o.dma and o.eng == "pe":
                    continue
                deps.add(d)
            if o.prev_dma is not None:
                deps.add(o.prev_dma)
            o.raw = deps
            for d in deps:
                ops[d].needed = True
        for i in self.dma_ops:
            ops[i].needed = True
        engs = ["pe", "act", "dve", "pool", "sp"]
        esem = {e: self.stack.enter_context(nc.semaphore("sem_" + e)) for e in engs}
        dsem = [self.stack.enter_context(nc.semaphore("dsem%d" % i)) for i in range(NDMA_SEM)]
        ssem = [self.stack.enter_context(nc.semaphore("ssem%d" % i)) for i in range(NSW_SEM)]
        cnt = {e: 0 for e in engs}
        for n, i in enumerate(self.dma_hw):
            ops[i].sig = (dsem[n % NDMA_SEM], 16 * (n // NDMA_SEM + 1), ("d", n % NDMA_SEM))
        for n, i in enumerate(self.dma_sw):
            ops[i].sig = (ssem[n % NSW_SEM], 16 * (n // NSW_SEM + 1), ("s", n % NSW_SEM))
        for i in order:
            o = ops[i]
            if o.dma:
                continue
            if o.needed:
                cnt[o.eng] += 1
                o.sig = (esem[o.eng], cnt[o.eng], ("e", o.eng))
        final = {}
        for n, i in enumerate(self.dma_ops):
            s = ops[i].sig
            final[s[2]] = (s[0], max(final.get(s[2], (None, 0))[1], s[1]))
        by_eng = {e: [ops[i] for i in order if ops[i].eng == e] for e in engs}
        self.n_instr = {e: len(by_eng[e]) for e in engs}

        def run(eng_name, e):
            waited = {}
            for o in by_eng[eng_name]:
                need = {}
                for d in o.raw:
                    s = ops[d].sig
                    if s[1] > need.get(s[2], (None, 0))[1]:
                        need[s[2]] = (s[0], s[1])
                for k, (sem, val) in need.items():
                    if waited.get(k, 0) >= val:
                        continue
                    e.wait_ge(sem, val)
                    waited[k] = val
                ins = o.fn(e)
                if o.sig is not None:
                    ins.then_inc(o.sig[0], 16 if o.dma else 1)
            if eng_name == "sp":
                for k, (sem, val) in final.items():
                    e.wait_ge(sem, val)

        with nc.Block() as block:
            @block.tensor
            def _(e):
                run("pe", e)

            @block.scalar
            def _(e):
                run("act", e)

            @block.vector
            def _(e):
                run("dve", e)

            @block.gpsimd
            def _(e):
                run("pool", e)

            @block.sync
            def _(e):
                run("sp", e)


def build_nc(NSB=4, with_sample=True, STOP=99, DBG=False):
    nc = bass.Bass("TRN2", target_bir_lowering=False)
    stack = contextlib.ExitStack()
    P = Prog(nc, stack)
    L = NSB * TB

    def din(name, shape):
        return nc.dram_tensor(name, list(shape), F32, kind="ExternalInput").ap()

    def dout(name, shape):
        return nc.dram_tensor(name, list(shape), F32, kind="ExternalOutput").ap()

    xp = din("xp", [L, D])
    xs_in = din("xs", [64, D])
    call = din("call", [17, D])
    st_conf = din("st_conf", [16, 30, D])
    st_ssmc = din("st_ssmc", [16, 3, 4096])
    st_ssm = din("st_ssm", [16, 32, 64, 128])
    w_ada = din("w_ada", [D, 3 * D])
    b_ada = din("b_ada", [3 * D])
    g_pre = din("g_pre", [D])
    g_post = din("g_post", [D])
    w_in = din("w_in", [D, IN_DIM])
    conf_dw_w = din("conf_dw_w", [31, D])
    conf_dw_b = din("conf_dw_b", [D])
    conf_ln_g = din("conf_ln_g", [D])
    conf_ln_b = din("conf_ln_b", [D])
    w_conf_out = din("w_conf_out", [D, D])
    ssm_dw_w = din("ssm_dw_w", [4, 4096])
    ssm_dw_b = din("ssm_dw_b", [4096])
    ssm_dt_bias = din("ssm_dt_bias", [32])
    ssm_a_log = din("ssm_a_log", [32])
    ssm_d = din("ssm_d", [32])
    ssm_norm_g = din("ssm_norm_g", [2048])
    w_ssm_out = din("w_ssm_out", [2048, D])
    w_o = din("w_o", [D, D])
    c_ident = din("c_ident", [128, 128])
    c_tri = din("c_tri", [128, 128])
    c_sut = din("c_sut", [128, 128])
    c_smp = din("c_smp", [128, 1168])

    y_prompt = dout("y_prompt", [L, D])
    y_sample = dout("y_sample", [64, D])
    o_conf_p = dout("o_conf_p", [30, D])
    o_ssmc_p = dout("o_ssmc_p", [3, 4096])
    o_ssm_p = dout("o_ssm_p", [32 * 64, 128])
    o_conf_s = dout("o_conf_s", [16, 30, D])
    o_ssmc_s = dout("o_ssmc_s", [16, 3, 4096])
    o_ssm_s = dout("o_ssm_s", [16, 32 * 64, 128])

    win_v = w_in.rearrange("(kc kp) n -> kp kc n", kp=128)
    dbg = {}
    if DBG:
        for nm, shp in (("d_hT", [128, 8 * 512]), ("d_dt", [128, 512]), ("d_dt2", [128, 512]), ("d_xact", [128, 4 * 512]),
                        ("d_yn", [128, 16 * 512]), ("d_gbs", [128, 8 * 512]), ("d_vt", [128, 8 * 512]),
                        ("d_merged", [128, 8 * 512]), ("d_prmT", [128, 488]), ("d_modT", [128, 24 * 17]),
                        ("d_ytok", [128, 1024]), ("d_ycv", [128, 8 * 512]), ("d_GGp", [128, 1024])):
            dbg[nm] = dout(nm, shp)
    dbf = {}

    def dump(nm, view, bf=False):
        if not DBG:
            return
        if bf:
            P.dma("pool", dbg[nm], View(view.ap if len(view.ap.shape) == 2 else view.ap.rearrange("p a b -> p (a b)"), view.keys))
        else:
            P.dma("sp", dbg[nm], View(view.ap if len(view.ap.shape) == 2 else view.ap.rearrange("p a b -> p (a b)"), view.keys))

    P.make_psum()
    identf = P.sbt("identf", [128, 128], F32)
    identb = P.sbt("identb", [128, 128], BF16)
    tri = P.sbt("tri", [128, 128], F32)
    sut = P.sbt("sut", [128, 128], F32)
    sutb = P.sbt("sutb", [128, 128], BF16)
    onesf = P.sbt("onesf", [128, 128], F32)
    onesb = P.sbt("onesb", [128, 128], BF16)
    prmT = P.sbt("prmT", [128, 488], F32)
    modT = P.sbt("modT", [128, 24, 17], F32)
    G_all = P.sbt("G_all", [128, 8, 17], F32)
    dtb_bc = P.sbt("dtb_bc", [128, 32], F32)
    a_bc = P.sbt("a_bc", [128, 32], F32)
    D_fm = P.sbt("D_fm", [128, 16], F32)
    wdt = P.sbt("wdt", [128, 8, 32], BF16)
    GGp = P.sbt("GGp", [128, D], F32)
    ss = P.sbt("ss", [128, 8], F32)
    epsb = P.sbt("epsb", [128, 1], F32)

    tailst = [P.sbt("tailst%d" % i, [32, 512], F32) for i in range(2)]
    tailu = P.sbt("tailu", [32, 1024], F32)
    NSLOT = 92
    P.make_arena(NSLOT)
    hT = P.at(0, 4, BF16, "p (a b) -> p a b", a=8)
    gbs = P.at(4, 4, BF16, "p (a b) -> p a b", a=8)
    hT2 = P.at(8, 4, BF16, "p (a b) -> p a b", a=8)
    HT = P.at(12, 4, F32, "p (a b) -> p a b", a=8)
    uhalo = P.at(16, 1, BF16)
    xhalo = P.at(17, 1, BF16)
    dt_t = P.at(18, 1, F32)
    dt2_t = P.at(19, 1, F32)
    R0 = 20

    def prm(c0, n):
        return prmT[:, c0:c0 + n]

    P.dma("sp", identf.all(), c_ident)
    P.dma("pool", identb.all(), c_ident)
    P.dma("sp", tri.all(), c_tri)
    P.dma("sp", sut.all(), c_sut)
    P.dma("pool", sutb.all(), c_sut)
    P.memset("pool", onesf.all(), 1.0)
    P.memset("pool", onesb.all(), 1.0)
    P.memset("pool", epsb.all(), EPS)
    P.memset("pool", HT.all(), 0.0)
    P.memset("pool", uhalo.all(), 0.0)
    P.memset("pool", xhalo.all(), 0.0)
    P.dma("sp", dtb_bc.all(), ssm_dt_bias.rearrange("(o n) -> o n", o=1).partition_broadcast(128)[:, 0, :])
    P.dma("sp", a_bc.all(), ssm_a_log.rearrange("(o n) -> o n", o=1).partition_broadcast(128)[:, 0, :])
    P.dma("sp", D_fm[0:64, :], bass.AP(tensor=ssm_d.tensor, offset=0, ap=[[0, 64], [2, 16]]), slow=True, grp="dfm")
    P.dma("sp", D_fm[64:128, :], bass.AP(tensor=ssm_d.tensor, offset=1, ap=[[0, 64], [2, 16]]), slow=True, grp="dfm")
    P.dma("pool", wdt.all(), win_v[:, :, OFF_DT:OFF_DT + 32])
    P.act(a_bc.all(), a_bc.all(), AF.Exp)
    P.ts("dve", a_bc.all(), a_bc.all(), -1.0, None, ALU.mult)

    stg = P.at(R0, 1, F32)
    cw = conf_dw_w.rearrange("j (c p) -> (j c) p", p=128)
    sw = ssm_dw_w.rearrange("j (c p) -> (j c) p", p=128)

    def v128(x):
        return x.rearrange("(c p) -> c p", p=128)
    groups = [
        (0, 128, [(cw[0:128, :], 128)]),
        (128, 120, [(cw[128:248, :], 120)]),
        (248, 128, [(sw, 128)]),
        (376, 112, [(v128(g_pre), 8), (v128(g_post), 8), (v128(b_ada), 24), (v128(conf_dw_b), 8),
                    (v128(conf_ln_g), 8), (v128(conf_ln_b), 8), (v128(ssm_dw_b), 32), (v128(ssm_norm_g), 16)]),
    ]
    for gi, (c0, nrow, srcs) in enumerate(groups):
        r0 = 0
        for (src, n) in srcs:
            P.dma("sp", stg[r0:r0 + n, 0:128], src, grp=("stg", gi))
            r0 += n
        pt = P.ps(gi % 2)
        P.tr(pt[:, 0:nrow], stg[0:nrow, 0:128], identf[0:nrow, 0:nrow])
        P.copy("dve", prmT[:, c0:c0 + nrow], pt[:, 0:nrow])
    PC_CDW, PC_SDW, PC_GPRE, PC_GPOST, PC_BADA, PC_CDWB, PC_LNG, PC_LNB, PC_SDWB, PC_NG = (
        0, 248, 376, 384, 392, 416, 424, 432, 440, 472)

    c_sb = P.at(R0 + 1, 2, F32)
    csl = P.at(R0 + 3, 2, F32)
    scT = P.at(R0 + 5, 1, F32)
    scT3 = scT.v(scT.ap[:, 0:256].rearrange("p (a b) -> p a b", a=8))
    P.dma("sp", c_sb[0:17, :], call)
    P.act(csl[0:17, :], c_sb[0:17, :], AF.Silu)
    pt = P.ps(2)
    for kc in range(8):
        P.tr(pt[:, kc * 32:kc * 32 + 17], csl[0:17, kc * 128:(kc + 1) * 128], identf[0:17, 0:17])
        P.copy("dve", scT[:, kc * 32:kc * 32 + 17], pt[:, kc * 32:kc * 32 + 17])
    wada_v = w_ada.rearrange("(kc kp) n -> kp kc n", kp=128)
    pm = P.ps(3)
    for c6 in range(6):
        wa = P.at(R0 + 24 + 8 * (c6 % 2), 8, F32, "p (a b) -> p a b", a=8)
        P.dma("sp", wa.all(), wada_v[:, :, c6 * 512:(c6 + 1) * 512])
        for q in range(4):
            dc = c6 * 4 + q
            for kc in range(8):
                P.mm(pm[:, dc * 17:(dc + 1) * 17], wa[:, kc, q * 128:(q + 1) * 128],
                     View(scT3.ap[:, kc, 0:17], scT.keys), start=(kc == 0), stop=(kc == 7))
    pm3 = pm.v(pm.ap[:, 0:408].rearrange("p (a b) -> p a b", a=24))
    P.tt("dve", modT.all(), pm3, prmT.v(prmT.ap[:, PC_BADA:PC_BADA + 24].unsqueeze(2).broadcast_to([128, 24, 17])),
         ALU.add)
    P.stt("dve", G_all.all(), modT[:, 8:16, :], 1.0,
          prmT.v(prmT.ap[:, PC_GPRE:PC_GPRE + 8].unsqueeze(2).broadcast_to([128, 8, 17])), ALU.add, ALU.mult)
    ggrep = P.at(R0 + 13, 8, F32, "p (a b) -> p a b", a=8)
    P.tt("dve", ggrep[:, :, 0:128], modT.v(modT.ap[:, 16:24, 0:1].broadcast_to([128, 8, 128])),
         prmT.v(prmT.ap[:, PC_GPOST:PC_GPOST + 8].unsqueeze(2).broadcast_to([128, 8, 128])), ALU.mult)
    for half in range(2):
        pg = P.ps(4 + half)
        for q in range(4):
            dc = half * 4 + q
            P.tr(pg[:, q * 128:(q + 1) * 128], ggrep[:, dc, 0:128], identf.all())
        P.copy("act", GGp[:, half * 512:(half + 1) * 512], pg.all())

    dump("d_prmT", prmT.all())
    dump("d_modT", modT.all())
    dump("d_GGp", GGp.all())
    if STOP <= 0:
        P.emit()
        return nc, P, stack
    def A(off, n, dtype, pattern=None, **kw):
        return P.at(R0 + off, n, dtype, pattern, **kw)

    RB = [A(6 * i, 6, BF16) for i in range(4)]
    S0 = 24
    yn = A(S0 + 0, 8, BF16, "p (a b) -> p a b", a=16)
    xpad = [A(S0 + 8 + 3 * i, 3, BF16) for i in range(2)]
    xact = [A(S0 + 14 + 2 * i, 2, BF16, "p (a b) -> p a b", a=4) for i in range(2)]
    zs = [A(S0 + 18 + i, 1, BF16, "p (a b) -> p a b", a=2) for i in range(2)]
    Dg = [A(S0 + 20 + 2 * i, 2, BF16, "p (a b c) -> p a b c", a=4, b=4) for i in range(2)]
    xdt = A(S0 + 24, 1, BF16, "p (a b) -> p a b", a=4)
    xdtd = A(S0 + 25, 1, BF16, "p (a b) -> p a b", a=4)
    Btok = A(S0 + 26, 1, BF16)
    cbm = A(S0 + 27, 1, F32, "p (a b) -> p a b", a=4)
    R1 = [A(S0 + 28 + i, 1, F32, "p (a b) -> p a b", a=4) for i in range(2)]
    Et = [A(S0 + 30 + i, 1, F32, "p (a b) -> p a b", a=4) for i in range(2)]
    MT = [A(S0 + 32 + i, 1, BF16) for i in range(2)]
    ytmp = A(S0 + 34, 1, F32)
    ytok = A(S0 + 35, 1, BF16, "p (a b) -> p a b", a=4)
    HTb = A(S0 + 36, 1, BF16)
    t1 = A(S0 + 37, 2, F32, "p (a b) -> p a b", a=2)
    yg = A(S0 + 39, 2, F32, "p (a b) -> p a b", a=2)
    sq = A(S0 + 41, 1, BF16, "p (a b) -> p a b", a=2)
    rstd = A(S0 + 42, 1, F32)
    sgm = A(S0 + 43, 1, F32)
    xdt2 = A(S0 + 44, 1, BF16, "p (a b) -> p a b", a=4)
    cbm2 = A(S0 + 45, 1, F32, "p (a b) -> p a b", a=4)
    ytok2 = A(S0 + 46, 1, BF16, "p (a b) -> p a b", a=4)
    xdtd2 = A(S0 + 47, 1, BF16, "p (a b) -> p a b", a=4)
    upad = [A(S0 + i, 1, BF16) for i in range(2)]
    sgl = A(S0 + 2, 1, F32)
    ycv = A(S0 + 3, 8, F32, "p (a b) -> p a b", a=8)
    sgt = A(S0 + 11, 4, BF16, "p (a b) -> p a b", a=8)
    Dc = [A(S0 + 15 + 4 * i, 4, BF16) for i in range(2)]
    ysq = A(S0 + 23, 1, BF16)
    mean_t = A(S0 + 24, 1, F32)
    var_t = A(S0 + 25, 1, F32)
    vt = A(S0 + 26, 4, BF16, "p (a b) -> p a b", a=8)
    sgc = A(S0 + 30, 1, F32)
    merged = A(S0 + 31, 4, BF16, "p (a b) -> p a b", a=8)
    xt = [A(S0 + 35 + 2 * i, 2, F32) for i in range(2)]
    ot = [A(S0 + 39 + 2 * i, 2, F32) for i in range(2)]
    junk = A(S0 + 43, 2, BF16)
    xn = A(S0 + 45, 1, BF16)
    tk = A(S0 + 46, 2, F32)

    wso_v = w_ssm_out.rearrange("(cc cp) n -> cp cc n", cp=128)
    wco_v = w_conf_out.rearrange("(cc cp) n -> cp cc n", cp=128)
    wo_v = w_o.rearrange("(kc kp) n -> kp kc n", kp=128)

    def rbv(rb, c0, a, n):
        return View(rb.ap[:, c0:c0 + a * n].rearrange("p (a b) -> p a b", a=a), rb.keys)

    NJ_SB = 34
    scr = nc.dram_tensor("scr", [NJ_SB, 128, 6144], BF16, kind="Internal").ap()

    def job_wg(g):
        def f(rb):
            w = rbv(rb, 0, 8, 768)
            P.dma("pool", w[:, :, 0:256], win_v[:, :, OFF_XS + 256 * g:OFF_XS + 256 * g + 256])
            P.dma("pool", w[:, :, 256:384], win_v[:, :, OFF_B + 128 * g:OFF_B + 128 * g + 128])
            P.dma("pool", w[:, :, 384:512], win_v[:, :, OFF_C + 128 * g:OFF_C + 128 * g + 128])
            P.dma("pool", w[:, :, 512:768], win_v[:, :, OFF_Z + 256 * g:OFF_Z + 256 * g + 256])
        return f, 6144

    def job_s3(dc):
        def f(rb):
            P.dma("pool", rbv(rb, 0, 16, 128), wso_v[:, :, dc * 128:(dc + 1) * 128])
            P.dma("pool", rbv(rb, 2048, 8, 128), win_v[:, :, OFF_MS + dc * 128:OFF_MS + (dc + 1) * 128])
        return f, 3072

    def job_wc(cc):
        def f(rb):
            w = rbv(rb, 0, 8, 384)
            for q, off in enumerate((OFF_VAL, OFF_GLU, OFF_GATE)):
                P.dma("pool", w[:, :, q * 128:(q + 1) * 128], win_v[:, :, off + cc * 128:off + (cc + 1) * 128])
        return f, 3072

    def job_s5(dc):
        def f(rb):
            P.dma("pool", rbv(rb, 0, 8, 128), wco_v[:, :, dc * 128:(dc + 1) * 128])
            P.dma("pool", rbv(rb, 1024, 8, 128), win_v[:, :, OFF_MC + dc * 128:OFF_MC + (dc + 1) * 128])
        return f, 2048

    def job_wo(half):
        def f(rb):
            P.dma("pool", rbv(rb, 0, 8, 512), wo_v[:, :, half * 512:(half + 1) * 512])
        return f, 4096

    def job_win(col0):
        def f(rb):
            P.dma("pool", rbv(rb, 0, 8, 512), win_v[:, :, col0:col0 + 512])
        return f, 4096

    def cached(jf, slot, first):
        f, ncol = jf
        sv = View(scr[slot, :, 0:ncol], [("scr", slot)])

        def g_(rb):
            if first:
                f(rb)
                P.dma("sp", sv, rb[:, 0:ncol])
            else:
                P.dma("sp", rb[:, 0:ncol], sv)
        return g_

    jobs = []
    jid = {}
    for sb_ in range(NSB):
        sl = 0
        for kind, n_, mk in (("wg", 8, job_wg), ("s3", 8, job_s3), ("wc", 8, job_wc), ("s5", 8, job_s5), ("wo", 2, job_wo)):
            for i_ in range(n_):
                jid[(sb_, kind, i_)] = len(jobs)
                jobs.append(cached(mk(i_), sl, sb_ == 0))
                sl += 1
    job_issued = [0]
    PF = 3

    def w_get(k, hold=None):
        lim = k + 1 + PF if hold is None else min(k + 1 + PF, hold + 4)
        while job_issued[0] < min(len(jobs), lim):
            j = job_issued[0]
            P.cur_grp = ("job", j)
            jobs[j](RB[j % 4])
            P.cur_grp = None
            job_issued[0] += 1
        return RB[k % 4]

    dtv = dt_t.v(dt_t.ap[:, 0:128].rearrange("p (a b) -> p a b", a=4))
    dtav = dt_t.v(dt_t.ap[:, 128:256].rearrange("p (a b) -> p a b", a=4))
    ecv = dt_t.v(dt_t.ap[:, 256:384].rearrange("p (a b) -> p a b", a=4))
    cdv = dt_t.v(dt_t.ap[:, 384:512].rearrange("p (a b) -> p a b", a=4))
    dtmp = dt2_t.v(dt2_t.ap[:, 0:128].rearrange("p (a b) -> p a b", a=4))
    cqs = dt2_t.v(dt2_t.ap[:, 128:256])
    dendv = dt2_t.v(dt2_t.ap[:, 256:384].rearrange("p (a b) -> p a b", a=4))
    dtalv = dt2_t.v(dt2_t.ap[:, 384:512].rearrange("p (a b) -> p a b", a=4))

    def bc_heads(v3, c, g, width):
        ap = v3.ap[:, c, 4 * g:4 * g + 4].unsqueeze(2).broadcast_to([128, 4, width])
        return View(ap, v3.keys)

    pp_ctr = [0]

    def next_pp():
        b = pp_ctr[0] % 2
        pp_ctr[0] += 1
        return b

    def load_x_tile(row0, buf):
        P.dma("sp", xt[buf].all(), xp[row0:row0 + 128, :])

    hT_bufs = [hT, hT2]
    xt1 = [A(S0 + 0, 2, F32), A(S0 + 2, 2, F32)]
    xn1 = A(S0 + 4, 1, BF16)
    junk1 = A(S0 + 5, 2, BF16)
    ss1 = P.sbt("ss1", [128, 4], F32)
    for sb in range(NSB):
        hT = hT_bufs[sb % 2]
        P.tag = "sb%d.st1" % sb
        for i in range(4):
            row0 = sb * TB + i * 128
            buf = i % 2
            P.dma("sp", xt1[buf].all(), xp[row0:row0 + 128, :])
            P.stt("dve", junk1[:, 0:1024], xt1[buf].all(), 1.0, xt1[buf].all(), ALU.mult, ALU.mult,
                  accum=ss1[:, 0:1])
            P.act(ss1[:, 1:2], ss1[:, 0:1], AF.Ln, bias=epsb[:, 0:1], scale=1.0 / D)
            P.act(ss1[:, 2:3], ss1[:, 1:2], AF.Exp, scale=-0.5)
            P.ts("dve", xn1.all(), xt1[buf].all(), ss1[:, 2:3], None, ALU.mult)
            pt = P.psb(2 + (i % 2))
            pt3 = pt.v(pt.ap.rearrange("p (a b) -> p a b", a=8))
            for kc in range(8):
                P.tr(View(pt3.ap[:, kc, :], pt.keys), xn1[:, kc * 128:(kc + 1) * 128], identb.all())
            for kc in range(8):
                P.act(hT[:, kc, i * 128:(i + 1) * 128], View(pt3.ap[:, kc, :], pt.keys), AF.Identity,
                      bias=modT[:, kc, 0:1], scale=G_all[:, kc, 0:1])

        if sb == 0:
            dump("d_hT", hT.all(), bf=True)
        if STOP <= 1:
            break
        P.tag = "sb%d.st2" % sb
        pdt = P.ps(3, 0)
        for i in range(4):
            for kc in range(8):
                P.mm(pdt[:, i * 32:(i + 1) * 32], hT[:, kc, i * 128:(i + 1) * 128], wdt[:, kc, :],
                     start=(kc == 0), stop=(kc == 7))
        P.tt("dve", dtmp, pdt.v(pdt.ap[:, 0:128].rearrange("p (a b) -> p a b", a=4)),
             dtb_bc.v(dtb_bc.ap.unsqueeze(1).broadcast_to([128, 4, 32])), ALU.add)
        P.act(dtmp, dtmp, AF.Exp)
        P.act(dtv, dtmp, AF.Ln, bias=1.0)
        P.tt("dve", dtav, dtv, a_bc.v(a_bc.ap.unsqueeze(1).broadcast_to([128, 4, 32])), ALU.mult)
        pcq = P.ps(4, 0)
        pct = P.ps(4, 1)
        dta2 = dt_t[:, 128:256]
        P.mm(pcq[:, 0:128], tri.all(), dta2)
        P.mm(pct[:, 0:128], onesf.all(), dta2)
        P.act(dt_t[:, 256:384], pcq[:, 0:128], AF.Exp)
        P.act(dt_t[:, 384:512], pct[:, 0:128], AF.Exp)
        P.copy("dve", cqs, pcq[:, 0:128])
        P.tt("dve", dt2_t[:, 256:384], pct[:, 0:128], cqs, ALU.subtract)
        P.act(dt2_t[:, 256:384], dt2_t[:, 256:384], AF.Exp)
        P.copy("dve", xn1[:, 0:128], dta2)
        P.copy("dve", dt2_t[:, 0:128], xn1[:, 0:128])
        P.tt("dve", dt2_t[:, 384:512], dta2, dt2_t[:, 0:128], ALU.subtract)

        if sb == 0:
            dump("d_dt", dt_t.all())
            dump("d_dt2", dt2_t.all())
        if STOP <= 2:
            break
        xh3 = xhalo.v(xhalo.ap[:, 0:96].rearrange("p (a b) -> p a b", a=32))
        CH = lambda g, j: (2 * g, 2 * g + 1, 16 + g, 24 + g)[j]

        def phaseA(g):
            w = rbv(w_get(jid[(sb, "wg", g)]), 0, 8, 768)
            xp_t = xpad[g % 2]
            xp3 = xp_t.v(xp_t.ap[:, 0:2060].rearrange("p (a b) -> p a b", a=4))
            xa = xact[g % 2]
            zz = zs[g % 2]
            dg = Dg[g % 2]
            P.copy("pool", View(xp3.ap[:, :, 0:3], xp_t.keys), View(xh3.ap[:, 4 * g:4 * g + 4, :], xhalo.keys))
            for j in range(4):
                c0 = PC_SDW + CH(g, j)
                P.tt("dve" if sb == 0 else "pool", dg[:, j, :, :], identb.v(identb.ap.unsqueeze(1).broadcast_to([128, 4, 128])),
                     prmT.v(prmT.ap[:, c0:c0 + 97:32].unsqueeze(2).broadcast_to([128, 4, 128])), ALU.mult)
            yield
            if sb == NSB - 1:
                pq = P.ps(next_pp())
                for kc in range(8):
                    P.mm(pq[0:32, :], hT[:, kc, 480:512], w[:, kc, 0:512], start=(kc == 0), stop=(kc == 7))
                tst = tailst[g % 2]
                P.copy("act", tst[0:32, :], pq[0:32, :])
                P.dma("sp", o_ssmc_p[:, 256 * g:256 * g + 256], tst[29:32, 0:256])
                P.dma("sp", o_ssmc_p[:, 2048 + 128 * g:2048 + 128 * g + 128], tst[29:32, 256:384])
                P.dma("sp", o_ssmc_p[:, 3072 + 128 * g:3072 + 128 * g + 128], tst[29:32, 384:512])
            for j in range(4):
                pj = P.ps(next_pp())
                for kc in range(8):
                    P.mm(pj.all(), w[:, kc, j * 128:(j + 1) * 128], hT[:, kc, :], start=(kc == 0), stop=(kc == 7))
                P.copy("act", View(xp3.ap[:, j, 3:515], xp_t.keys), pj.all())
                yield
            P.copy("pool", View(xh3.ap[:, 4 * g:4 * g + 4, :], xhalo.keys), View(xp3.ap[:, :, 512:515], xp_t.keys))
            for j in range(4):
                pj = P.ps(next_pp())
                for tap in range(4):
                    P.mm(pj.all(), dg[:, j, tap, :], View(xp3.ap[:, j, tap:tap + 512], xp_t.keys),
                         start=(tap == 0), stop=(tap == 3))
                col = PC_SDWB + CH(g, j)
                P.act(xa[:, j, :], pj.all(), AF.Silu, bias=prmT[:, col:col + 1])
                yield
            for j in range(2):
                pj = P.ps(next_pp())
                for kc in range(8):
                    P.mm(pj.all(), w[:, kc, 512 + j * 128:512 + (j + 1) * 128], hT[:, kc, :],
                         start=(kc == 0), stop=(kc == 7))
                P.act(zz[:, j, :], pj.all(), AF.Silu)
                yield

        aux = "dve" if sb == 0 else "pool"
        xdt_b = [xdt, xdt2]
        xdtd_b = [xdtd, xdtd2]
        cbm_b = [cbm, cbm2]
        ytok_b = [ytok, ytok2]

        def pre(g):
            xa = xact[g % 2]
            xdt_ = xdt_b[g % 2]
            xdtd_ = xdtd_b[g % 2]
            cbm_ = cbm_b[g % 2]
            bo = (g % 2) * 512
            ptx = P.psb(2)
            ptx3 = ptx.v(ptx.ap.rearrange("p (a b) -> p a b", a=4))
            for c in range(4):
                for j in range(2):
                    P.tr(View(ptx3.ap[:, c, j * 128:(j + 1) * 128], ptx.keys), xa[:, j, c * 128:(c + 1) * 128],
                         identb.all())
            dt_b = View(dtv.ap[:, :, 4 * g:4 * g + 4].unsqueeze(3).broadcast_to([128, 4, 4, 64]), dt_t.keys)
            P.tt("dve", xdt_.v(xdt_.ap.rearrange("p a (h q) -> p a h q", h=4)),
                 ptx.v(ptx.ap.rearrange("p (a h q) -> p a h q", a=4, h=4)), dt_b, ALU.mult)
            yield
            ptb = P.psb(4)
            for c in range(4):
                P.tr(ptb[:, c * 128:(c + 1) * 128], xa[:, 2, c * 128:(c + 1) * 128], identb.all())
            P.copy("act", Btok[:, bo:bo + 512], ptb[:, 0:512])
            pcb = P.ps(4)
            for c in range(4):
                P.mm(pcb[:, c * 128:(c + 1) * 128], xa[:, 2, c * 128:(c + 1) * 128], xa[:, 3, c * 128:(c + 1) * 128])
            P.tt("dve", cbm_.all(), pcb.v(pcb.ap.rearrange("p (a b) -> p a b", a=4)),
                 tri.v(tri.ap.unsqueeze(1).broadcast_to([128, 4, 128])), ALU.mult)
            dend_b = View(dendv.ap[:, :, 4 * g:4 * g + 4].unsqueeze(3).broadcast_to([128, 4, 4, 64]), dt2_t.keys)
            P.tt(aux, xdtd_.v(xdtd_.ap.rearrange("p a (h q) -> p a h q", h=4)),
                 xdt_.v(xdt_.ap.rearrange("p a (h q) -> p a h q", h=4)), dend_b, ALU.mult)
            yield

        def chunks(g):
            xa = xact[g % 2]
            xdt_ = xdt_b[g % 2]
            xdtd_ = xdtd_b[g % 2]
            cbm_ = cbm_b[g % 2]
            ytok_ = ytok_b[g % 2]
            bo = (g % 2) * 512
            P.copy("act", HTb[:, 0:256], HT[:, g, :])

            def front(c):
                r1 = R1[c % 2]
                et = Et[c % 2]
                mt = MT[c % 2]
                mt3 = mt.v(mt.ap[:, 0:512].rearrange("p (a b) -> p a b", a=4))
                r1b = r1.ap.rearrange("p a b -> p (a b)").bitcast(BF16)
                r1h = View(r1b[:, 0:512].rearrange("p (a b) -> p a b", a=4), r1.keys)
                r1l = View(r1b[:, 512:1024].rearrange("p (a b) -> p a b", a=4), r1.keys)
                P.tt("dve", r1h, tri.v(tri.ap.unsqueeze(1).broadcast_to([128, 4, 128])),
                     View(dtav.ap[:, c, 4 * g:4 * g + 4].unsqueeze(2).broadcast_to([128, 4, 128]), dt_t.keys),
                     ALU.mult)
                P.tt(aux, r1l, tri.v(tri.ap.unsqueeze(1).broadcast_to([128, 4, 128])),
                     View(dtalv.ap[:, c, 4 * g:4 * g + 4].unsqueeze(2).broadcast_to([128, 4, 128]), dt2_t.keys),
                     ALU.mult)
                pseg = P.ps(5 + (c % 2))
                P.mm(pseg.all(), sutb.all(), View(r1b[:, 0:512], r1.keys), start=True, stop=False)
                P.mm(pseg.all(), sutb.all(), View(r1b[:, 512:1024], r1.keys), start=False, stop=True)
                P.act(et.v(et.ap.rearrange("p a b -> p (a b)")), pseg.all(), AF.Exp)
                P.tt("dve", mt3, et.all(), View(cbm_.ap[:, c:c + 1, :].broadcast_to([128, 4, 128]), cbm_.keys), ALU.mult)

            def back(c):
                mt = MT[c % 2]
                mt3 = mt.v(mt.ap[:, 0:512].rearrange("p (a b) -> p a b", a=4))
                py1 = P.ps(7, 0)
                py2 = P.ps(7, 1)
                for h in range(4):
                    P.mm(py1[:, h * 64:(h + 1) * 64], View(mt3.ap[:, h, :], mt.keys), xdt_[:, c, h * 64:(h + 1) * 64])
                P.mm(py2.all(), xa[:, 3, c * 128:(c + 1) * 128], HTb[:, 0:256])
                P.tt("dve", ytmp.v(ytmp.ap[:, 0:256].rearrange("p (h q) -> p h q", h=4)),
                     py2.v(py2.ap.rearrange("p (h q) -> p h q", h=4)), bc_heads(ecv, c, g, 64), ALU.mult)
                P.tt("dve", ytok_[:, c, :], ytmp[:, 0:256], py1.all(), ALU.add)
                pst = P.ps(3, 0)
                P.mm(pst.all(), Btok[:, bo + c * 128:bo + (c + 1) * 128], xdtd_[:, c, :])
                hg = View(HT.ap[:, g, :].rearrange("p (h q) -> p h q", h=4), HT.keys)
                P.tt("dve", hg, hg, bc_heads(cdv, c, g, 64), ALU.mult)
                P.tt("dve", HT[:, g, :], HT[:, g, :], pst.all(), ALU.add)
                if c < 3:
                    P.copy("act", HTb[:, 0:256], HT[:, g, :])

            front(0)
            for c in range(4):
                if c + 1 < 4:
                    front(c + 1)
                yield
                back(c)
                yield
            if sb == 0 and g == 0:
                dump("d_xact", xa.all(), bf=True)
                dump("d_ytok", ytok_.all(), bf=True)

        def epi(g):
            xa = xact[g % 2]
            zz = zs[g % 2]
            ytok_ = ytok_b[g % 2]
            pyt = P.psb(next_pp())
            pyt3 = pyt.v(pyt.ap.rearrange("p (a b) -> p a b", a=2))
            for j in range(2):
                for c in range(4):
                    P.tr(View(pyt3.ap[:, j, c * 128:(c + 1) * 128], pyt.keys), ytok_[:, c, j * 128:(j + 1) * 128],
                         identb.all())
            for j in range(2):
                P.stt("dve", t1[:, j, :], xa[:, j, :], D_fm[:, 2 * g + j:2 * g + j + 1],
                      View(pyt3.ap[:, j, :], pyt.keys), ALU.mult, ALU.add)
            yield
            for j in range(2):
                P.tt(aux, yg[:, j, :], t1[:, j, :], zz[:, j, :], ALU.mult)
                P.act(sq[:, j, :], yg[:, j, :], AF.Square)
            yield
            pms = P.ps(next_pp())
            for j in range(2):
                P.mm(pms.all(), onesb.all(), sq[:, j, :], start=(j == 0), stop=(j == 1))
            P.act(rstd.all(), pms.all(), AF.Ln, bias=epsb[:, 0:1], scale=1.0 / 256.0)
            yield
            P.act(rstd.all(), rstd.all(), AF.Exp, scale=-0.5)
            for j in range(2):
                col = PC_NG + 2 * g + j
                P.stt("dve", yn[:, 2 * g + j, :], yg[:, j, :], prmT[:, col:col + 1], rstd.all(), ALU.mult, ALU.mult)
            yield

        def chain(*gens):
            for g_ in gens:
                yield from g_

        def interleave(*gens):
            live = [g_ for g_ in gens if g_ is not None]
            while live:
                for g_ in list(live):
                    try:
                        next(g_)
                    except StopIteration:
                        live.remove(g_)

        interleave(chain(phaseA(0), pre(0)))
        for g in range(8):
            interleave(chunks(g),
                       chain(phaseA(g + 1), pre(g + 1)) if g + 1 < 8 else None,
                       epi(g - 1) if g >= 1 else None)
        interleave(epi(7))

        if sb == 0:
            dump("d_yn", yn.all(), bf=True)
        if STOP <= 3:
            break
        P.tag = "sb%d.st3" % sb
        for dc in range(8):
            rb = w_get(jid[(sb, "s3", dc)])
            wso_t = rbv(rb, 0, 16, 128)
            wm_t = rbv(rb, 2048, 8, 128)
            pa = P.ps(next_pp())
            for cc in range(16):
                P.mm(pa.all(), wso_t[:, cc, :], yn[:, cc, :], start=(cc == 0), stop=(cc == 15))
            pb = P.ps(2 + (dc % 2))
            for kc in range(8):
                P.mm(pb.all(), wm_t[:, kc, :], hT[:, kc, :], start=(kc == 0), stop=(kc == 7))
            P.sigmoid(sgm.all(), pb.all())
            P.tt("dve", gbs[:, dc, :], pa.all(), sgm.all(), ALU.mult)

        if sb == 0:
            dump("d_gbs", gbs.all(), bf=True)
        if STOP <= 4:
            break
        P.tag = "sb%d.st4" % sb
        uh3 = uhalo.v(uhalo.ap[:, 0:240].rearrange("p (a b) -> p a b", a=8))

        def conv_proj(cc):
            w = rbv(w_get(jid[(sb, "wc", cc)]), 0, 8, 384)
            up = upad[cc % 2]
            dcw = Dc[cc % 2]
            dcw3 = dcw.v(dcw.ap[:, 0:3968].rearrange("p (a b) -> p a b", a=31))
            P.copy("pool", up[:, 0:30], View(uh3.ap[:, cc, :], uhalo.keys))
            c0 = PC_CDW + cc
            P.tt("dve", dcw3, identb.v(identb.ap.unsqueeze(1).broadcast_to([128, 31, 128])),
                 prmT.v(prmT.ap[:, c0:c0 + 241:8].unsqueeze(2).broadcast_to([128, 31, 128])), ALU.mult)
            banks = (0, 1, 2) if cc % 2 == 0 else (5, 6, 7)
            pv, pgl, pgt = (P.ps(bk) for bk in banks)
            for q, pq in enumerate((pv, pgl, pgt)):
                for kc in range(8):
                    P.mm(pq.all(), w[:, kc, q * 128:(q + 1) * 128], hT[:, kc, :], start=(kc == 0), stop=(kc == 7))
            P.sigmoid(sgl.all(), pgl.all())
            P.act(sgt[:, cc, :], pgt.all(), AF.Silu)
            P.tt("dve", up[:, 30:542], pv.all(), sgl.all(), ALU.mult)
            P.copy("pool", View(uh3.ap[:, cc, :], uhalo.keys), up[:, 512:542])
            if sb == NSB - 1:
                pq = P.ps(3 + (cc % 2))
                for kc in range(8):
                    P.mm(pq[0:32, 0:256], hT[:, kc, 480:512], w[:, kc, 0:256], start=(kc == 0), stop=(kc == 7))
                tst = tailst[cc % 2]
                P.act(tst[0:32, 0:128], pq[0:32, 128:256], AF.Sigmoid)
                P.tt("dve", tailu[0:32, cc * 128:(cc + 1) * 128], pq[0:32, 0:128], tst[0:32, 0:128], ALU.mult)

        def conv_taps(cc):
            up = upad[cc % 2]
            dcw = Dc[cc % 2]
            dcw3 = dcw.v(dcw.ap[:, 0:3968].rearrange("p (a b) -> p a b", a=31))
            pcv = P.ps(3 + (cc % 2))
            for tap in range(31):
                P.mm(pcv.all(), View(dcw3.ap[:, tap, :], dcw.keys), up[:, tap:tap + 512],
                     start=(tap == 0), stop=(tap == 30))
            col = PC_CDWB + cc
            P.act(ycv[:, cc, :], pcv.all(), AF.Identity, bias=prmT[:, col:col + 1])

        conv_proj(0)
        for cc in range(8):
            if cc + 1 < 8:
                conv_proj(cc + 1)
            conv_taps(cc)
        if sb == 0:
            dump("d_ycv", ycv.all())
        pmean = P.ps(5)
        pmsq = P.ps(6)
        for cc in range(8):
            P.copy("act", ysq[:, 0:512], ycv[:, cc, :])
            P.mm(pmean.all(), onesb.all(), ysq[:, 0:512], start=(cc == 0), stop=(cc == 7))
            P.act(ysq[:, 512:1024], ycv[:, cc, :], AF.Square)
            P.mm(pmsq.all(), onesb.all(), ysq[:, 512:1024], start=(cc == 0), stop=(cc == 7))
        P.ts("dve", mean_t.all(), pmean.all(), 1.0 / D, None, ALU.mult)
        P.tt("dve", var_t.all(), mean_t.all(), mean_t.all(), ALU.mult)
        P.stt("dve", var_t.all(), pmsq.all(), 1.0 / D, var_t.all(), ALU.mult, ALU.subtract)
        P.act(var_t.all(), var_t.all(), AF.Ln, bias=epsb[:, 0:1])
        P.act(var_t.all(), var_t.all(), AF.Exp, scale=-0.5)
        for cc in range(8):
            P.tt("dve", ycv[:, cc, :], ycv[:, cc, :], mean_t.all(), ALU.subtract)
            P.tt("dve", ycv[:, cc, :], ycv[:, cc, :], var_t.all(), ALU.mult)
            P.act(ycv[:, cc, :], ycv[:, cc, :], AF.Silu, bias=prmT[:, PC_LNB + cc:PC_LNB + cc + 1],
                  scale=prmT[:, PC_LNG + cc:PC_LNG + cc + 1])
            P.tt("dve", vt[:, cc, :], ycv[:, cc, :], sgt[:, cc, :], ALU.mult)

        if sb == 0:
            dump("d_vt", vt.all(), bf=True)
        if STOP <= 5:
            break
        P.tag = "sb%d.st5" % sb
        for dc in range(8):
            rb = w_get(jid[(sb, "s5", dc)])
            wco_t = rbv(rb, 0, 8, 128)
            wmc_t = rbv(rb, 1024, 8, 128)
            pa = P.ps(next_pp())
            for cc in range(8):
                P.mm(pa.all(), wco_t[:, cc, :], vt[:, cc, :], start=(cc == 0), stop=(cc == 7))
            pb = P.ps(2 + (dc % 2))
            for kc in range(8):
                P.mm(pb.all(), wmc_t[:, kc, :], hT[:, kc, :], start=(kc == 0), stop=(kc == 7))
            P.sigmoid(sgc.all(), pb.all())
            P.tt("dve", sgc.all(), pa.all(), sgc.all(), ALU.mult)
            P.tt("dve", merged[:, dc, :], sgc.all(), gbs[:, dc, :], ALU.add)

        if sb == 0:
            dump("d_merged", merged.all(), bf=True)
        if STOP <= 6:
            break
        P.tag = "sb%d.st6" % sb
        wo_t = [rbv(w_get(jid[(sb, "wo", half)], hold=jid[(sb, "wo", 0)]), 0, 8, 512) for half in range(2)]
        for i in range(4):
            row0 = sb * TB + i * 128
            buf = i % 2
            load_x_tile(row0, buf)
            po = [P.ps(next_pp()), None]
            po[1] = P.ps(2 + (i % 2))
            for half in range(2):
                for kc in range(8):
                    P.mm(po[half].all(), merged[:, kc, i * 128:(i + 1) * 128], wo_t[half][:, kc, :],
                         start=(kc == 0), stop=(kc == 7))
            o_t = ot[buf]
            for half in range(2):
                P.copy("act", o_t[:, half * 512:(half + 1) * 512], po[half].all())
            P.stt("dve", junk[:, 1024:2048], o_t.all(), 1.0, o_t.all(), ALU.mult, ALU.mult, accum=ss[:, 4:5])
            P.act(ss[:, 6:7], ss[:, 4:5], AF.Ln, bias=epsb[:, 0:1], scale=1.0 / D)
            P.act(ss[:, 7:8], ss[:, 6:7], AF.Exp, scale=-0.5)
            P.tt("dve", o_t.all(), o_t.all(), GGp.all(), ALU.mult)
            P.stt("dve", o_t.all(), o_t.all(), ss[:, 7:8], xt[buf].all(), ALU.mult, ALU.add)
            P.dma("sp", y_prompt[row0:row0 + 128, :], o_t.all())

        P.tag = "tail"
        if sb == NSB - 1:
            P.dma("sp", o_conf_p, tailu[2:32, 0:1024])
            for g in range(8):
                pq = P.ps(g % 2)
                for j in range(2):
                    P.tr(pq[:, j * 128:(j + 1) * 128], HT[:, g, j * 128:(j + 1) * 128], identf.all())
                P.copy("act", xt[g % 2][:, 0:256], pq[:, 0:256])
                for j in range(2):
                    r0 = (2 * g + j) * 128
                    P.dma("sp", o_ssm_p[r0:r0 + 128, :], xt[g % 2][:, j * 128:(j + 1) * 128])

    if with_sample and STOP > 6:
        smp = P.sbt("smp", [128, 144], F32)
        P.dma("sp", smp.all(), c_smp[:, 0:144])
        bmk = P.sbt("bmk", [128, 1024], BF16)
        P.dma("pool", bmk.all(), c_smp[:, 144:1168])
        bdtri = smp[0:64, 0:64]
        samem = smp[0:64, 64:128]
        seqm = smp[0:64, 128:144]
        regs = {"a": [0, 20], "t": [20, 32], "b": [32, 44], "e": [44, 69], "c": [69, 92]}

        def SA(n, dtype, pattern=None, reg="c", **kw):
            lo, hi = regs[reg]
            assert lo + n <= hi, (reg, lo, n, hi)
            t_ = P.at(lo, n, dtype, pattern, **kw)
            regs[reg][0] = lo + n
            return t_
        xts = SA(2, F32, reg="a")
        xns = SA(1, BF16, reg="a")
        sm1 = SA(1, F32, reg="a")
        hTs = SA(1, BF16, reg="e")
        hTs3 = hTs.v(hTs.ap[:, 0:512].rearrange("p (a b) -> p a b", a=8))
        projS = SA(12, F32, reg="e")
        prS = projS.v(projS.ap[:, 0:5696].rearrange("p (a b) -> p a b", a=89))
        tokS_slot0 = regs["t"][0]
        tokS = SA(12, F32, reg="t")
        wts = [SA(6, BF16, reg="e") for _ in range(2)]
        rowsA = SA(8, F32, reg="a")
        rowsB_slot0 = regs["a"][0]
        rowsB = SA(8, F32, reg="a")
        upad_s = SA(1, BF16, reg="b")
        ups3 = upad_s.v(upad_s.ap[:, 0:544].rearrange("p (a b) -> p a b", a=16))
        xpad_s = SA(4, BF16, reg="c")
        xps4 = xpad_s.v(xpad_s.ap[:, 0:3584].rearrange("p (c a b) -> p c a b", c=32, a=16))
        xact_s = SA(2, BF16, "p (a b) -> p a b", reg="b", a=32)
        zs_s = SA(1, BF16, "p (a b) -> p a b", reg="b", a=16)
        ycv_s = SA(1, F32, "p (a b) -> p a b", reg="b", a=8)
        sgt_s = SA(1, BF16, "p (a b) -> p a b", reg="b", a=16)
        vt_s = SA(1, BF16, "p (a b) -> p a b", reg="b", a=16)
        yn_s = SA(1, BF16, "p (a b) -> p a b", reg="b", a=16)
        gbs_s = SA(1, F32, "p (a b) -> p a b", reg="b", a=8)
        mrg_s = SA(1, BF16, "p (a b) -> p a b", reg="b", a=16)
        dgs = SA(1, BF16, "p (a b) -> p a b", reg="b", a=8)
        Dcs = SA(4, BF16)
        Dcs3 = Dcs.v(Dcs.ap[:, 0:3968].rearrange("p (a b) -> p a b", a=31))
        GGs = SA(2, F32)
        sm2 = SA(1, F32)
        sm3 = SA(1, F32)
        sm4 = SA(1, F32)
        sb1 = SA(1, BF16)
        sb2 = SA(1, BF16)
        sb3 = SA(1, BF16)
        sb4 = SA(1, BF16)
        sb5 = SA(1, BF16)
        ots = SA(2, F32)
        dts = SA(1, F32)

        P.tag = "S1"
        P.dma("sp", xts[0:64, :], xs_in)
        P.stt("dve", wts[1][0:64, 0:1024], xts[0:64, :], 1.0, xts[0:64, :], ALU.mult, ALU.mult, accum=ss1[0:64, 0:1])
        P.act(ss1[0:64, 1:2], ss1[0:64, 0:1], AF.Ln, bias=epsb[0:64, 0:1], scale=1.0 / D)
        P.act(ss1[0:64, 2:3], ss1[0:64, 1:2], AF.Exp, scale=-0.5)
        P.ts("dve", xns[0:64, :], xts[0:64, :], ss1[0:64, 2:3], None, ALU.mult)
        pt = P.psb(2)
        for kc in range(8):
            P.tr(pt[:, kc * 64:(kc + 1) * 64], xns[0:64, kc * 128:(kc + 1) * 128], identb[0:64, 0:64])
        pt4 = pt.v(pt.ap[:, 0:512].rearrange("p (a b c) -> p a b c", a=8, b=16))
        Gs = View(G_all.ap[:, :, 1:17].unsqueeze(3).broadcast_to([128, 8, 16, 4]), G_all.keys)
        Ss = View(modT.ap[:, 0:8, 1:17].unsqueeze(3).broadcast_to([128, 8, 16, 4]), modT.keys)
        t4 = sm1.v(sm1.ap[:, 0:512].rearrange("p (a b c) -> p a b c", a=8, b=16))
        P.tt("dve", t4, pt4, Gs, ALU.mult)
        P.tt("dve", hTs.v(hTs.ap[:, 0:512].rearrange("p (a b c) -> p a b c", a=8, b=16)), t4, Ss, ALU.add)
        ggr = sm2.v(sm2.ap[:, 0:512].rearrange("p (a b c) -> p a b c", a=8, b=16))
        P.tt("dve", ggr, View(modT.ap[:, 16:24, 1:17].unsqueeze(3).broadcast_to([128, 8, 16, 4]), modT.keys),
             View(prmT.ap[:, PC_GPOST:PC_GPOST + 8].unsqueeze(2).unsqueeze(3).broadcast_to([128, 8, 16, 4]), prmT.keys),
             ALU.mult)
        for half in range(2):
            pg = P.ps(4 + half)
            for q in range(4):
                dc = half * 4 + q
                P.tr(pg[0:64, q * 128:(q + 1) * 128], sm2[:, dc * 64:(dc + 1) * 64], identf.all())
            P.copy("act", GGs[0:64, half * 512:(half + 1) * 512], pg[0:64, :])

        P.tag = "S2"
        tile_ctr = [0]

        def s2_load(slot, c_lo, c_hi):
            w = wts[tile_ctr[0] % 2]
            tile_ctr[0] += 1
            P.dma("sp", w[:, c_lo:c_hi], View(scr[slot, :, c_lo:c_hi], [("scr", slot)]))
            return w

        def fm_block(w, base, W, c0, pslot):
            wv = w.ap[:, base:base + 8 * W].rearrange("p (a b) -> p a b", a=8)
            pj = P.ps(next_pp())
            for kc in range(8):
                P.mm(pj[:, 0:64], View(wv[:, kc, c0:c0 + 128], w.keys), View(hTs3.ap[:, kc, :], hTs.keys),
                     start=(kc == 0), stop=(kc == 7))
            P.copy("act", View(prS.ap[:, pslot, :], projS.keys), pj[:, 0:64])

        def tm_block(w, base, W, c0, wid, tok_off):
            wv = w.ap[:, base:base + 8 * W].rearrange("p (a b) -> p a b", a=8)
            pj = P.ps(2 + (tile_ctr[0] % 2))
            for kc in range(8):
                P.mm(pj[0:64, 0:wid], View(hTs3.ap[:, kc, :], hTs.keys), View(wv[:, kc, c0:c0 + wid], w.keys),
                     start=(kc == 0), stop=(kc == 7))
            P.copy("act", tokS[0:64, tok_off:tok_off + wid], pj[0:64, 0:wid])

        for g in range(8):
            w = s2_load(g, 0, 6144)
            for c0, ps_ in ((0, 40 + 2 * g), (128, 41 + 2 * g), (256, 56 + g), (384, 64 + g),
                            (512, 24 + 2 * g), (640, 25 + 2 * g)):
                fm_block(w, 0, 768, c0, ps_)
            tm_block(w, 0, 768, 0, 256, 2048 + 256 * g)
            tm_block(w, 0, 768, 256, 128, 2048 + 2048 + 128 * g)
            tm_block(w, 0, 768, 384, 128, 2048 + 3072 + 128 * g)
        for dc in range(8):
            w = s2_load(8 + dc, 2048, 3072)
            fm_block(w, 2048, 128, 0, 81 + dc)
        for cc in range(8):
            w = s2_load(16 + cc, 0, 3072)
            fm_block(w, 0, 384, 0, cc)
            fm_block(w, 0, 384, 128, 8 + cc)
            fm_block(w, 0, 384, 256, 16 + cc)
            tm_block(w, 0, 384, 0, 128, cc * 128)
            tm_block(w, 0, 384, 128, 128, 1024 + cc * 128)
        for dc in range(8):
            w = s2_load(24 + dc, 1024, 2048)
            fm_block(w, 1024, 128, 0, 73 + dc)
        pdt = P.ps(3, 0)
        for kc in range(8):
            P.mm(pdt[0:64, 0:32], View(hTs3.ap[:, kc, :], hTs.keys), wdt[:, kc, :], start=(kc == 0), stop=(kc == 7))
        dt_s = dts[0:64, 0:32]
        dta_s = dts[0:64, 32:64]
        e_s = dts[0:64, 64:96]
        dend_s = dts[0:64, 96:128]
        P.tt("dve", dt_s, pdt[0:64, 0:32], dtb_bc[0:64, :], ALU.add)
        P.act(dt_s, dt_s, AF.Exp)
        P.act(dt_s, dt_s, AF.Ln, bias=1.0)
        P.tt("dve", dta_s, dt_s, a_bc[0:64, :], ALU.mult)
        pcq = P.ps(4, 0)
        pct = P.ps(5, 0)
        P.mm(pcq[0:64, 0:32], bdtri, dta_s)
        P.mm(pct[0:64, 0:32], samem, dta_s)
        P.act(e_s, pcq[0:64, 0:32], AF.Exp)
        P.copy("dve", sm3[0:64, 0:32], pcq[0:64, 0:32])
        P.tt("dve", dend_s, pct[0:64, 0:32], sm3[0:64, 0:32], ALU.subtract)
        P.act(dend_s, dend_s, AF.Exp)

        P.tag = "S3"
        P.act(sm4[0:64, :], tokS[0:64, 1024:1536], AF.Sigmoid)
        P.tt("dve", ots[0:64, 0:512], tokS[0:64, 0:512], sm4[0:64, :], ALU.mult)
        P.act(sm4[0:64, :], tokS[0:64, 1536:2048], AF.Sigmoid)
        P.tt("dve", ots[0:64, 512:1024], tokS[0:64, 512:1024], sm4[0:64, :], ALU.mult)
        for b in range(16):
            P.dma("sp", o_conf_s[b, 0:26, :], st_conf[b, 4:30, :])
            P.dma("sp", o_conf_s[b, 26:30, :], ots[4 * b:4 * b + 4, :])
            P.dma("sp", o_ssmc_s[b, :, :], tokS[4 * b + 1:4 * b + 4, 2048:6144])
        stc = st_conf.rearrange("b j d -> (b j) d")
        rA = rowsA.v(rowsA.ap.rearrange("p (a b) -> p a b", a=4))
        for rt in range(4):
            nr = 128 if rt < 3 else 96
            P.dma("sp", View(rA.ap[0:nr, rt, :], rowsA.keys), stc[rt * 128:rt * 128 + nr, :], grp="rowsA")
        for cc in range(8):
            ph = P.ps(6)
            for rt in range(4):
                nr = 128 if rt < 3 else 96
                P.tr(ph[:, rt * 128:rt * 128 + nr], View(rA.ap[0:nr, rt, cc * 128:(cc + 1) * 128], rowsA.keys),
                     identf[0:nr, 0:nr])
            P.copy("act", View(ups3.ap[:, :, 0:30], upad_s.keys),
                   ph.v(ph.ap[:, 0:480].rearrange("p (a b) -> p a b", a=16)))
            P.act(sm4[:, 0:64], View(prS.ap[:, 8 + cc, :], projS.keys), AF.Sigmoid)
            P.tt("dve", View(ups3.ap[:, :, 30:34], upad_s.keys),
                 View(prS.ap[:, cc, :].rearrange("p (a b) -> p a b", a=16), projS.keys),
                 sm4.v(sm4.ap[:, 0:64].rearrange("p (a b) -> p a b", a=16)), ALU.mult)
            P.act(sgt_s[:, cc, :], View(prS.ap[:, 16 + cc, :], projS.keys), AF.Silu)
            for tap in range(31):
                col = PC_CDW + tap * 8 + cc
                P.ts("dve", View(Dcs3.ap[:, tap, :], Dcs.keys), identb.all(), prmT[:, col:col + 1], None, ALU.mult)
            pcv = P.ps(7)
            pcv3 = pcv.v(pcv.ap[:, 0:64].rearrange("p (a b) -> p a b", a=16))
            for tap in range(31):
                P.mm(pcv3, View(Dcs3.ap[:, tap, :], Dcs.keys), View(ups3.ap[:, :, tap:tap + 4], upad_s.keys),
                     start=(tap == 0), stop=(tap == 30))
            P.act(ycv_s[:, cc, :], pcv[:, 0:64], AF.Identity, bias=prmT[:, PC_CDWB + cc:PC_CDWB + cc + 1])
        pmean = P.ps(5)
        pmsq = P.ps(6)
        for cc in range(8):
            P.copy("act", sb1[:, 0:64], ycv_s[:, cc, :])
            P.mm(pmean[:, 0:64], onesb.all(), sb1[:, 0:64], start=(cc == 0), stop=(cc == 7))
            P.act(sb1[:, 64:128], ycv_s[:, cc, :], AF.Square)
            P.mm(pmsq[:, 0:64], onesb.all(), sb1[:, 64:128], start=(cc == 0), stop=(cc == 7))
        mean_s = sm3[:, 64:128]
        var_s = sm3[:, 128:192]
        P.ts("dve", mean_s, pmean[:, 0:64], 1.0 / D, None, ALU.mult)
        P.tt("dve", var_s, mean_s, mean_s, ALU.mult)
        P.stt("dve", var_s, pmsq[:, 0:64], 1.0 / D, var_s, ALU.mult, ALU.subtract)
        P.act(var_s, var_s, AF.Ln, bias=epsb[:, 0:1])
        P.act(var_s, var_s, AF.Exp, scale=-0.5)
        for cc in range(8):
            P.tt("dve", ycv_s[:, cc, :], ycv_s[:, cc, :], mean_s, ALU.subtract)
            P.tt("dve", ycv_s[:, cc, :], ycv_s[:, cc, :], var_s, ALU.mult)
            P.act(ycv_s[:, cc, :], ycv_s[:, cc, :], AF.Silu, bias=prmT[:, PC_LNB + cc:PC_LNB + cc + 1],
                  scale=prmT[:, PC_LNG + cc:PC_LNG + cc + 1])
            P.tt("dve", vt_s[:, cc, :], ycv_s[:, cc, :], sgt_s[:, cc, :], ALU.mult)

        P.tag = "S4"
        stm = st_ssmc.rearrange("b j d -> (b j) d")
        rB = rowsB.v(rowsB.ap)
        P.dma("sp", View(rB.ap[0:48, :], rowsB.keys), stm)
        for ch in range(32):
            ph = P.ps(6)
            P.tr(ph[:, 0:48], View(rB.ap[0:48, ch * 128:(ch + 1) * 128], rowsB.keys), identf[0:48, 0:48])
            P.copy("act", View(xps4.ap[:, ch, :, 0:3], xpad_s.keys), ph.v(ph.ap[:, 0:48].rearrange("p (a b) -> p a b", a=16)))
            P.copy("dve", View(xps4.ap[:, ch, :, 3:7], xpad_s.keys),
                   View(prS.ap[:, 40 + ch, :].rearrange("p (a b) -> p a b", a=16), projS.keys))
            for tap in range(4):
                col = PC_SDW + tap * 32 + ch
                P.ts("dve", dgs[:, tap, :], identb.all(), prmT[:, col:col + 1], None, ALU.mult)
            pj = P.ps(7)
            pj3 = pj.v(pj.ap[:, 0:64].rearrange("p (a b) -> p a b", a=16))
            for tap in range(4):
                P.mm(pj3, dgs[:, tap, :], View(xps4.ap[:, ch, :, tap:tap + 4], xpad_s.keys),
                     start=(tap == 0), stop=(tap == 3))
            P.act(xact_s[:, ch, :], pj[:, 0:64], AF.Silu, bias=prmT[:, PC_SDWB + ch:PC_SDWB + ch + 1])
        for zc in range(16):
            P.act(zs_s[:, zc, :], View(prS.ap[:, 24 + zc, :], projS.keys), AF.Silu)

        P.tag = "S5"
        tokS8 = P.at(tokS_slot0, 8, F32)
        h0gs = [rowsA.v(rowsA.ap.rearrange("p (b r n) -> p b r n", b=16, r=2)),
                tokS8.v(tokS8.ap.rearrange("p (b r n) -> p b r n", b=16, r=2))]
        h0keys = [rowsA.keys, tokS8.keys]
        newhs = []
        for i in range(8):
            t_ = P.at(rowsB_slot0 + i, 1, F32)
            newhs.append(Tile(t_.ap[:, 0:256].rearrange("p (r n) -> p r n", r=2), t_.keys))
        nh_ctr = [0]

        def load_h0(g_):
            for r_ in range(2):
                src = st_v[:, 256 * g_ + 128 * r_:256 * g_ + 128 * r_ + 128, :].rearrange("b p n -> p b n")
                P.dma(("sp", "act")[r_], View(h0gs[g_ % 2].ap[:, :, r_, :], h0keys[g_ % 2]), src, grp=("h0", g_))
        h0T = wts[0].v(wts[0].ap[:, 0:4096].rearrange("p (b c) -> p b c", b=16))
        xdm = wts[1].v(wts[1].ap[:, 0:4096].rearrange("p (b c) -> p b c", b=16))
        cmask = sb1.v(sb1.ap.rearrange("p (a b) -> p a b", a=16))
        bmask = bmk.v(bmk.ap.rearrange("p (a b) -> p a b", a=16))
        xdt_s = sb2[0:64, 0:256]
        xdtd_s = sb3[0:64, 0:256]
        Btok_s = sb4[0:64, 0:128]
        MT_s = sb5.v(sb5.ap[0:64, 0:256].rearrange("p (a b) -> p a b", a=4))
        cbm_s = sm1[0:64, 0:64]
        R1_s = sm2.v(sm2.ap[0:64, 0:256].rearrange("p (a b) -> p a b", a=4))
        E_s = sm4.v(sm4.ap[0:64, 0:256].rearrange("p (a b) -> p a b", a=4))
        ytmp_s = sm3[0:64, 256:512]
        ytok_s = Dcs[0:64, 0:256]
        dtaB = ots[0:64, 0:128]
        cdcol = dts[:, 128:160]
        st_v = st_ssm.rearrange("b h p n -> b (h p) n")
        load_h0(0)
        for g in range(8):
            if g + 1 < 8:
                load_h0(g + 1)
            h0g = h0gs[g % 2]
            h0k = h0keys[g % 2]
            for b in range(16):
                ph = P.ps(next_pp())
                for r_ in range(2):
                    P.tr(ph[:, r_ * 128:(r_ + 1) * 128], View(h0g.ap[:, b, r_, :], h0k), identf.all())
                P.copy("act", View(h0T.ap[:, b, :], wts[0].keys), ph[:, 0:256])
            ptx = P.psb(2)
            for j in range(2):
                P.tr(ptx[0:64, j * 128:(j + 1) * 128], xact_s[:, 2 * g + j, :], identb.all())
            P.tt("dve", View(xdt_s.ap.rearrange("p (h q) -> p h q", h=4), sb2.keys),
                 ptx.v(ptx.ap[0:64, 0:256].rearrange("p (h q) -> p h q", h=4)),
                 View(dt_s.ap[:, 4 * g:4 * g + 4].unsqueeze(2).broadcast_to([64, 4, 64]), dts.keys), ALU.mult)
            P.tt("dve", View(xdtd_s.ap.rearrange("p (h q) -> p h q", h=4), sb3.keys),
                 View(xdt_s.ap.rearrange("p (h q) -> p h q", h=4), sb2.keys),
                 View(dend_s.ap[:, 4 * g:4 * g + 4].unsqueeze(2).broadcast_to([64, 4, 64]), dts.keys), ALU.mult)
            ptb = P.psb(3)
            P.tr(ptb[0:64, 0:128], xact_s[:, 16 + g, :], identb.all())
            P.copy("act", Btok_s, ptb[0:64, 0:128])
            pcb = P.ps(4)
            P.mm(pcb[0:64, 0:64], xact_s[:, 16 + g, :], xact_s[:, 24 + g, :])
            P.tt("dve", cbm_s, pcb[0:64, 0:64], bdtri, ALU.mult)
            P.tt("pool", R1_s, View(tri.ap[0:64, 0:64].unsqueeze(1).broadcast_to([64, 4, 64]), tri.keys),
                 View(dta_s.ap[:, 4 * g:4 * g + 4].unsqueeze(2).broadcast_to([64, 4, 64]), dts.keys), ALU.mult)
            pseg = P.ps(5)
            P.mm(pseg[0:64, 0:256], sut[0:64, 0:64], View(R1_s.ap.rearrange("p a b -> p (a b)"), sm2.keys))
            P.act(View(E_s.ap.rearrange("p a b -> p (a b)"), sm4.keys), pseg[0:64, 0:256], AF.Exp)
            P.tt("dve", MT_s, E_s, View(cbm_s.ap.unsqueeze(1).broadcast_to([64, 4, 64]), sm1.keys), ALU.mult)
            py1 = P.ps(7, 0)
            py2 = P.ps(7, 1)
            for h in range(4):
                P.mm(py1[0:64, h * 64:(h + 1) * 64], View(MT_s.ap[:, h, :], sb5.keys),
                     View(xdt_s.ap[:, h * 64:(h + 1) * 64], sb2.keys))
            P.tt("dve", cmask, View(xact_s.ap[:, 24 + g, :].unsqueeze(1).broadcast_to([128, 16, 64]), xact_s.keys),
                 bmask, ALU.mult)
            for b in range(16):
                P.mm(py2[0:64, :], View(cmask.ap[:, b, :], sb1.keys), View(h0T.ap[:, b, :], wts[0].keys),
                     start=(b == 0), stop=(b == 15))
            P.tt("dve", View(ytmp_s.ap.rearrange("p (h q) -> p h q", h=4), sm3.keys),
                 py2.v(py2.ap[0:64, :].rearrange("p (h q) -> p h q", h=4)),
                 View(e_s.ap[:, 4 * g:4 * g + 4].unsqueeze(2).broadcast_to([64, 4, 64]), dts.keys), ALU.mult)
            P.tt("dve", ytok_s, ytmp_s, py1[0:64, :], ALU.add)
            pyt = P.psb(2)
            for j in range(2):
                P.tr(pyt[:, j * 64:(j + 1) * 64], View(ytok_s.ap[:, j * 128:(j + 1) * 128], Dcs.keys), identb[0:64, 0:64])
            t1s = sm2.v(sm2.ap[:, 256:384].rearrange("p (a b) -> p a b", a=2))
            ygs = sm2.v(sm2.ap[:, 384:512].rearrange("p (a b) -> p a b", a=2))
            sqs = sb5.v(sb5.ap[:, 512:640].rearrange("p (a b) -> p a b", a=2))
            for j in range(2):
                P.stt("dve", View(t1s.ap[:, j, :], sm2.keys), xact_s[:, 2 * g + j, :], D_fm[:, 2 * g + j:2 * g + j + 1],
                      pyt[:, j * 64:(j + 1) * 64], ALU.mult, ALU.add)
                P.tt("dve", View(ygs.ap[:, j, :], sm2.keys), View(t1s.ap[:, j, :], sm2.keys), zs_s[:, 2 * g + j, :], ALU.mult)
                P.act(View(sqs.ap[:, j, :], sb5.keys), View(ygs.ap[:, j, :], sm2.keys), AF.Square)
            pms = P.ps(4)
            for j in range(2):
                P.mm(pms[:, 0:64], onesb.all(), View(sqs.ap[:, j, :], sb5.keys), start=(j == 0), stop=(j == 1))
            rs_s = sm1[:, 64:128]
            P.act(rs_s, pms[:, 0:64], AF.Ln, bias=epsb[:, 0:1], scale=1.0 / 256.0)
            P.act(rs_s, rs_s, AF.Exp, scale=-0.5)
            for j in range(2):
                col = PC_NG + 2 * g + j
                P.stt("dve", yn_s[:, 2 * g + j, :], View(ygs.ap[:, j, :], sm2.keys), prmT[:, col:col + 1], rs_s,
                      ALU.mult, ALU.mult)
            P.tt("dve", xdm.v(xdm.ap[0:64, :, :]), View(xdtd_s.ap.unsqueeze(1).broadcast_to([64, 16, 256]), sb3.keys),
                 View(seqm.ap.unsqueeze(2).broadcast_to([64, 16, 256]), smp.keys), ALU.mult)
            pcd = P.ps(6)
            for j in range(2):
                P.copy("dve", View(dtaB.ap.rearrange("p (h q) -> p h q", h=2), ots.keys),
                       View(dta_s.ap[:, 4 * g + 2 * j:4 * g + 2 * j + 2].unsqueeze(2).broadcast_to([64, 2, 64]), dts.keys))
                P.mm(pcd[:, j * 16:(j + 1) * 16], dtaB, seqm)
            P.act(cdcol, pcd[:, 0:32], AF.Exp)
            for b in range(16):
                nh = newhs[nh_ctr[0] % 8]
                nh_ctr[0] += 1
                for j in range(2):
                    pst = P.ps(next_pp())
                    P.mm(pst[:, 0:128], View(xdm.ap[0:64, b, j * 128:(j + 1) * 128], wts[1].keys), Btok_s)
                    P.stt("dve", nh[:, j, :], View(h0g.ap[:, b, j, :], h0k),
                          View(cdcol.ap[:, j * 16 + b:j * 16 + b + 1], dts.keys), pst[:, 0:128], ALU.mult, ALU.add)
                P.dma("sp", o_ssm_s[b, 256 * g:256 * g + 256, :].rearrange("(r p) n -> p r n", p=128), nh.all())

        P.tag = "S6"
        wso_v2 = w_ssm_out.rearrange("(cc cp) n -> cp cc n", cp=128)
        wco_v2 = w_conf_out.rearrange("(cc cp) n -> cp cc n", cp=128)
        s6b = [wts[0], wts[1], rowsA.v(rowsA.ap.bitcast(BF16)), rowsB.v(rowsB.ap.bitcast(BF16))]
        s6k = [wts[0].keys, wts[1].keys, rowsA.keys, rowsB.keys]
        for dc in range(8):
            wb_ = s6b[dc % 4]
            wk_ = s6k[dc % 4]
            wsoS = View(wb_.ap[:, 0:2048].rearrange("p (a b) -> p a b", a=16), wk_)
            wcoS = View(wb_.ap[:, 2048:3072].rearrange("p (a b) -> p a b", a=8), wk_)
            P.dma("sp", View(wb_.ap[:, 0:2048], wk_), View(scr[8 + dc, :, 0:2048], [("scr", 8 + dc)]))
            P.dma("sp", View(wb_.ap[:, 2048:3072], wk_), View(scr[24 + dc, :, 0:1024], [("scr", 24 + dc)]))
            pa = P.ps(next_pp())
            for cc in range(16):
                P.mm(pa[:, 0:64], View(wsoS.ap[:, cc, :], wk_), yn_s[:, cc, :], start=(cc == 0), stop=(cc == 15))
            P.act(sm1[:, 0:64], View(prS.ap[:, 81 + dc, :], projS.keys), AF.Sigmoid)
            P.tt("dve", gbs_s[:, dc, :], pa[:, 0:64], sm1[:, 0:64], ALU.mult)
            pb = P.ps(next_pp())
            for cc in range(8):
                P.mm(pb[:, 0:64], View(wcoS.ap[:, cc, :], wk_), vt_s[:, cc, :], start=(cc == 0), stop=(cc == 7))
            P.act(sm1[:, 64:128], View(prS.ap[:, 73 + dc, :], projS.keys), AF.Sigmoid)
            P.tt("dve", sm1[:, 128:192], pb[:, 0:64], sm1[:, 64:128], ALU.mult)
            P.tt("dve", mrg_s[:, dc, :], sm1[:, 128:192], gbs_s[:, dc, :], ALU.add)
        wo_v2 = w_o.rearrange("(kc kp) n -> kp kc n", kp=128)
        for half in range(2):
            P.dma("sp", wts[half][:, 0:4096], View(scr[32 + half, :, 0:4096], [("scr", 32 + half)]))
        for half in range(2):
            wov = wts[half].ap[:, 0:4096].rearrange("p (a b) -> p a b", a=8)
            po_ = P.ps(next_pp())
            for kc in range(8):
                P.mm(po_[0:64, :], mrg_s[:, kc, :], View(wov[:, kc, :], wts[half].keys), start=(kc == 0), stop=(kc == 7))
            P.copy("act", ots[0:64, half * 512:(half + 1) * 512], po_[0:64, :])
        P.stt("dve", sb1[0:64, 0:1024], ots[0:64, :], 1.0, ots[0:64, :], ALU.mult, ALU.mult, accum=ss[0:64, 4:5])
        P.act(ss[0:64, 6:7], ss[0:64, 4:5], AF.Ln, bias=epsb[0:64, 0:1], scale=1.0 / D)
        P.act(ss[0:64, 7:8], ss[0:64, 6:7], AF.Exp, scale=-0.5)
        P.stt("dve", ots[0:64, :], ots[0:64, :], ss[0:64, 7:8], GGs[0:64, :], ALU.mult, ALU.mult)
        P.tt("pool", ots[0:64, :], ots[0:64, :], xts[0:64, :], ALU.add)
        P.dma("sp", y_sample, ots[0:64, :])

    P.emit()
    return nc, P, stack


_CONSTS = None


def _consts():
    global _CONSTS
    if _CONSTS is None:
        i = np.arange(128)
        ident = np.eye(128, dtype=np.float32)
        tri = (i[:, None] <= i[None, :]).astype(np.float32)
        sut = (i[:, None] > i[None, :]).astype(np.float32)
        smp = np.zeros((128, 1168), np.float32)
        t = np.arange(64)
        same = (t[:, None] // 4 == t[None, :] // 4)
        smp[:64, 0:64] = (same & (t[:, None] <= t[None, :])).astype(np.float32)
        smp[:64, 64:128] = same.astype(np.float32)
        smp[:64, 128:144] = (t[:, None] // 4 == np.arange(16)[None, :]).astype(np.float32)
        bm = (np.arange(16)[:, None] == t[None, :] // 4).astype(np.float32).reshape(1, 1024)
        smp[:, 144:1168] = bm
        _CONSTS = dict(c_ident=ident, c_tri=tri, c_sut=sut, c_smp=smp)
    return _CONSTS


def make_in_maps(inp, ncores=NCORE, NSB=4):
    f = lambda a: np.ascontiguousarray(np.asarray(a, dtype=np.float32))
    L = NSB * TB
    shared = dict(
        w_ada=f(inp["w_ada"][0]), b_ada=f(inp["b_ada"][0]), g_pre=f(inp["g_pre"][0]), g_post=f(inp["g_post"][0]),
        w_in=f(inp["w_in"][0]), conf_dw_w=f(inp["conf_dw_w"][0]), conf_dw_b=f(inp["conf_dw_b"][0]),
        conf_ln_g=f(inp["conf_ln_g"][0]), conf_ln_b=f(inp["conf_ln_b"][0]), w_conf_out=f(inp["w_conf_out"][0]),
        ssm_dw_w=f(inp["ssm_dw_w"][0]), ssm_dw_b=f(inp["ssm_dw_b"][0]), ssm_dt_bias=f(inp["ssm_dt_bias"][0]),
        ssm_a_log=f(inp["ssm_a_log"][0]), ssm_d=f(inp["ssm_d"][0]), ssm_norm_g=f(inp["ssm_norm_g"][0]),
        w_ssm_out=f(inp["w_ssm_out"][0]), w_o=f(inp["w_o"][0]), **_consts())
    maps = []
    for c in range(ncores):
        s = slice(16 * c, 16 * c + 16)
        m = dict(shared)
        m["xp"] = f(inp["x_prompt"][c][:L])
        m["xs"] = f(inp["x_sample"][s]).reshape(64, D)
        m["call"] = f(np.concatenate([np.asarray(inp["c_prompt"])[c:c + 1], np.asarray(inp["c_sample"])[s]], axis=0))
        m["st_conf"] = f(inp["state_conf_conv"][0, s])
        m["st_ssmc"] = f(inp["state_ssm_conv"][0, s])
        m["st_ssm"] = f(inp["state_ssm"][0, s])
        maps.append(m)
    return maps


_NC_CACHE = {}


def kernel(**inputs):
    if "full" not in _NC_CACHE:
        _NC_CACHE["full"] = build_nc(4, True)
    nc = _NC_CACHE["full"][0]
    maps = make_in_maps(inputs)
    res = run_bass_kernel_spmd(nc, maps, core_ids=list(range(NCORE)))
    R = res.results
    y_prompt = np.stack([R[c]["y_prompt"] for c in range(NCORE)], axis=0)
    y_sample = np.concatenate([R[c]["y_sample"].reshape(16, 4, D) for c in range(NCORE)], axis=0)
    conf_p = np.stack([R[c]["o_conf_p"] for c in range(NCORE)], axis=0)[None]
    ssmc_p = np.stack([R[c]["o_ssmc_p"] for c in range(NCORE)], axis=0)[None]
    ssm_p = np.stack([R[c]["o_ssm_p"].reshape(32, 64, 128) for c in range(NCORE)], axis=0)[None]
    conf_s = np.concatenate([R[c]["o_conf_s"] for c in range(NCORE)], axis=0)[None]
    ssmc_s = np.concatenate([R[c]["o_ssmc_s"] for c in range(NCORE)], axis=0)[None]
    ssm_s = np.concatenate([R[c]["o_ssm_s"].reshape(16, 32, 64, 128) for c in range(NCORE)], axis=0)[None]
    return (y_prompt, y_sample, conf_p, ssmc_p, ssm_p, conf_s, ssmc_s, ssm_s)
```

```python
import contextlib
import numpy as np
import concourse.bass as bass
import concourse.mybir as mybir
from concourse.bass_utils import run_bass_kernel_spmd

F32 = mybir.dt.float32
BF16 = mybir.dt.bfloat16
AF = mybir.ActivationFunctionType
ALU = mybir.AluOpType

D = 1024
NCORE = 8
SEQ = 2048
TB = 512
IN_DIM = 11296
OFF_VAL, OFF_GLU, OFF_GATE, OFF_Z, OFF_XS, OFF_B, OFF_C, OFF_DT, OFF_MC, OFF_MS = (
    0, 1024, 2048, 3072, 5120, 7168, 8192, 9216, 9248, 10272)
EPS = 1e-6
NDMA_SEM = 32
NSW_SEM = 16


class View:
    def __init__(self, ap, keys):
        self.ap = ap
        self.keys = keys

    def v(self, ap):
        return View(ap, self.keys)

    def __getitem__(self, idx):
        return View(self.ap[idx], self.keys)


class Tile:
    def __init__(self, ap, keys):
        self.ap = ap
        self.keys = list(keys)

    def __getitem__(self, idx):
        return View(self.ap[idx], self.keys)

    def v(self, ap):
        return View(ap, self.keys)

    def all(self):
        return View(self.ap, self.keys)


class Op:
    __slots__ = ("idx", "eng", "fn", "dma", "raw", "war", "needed", "sig", "prev_dma", "cost", "xfer", "tbl", "tag",
                 "t0", "t1")

    def __init__(self, idx, eng, fn, dma):
        self.idx, self.eng, self.fn, self.dma = idx, eng, fn, dma
        self.cost = 0.1
        self.xfer = 0.0
        self.tbl = None
        self.raw, self.war = set(), set()
        self.needed = False
        self.sig = None
        self.prev_dma = None


def _ap(x):
    return x.ap if isinstance(x, View) else x


def _fsz(x):
    sh = _ap(x).shape
    n = 1
    for d in sh[1:]:
        n *= d
    return n


def _nbytes(x):
    a = _ap(x)
    n = 1
    for d in a.shape:
        n *= d
    return n * (4 if a.dtype == F32 else 2)


_TBL = {}


class Prog:
    def __init__(self, nc, stack):
        self.nc = nc
        self.stack = stack
        self.ops = []
        self.state = {}
        self.pstate = {}
        self.dma_ops = []

    def sbt(self, name, shape, dtype):
        h = self.stack.enter_context(self.nc.sbuf_tensor(name, list(shape), dtype))
        return Tile(h[:], [(name,)])

    def make_arena(self, nslots):
        self.arena = self.stack.enter_context(self.nc.sbuf_tensor("arena", [128, nslots * 512], F32))
        self.nslots = nslots

    def at(self, slot0, nslots, dtype, pattern=None, **kw):
        assert slot0 + nslots <= self.nslots
        ap = self.arena[:, slot0 * 512:(slot0 + nslots) * 512]
        if dtype == BF16:
            ap = ap.bitcast(BF16)
        if pattern is not None:
            ap = ap.rearrange(pattern, **kw)
        return Tile(ap, [("A", s) for s in range(slot0, slot0 + nslots)])

    def make_psum(self):
        self.psum = []
        for i in range(8):
            h = self.stack.enter_context(self.nc.psum_tensor("ps%d" % i, [128, 512], F32))
            self.psum.append(h)

    def ps(self, bank, half=None):
        ap = self.psum[bank][:]
        if half is None:
            return Tile(ap, [("ps", bank)])
        return Tile(ap[:, half * 256:(half + 1) * 256], [("ps", bank)])

    def psb(self, bank, half=None):
        ap = self.psum[bank][:].bitcast(BF16)
        if half is None:
            return Tile(ap, [("ps", bank)])
        return Tile(ap[:, half * 512:(half + 1) * 512], [("ps", bank)])

    def op(self, eng, fn, r=(), w=(), dma=False, grp=None):
        o = Op(len(self.ops), eng, fn, dma)
        o.tag = getattr(self, "tag", "")
        rk, wk = [], []
        for x in r:
            if isinstance(x, (View, Tile)):
                rk.extend(x.keys)
        for x in w:
            if isinstance(x, (View, Tile)):
                wk.extend(x.keys)
        prk = [k for k in rk if k[0] == "ps"]
        pwk = [k for k in wk if k[0] == "ps"]
        rk = [k for k in rk if k[0] != "ps"]
        wk = [k for k in wk if k[0] != "ps"]
        for k in prk:
            st = self.pstate.get(k)
            if st is not None:
                (o.raw if st[1] else o.war).add(st[0])
        for k in pwk:
            st = self.pstate.get(k)
            if st is not None:
                (o.raw if st[1] else o.war).add(st[0])
        for k in prk:
            if k not in pwk:
                self.pstate[k] = [o.idx, False]
        for k in pwk:
            self.pstate[k] = [o.idx, True]
        for k in rk:
            st = self.state.get(k)
            if st is not None:
                o.raw.update(st[0])
        for k in wk:
            st = self.state.get(k)
            if st is not None:
                if grp is not None and st[2] == grp:
                    o.raw.update(st[3])
                    o.war.update(st[4])
                    o.war.update(st[1])
                else:
                    o.raw.update(st[0])
                    o.war.update(st[1])
        for k in rk:
            self.state.setdefault(k, [[], [], None, [], []])[1].append(o.idx)
        for k in wk:
            st = self.state.get(k)
            if st is not None and grp is not None and st[2] == grp:
                st[0].append(o.idx)
            else:
                pw = st[0] if st is not None else []
                pr = st[1] if st is not None else []
                self.state[k] = [[o.idx], [], grp, pw, pr]
        o.raw.discard(o.idx)
        o.war.discard(o.idx)
        self.ops.append(o)
        if dma:
            n = len(self.dma_ops)
            if n >= NDMA_SEM:
                o.prev_dma = self.dma_ops[n - NDMA_SEM]
            self.dma_ops.append(o.idx)
        return o

    def mm(self, out, lhsT, rhs, start=True, stop=True):
        o = self.op("pe", lambda e: e.matmul(_ap(out), lhsT=_ap(lhsT), rhs=_ap(rhs), start=start, stop=stop),
                    r=[lhsT, rhs], w=[out])
        o.cost = max(_fsz(out), 64) / 2050.0 * (4.0 if _ap(lhsT).dtype == F32 else 1.0) + 0.01

    def tr(self, out, in_, ident):
        o = self.op("pe", lambda e: e.transpose(_ap(out), _ap(in_), _ap(ident)), r=[in_, ident], w=[out])
        o.cost = max(_fsz(out), 64) / 2050.0 * (4.0 if _ap(in_).dtype == F32 else 1.0) + 0.01

    def act(self, out, in_, func, bias=None, scale=None):
        kw = {}
        if bias is not None:
            kw["bias"] = _ap(bias)
        if scale is not None:
            kw["scale"] = _ap(scale)
        o = self.op("act", lambda e: e.activation(out=_ap(out), in_=_ap(in_), func=func, **kw),
                    r=[in_, bias, scale], w=[out])
        o.cost = 0.15 + _fsz(out) / 1250.0
        o.tbl = {AF.Exp: "e", AF.Ln: "e", AF.Silu: "s", AF.Sigmoid: "g", AF.Sqrt: "q"}.get(func)

    def _vcost(self, o, eng, out, k=1.0):
        n = _fsz(out)
        o.cost = (0.12 + n / 960.0 * k) if eng == "dve" else (0.15 + n / 500.0)

    def tt(self, eng, out, in0, in1, op):
        o = self.op(eng, lambda e: e.tensor_tensor(out=_ap(out), in0=_ap(in0), in1=_ap(in1), op=op),
                    r=[in0, in1], w=[out])
        self._vcost(o, eng, out)

    def ts(self, eng, out, in0, s1, s2, op0, op1=None):
        if op1 is None:
            o = self.op(eng, lambda e: e.tensor_scalar(out=_ap(out), in0=_ap(in0), scalar1=_ap(s1), scalar2=None,
                                                       op0=op0), r=[in0, s1], w=[out])
        else:
            o = self.op(eng, lambda e: e.tensor_scalar(out=_ap(out), in0=_ap(in0), scalar1=_ap(s1),
                                                       scalar2=_ap(s2), op0=op0, op1=op1),
                        r=[in0, s1, s2], w=[out])
        self._vcost(o, eng, out, 0.5)

    def stt(self, eng, out, in0, scalar, in1, op0, op1, accum=None):
        if accum is None:
            o = self.op(eng, lambda e: e.scalar_tensor_tensor(out=_ap(out), in0=_ap(in0), scalar=_ap(scalar),
                                                              in1=_ap(in1), op0=op0, op1=op1),
                        r=[in0, scalar, in1], w=[out])
        else:
            o = self.op(eng, lambda e: e.scalar_tensor_tensor(out=_ap(out), in0=_ap(in0), scalar=_ap(scalar),
                                                              in1=_ap(in1), op0=op0, op1=op1,
                                                              accum_out=_ap(accum)),
                        r=[in0, scalar, in1], w=[out, accum])
        self._vcost(o, eng, out)

    def copy(self, eng, out, in_):
        if eng == "act":
            o = self.op("act", lambda e: e.activation(out=_ap(out), in_=_ap(in_), func=AF.Copy), r=[in_], w=[out])
            o.cost = 0.2 + _fsz(out) / 1200.0
        else:
            o = self.op(eng, lambda e: e.tensor_copy(out=_ap(out), in_=_ap(in_)), r=[in_], w=[out])
            self._vcost(o, eng, out, 0.5)

    def recip(self, out, in_):
        o = self.op("dve", lambda e: e.reciprocal(out=_ap(out), in_=_ap(in_)), r=[in_], w=[out])
        o.cost = 0.12 + _fsz(out) * 6.3 / 960.0

    def memset(self, eng, out, val):
        self.op(eng, lambda e: e.memset(_ap(out), val), r=[], w=[out])

    def dma(self, eng, out, in_, slow=False, grp=None):
        if grp is None:
            grp = getattr(self, "cur_grp", None)
        if slow:
            o = self.op(eng, lambda e: e.dma_start(out=_ap(out), in_=_ap(in_), allow_slow_non_contiguous=True),
                        r=[in_], w=[out], dma=True, grp=grp)
        else:
            o = self.op(eng, lambda e: e.dma_start(out=_ap(out), in_=_ap(in_)), r=[in_], w=[out], dma=True, grp=grp)
        nb = _nbytes(out)
        if eng == "pool":
            o.cost = 1.05
            o.xfer = 2.0 + nb / 90e3
        else:
            o.cost = 0.5
            o.xfer = 2.0 + nb / 260e3

    def schedule(self):
        import heapq
        ops = self.ops
        n = len(ops)
        preds = [sorted(o.raw | o.war) for o in ops]
        succ = [[] for _ in range(n)]
        indeg = [0] * n
        for i, ps in enumerate(preds):
            indeg[i] = len(ps)
            for p in ps:
                succ[p].append(i)
        prio = [0.0] * n
        for i in range(n - 1, -1, -1):
            o = ops[i]
            m = 0.0
            for s_ in succ[i]:
                if prio[s_] > m:
                    m = prio[s_]
            prio[i] = m + o.cost + o.xfer
        fin = [0.0] * n
        efree = {"pe": 0.0, "act": 0.0, "dve": 0.0, "pool": 0.0, "sp": 0.0}
        swdge_free = [0.0]
        last_tbl = [None]
        ready = {e: [] for e in efree}
        dep_t = [0.0] * n
        for i in range(n):
            if indeg[i] == 0:
                ready[ops[i].eng].append(i)
        order = []
        LAT = 0.35
        while len(order) < n:
            best = None
            for e, lst in ready.items():
                if not lst:
                    continue
                ef = efree[e]
                cb = None
                for i in lst:
                    st = dep_t[i] if dep_t[i] > ef else ef
                    if e == "act" and ops[i].tbl is not None and last_tbl[0] is not None and ops[i].tbl != last_tbl[0]:
                        st += 1.3
                    key = (st, -prio[i], i)
                    if cb is None or key < cb:
                        cb = key
                if best is None or cb < best[0]:
                    best = (cb, e)
            (st, _, i), e = best
            ready[e].remove(i)
            o = ops[i]
            if e == "act" and o.tbl is not None:
                last_tbl[0] = o.tbl
            end_eng = st + o.cost
            efree[e] = end_eng
            if o.dma:
                if e == "pool":
                    t0 = max(end_eng, swdge_free[0])
                    fin[i] = t0 + o.xfer
                    swdge_free[0] = fin[i] - 2.0
                else:
                    fin[i] = end_eng + o.xfer
            else:
                fin[i] = end_eng
            order.append(i)
            o.t0, o.t1 = st, fin[i]
            for s_ in succ[i]:
                t = fin[i] + (LAT if ops[s_].eng != e or o.dma else 0.0)
                if t > dep_t[s_]:
                    dep_t[s_] = t
                indeg[s_] -= 1
                if indeg[s_] == 0:
                    ready[ops[s_].eng].append(s_)
        self.est_time = max(fin) if fin else 0.0
        return order

    def emit(self, reorder=True):
        nc = self.nc
        ops = self.ops
        if reorder:
            order = self.schedule()
        else:
            order = list(range(len(ops)))
        pos = {i: k for k, i in enumerate(order)}
        self.dma_ops = [i for i in order if ops[i].dma]
        self.dma_sw = [i for i in self.dma_ops if ops[i].eng == "pool"]
        self.dma_hw = [i for i in self.dma_ops if ops[i].eng != "pool"]
        for lst, nsem in ((self.dma_sw, NSW_SEM), (self.dma_hw, NDMA_SEM)):
            for n_, i in enumerate(lst):
                ops[i].prev_dma = lst[n_ - nsem] if n_ >= nsem else None
        for o in ops:
            deps = set()
            for d in o.raw:
                p = ops[d]
                if p.eng == o.eng and not p.dma and not o.dma and o.eng == "pe":
                    continue
                deps.add(d)
            for d in o.war:
                p = ops[d]
                if p.eng == o.eng and not p.dma and not o.dma and o.eng == "pe":
                    continue
                deps.add(d)
            if o.prev_dma is not None:
                deps.add(o.prev_dma)
            keep = set()
            latest = {}
            for d in deps:
                p = ops[d]
                if p.dma:
                    keep.add(d)
                else:
                    b_ = latest.get(p.eng)
                    if b_ is None or pos[d] > pos[b_]:
                        latest[p.eng] = d
            keep.update(latest.values())
            o.raw = keep
            for d in keep:
                ops[d].needed = True
        for i in self.dma_ops:
            ops[i].needed = True
        engs = ["pe", "act", "dve", "pool", "sp"]
        esem = {e: self.stack.enter_context(nc.semaphore("sem_" + e)) for e in engs}
        dsem = [self.stack.enter_context(nc.semaphore("dsem%d" % i)) for i in range(NDMA_SEM)]
        ssem = [self.stack.enter_context(nc.semaphore("ssem%d" % i)) for i in range(NSW_SEM)]
        cnt = {e: 0 for e in engs}
        for n, i in enumerate(self.dma_hw):
            ops[i].sig = (dsem[n % NDMA_SEM], 16 * (n // NDMA_SEM + 1), ("d", n % NDMA_SEM))
        for n, i in enumerate(self.dma_sw):
            ops[i].sig = (ssem[n % NSW_SEM], 16 * (n // NSW_SEM + 1), ("s", n % NSW_SEM))
        for i in order:
            o = ops[i]
            if o.dma:
                continue
            if o.needed:
                cnt[o.eng] += 1
                o.sig = (esem[o.eng], cnt[o.eng], ("e", o.eng))
        final = {}
        for n, i in enumerate(self.dma_ops):
            s = ops[i].sig
            final[s[2]] = (s[0], max(final.get(s[2], (None, 0))[1], s[1]))
        by_eng = {e: [ops[i] for i in order if ops[i].eng == e] for e in engs}
        self.n_instr = {e: len(by_eng[e]) for e in engs}

        def run(eng_name, e):
            waited = {}
            for o in by_eng[eng_name]:
                need = {}
                for d in o.raw:
                    s = ops[d].sig
                    if s[1] > need.get(s[2], (None, 0))[1]:
                        need[s[2]] = (s[0], s[1])
                for k, (sem, val) in need.items():
                    if waited.get(k, 0) >= val:
                        continue
                    e.wait_ge(sem, val)
                    waited[k] = val
                ins = o.fn(e)
                if o.sig is not None:
                    ins.then_inc(o.sig[0], 16 if o.dma else 1)
            if eng_name == "sp":
                for k, (sem, val) in final.items():
                    e.wait_ge(sem, val)

        with nc.Block() as block:
            @block.tensor
            def _(e):
                run("pe", e)

            @block.scalar
            def _(e):
                run("act", e)

            @block.vector
            def _(e):
                run("dve", e)

            @block.gpsimd
            def _(e):
                run("pool", e)

            @block.sync
            def _(e):
                run("sp", e)


def build_nc(NSB=4, with_sample=True, STOP=99, DBG=False):
    nc = bass.Bass("TRN2", target_bir_lowering=False)
    stack = contextlib.ExitStack()
    P = Prog(nc, stack)
    L = NSB * TB

    def din(name, shape):
        return nc.dram_tensor(name, list(shape), F32, kind="ExternalInput").ap()

    def dout(name, shape):
        return nc.dram_tensor(name, list(shape), F32, kind="ExternalOutput").ap()

    xp = din("xp", [L, D])
    xs_in = din("xs", [64, D])
    call = din("call", [17, D])
    st_conf = din("st_conf", [16, 30, D])
    st_ssmc = din("st_ssmc", [16, 3, 4096])
    st_ssm = din("st_ssm", [16, 32, 64, 128])
    w_ada = din("w_ada", [D, 3 * D])
    b_ada = din("b_ada", [3 * D])
    g_pre = din("g_pre", [D])
    g_post = din("g_post", [D])
    w_in = din("w_in", [D, IN_DIM])
    conf_dw_w = din("conf_dw_w", [31, D])
    conf_dw_b = din("conf_dw_b", [D])
    conf_ln_g = din("conf_ln_g", [D])
    conf_ln_b = din("conf_ln_b", [D])
    w_conf_out = din("w_conf_out", [D, D])
    ssm_dw_w = din("ssm_dw_w", [4, 4096])
    ssm_dw_b = din("ssm_dw_b", [4096])
    ssm_dt_bias = din("ssm_dt_bias", [32])
    ssm_a_log = din("ssm_a_log", [32])
    ssm_d = din("ssm_d", [32])
    ssm_norm_g = din("ssm_norm_g", [2048])
    w_ssm_out = din("w_ssm_out", [2048, D])
    w_o = din("w_o", [D, D])
    c_ident = din("c_ident", [128, 128])
    c_tri = din("c_tri", [128, 128])
    c_sut = din("c_sut", [128, 128])
    c_smp = din("c_smp", [128, 1168])

    y_prompt = dout("y_prompt", [L, D])
    y_sample = dout("y_sample", [64, D])
    o_conf_p = dout("o_conf_p", [30, D])
    o_ssmc_p = dout("o_ssmc_p", [3, 4096])
    o_ssm_p = dout("o_ssm_p", [32 * 64, 128])
    o_conf_s = dout("o_conf_s", [16, 30, D])
    o_ssmc_s = dout("o_ssmc_s", [16, 3, 4096])
    o_ssm_s = dout("o_ssm_s", [16, 32 * 64, 128])

    win_v = w_in.rearrange("(kc kp) n -> kp kc n", kp=128)
    dbg = {}
    if DBG:
        for nm, shp in (("d_hT", [128, 8 * 512]), ("d_dt", [128, 512]), ("d_dt2", [128, 512]), ("d_xact", [128, 4 * 512]),
                        ("d_yn", [128, 16 * 512]), ("d_gbs", [128, 8 * 512]), ("d_vt", [128, 8 * 512]),
                        ("d_merged", [128, 8 * 512]), ("d_prmT", [128, 488]), ("d_modT", [128, 24 * 17]),
                        ("d_ytok", [128, 1024]), ("d_ycv", [128, 8 * 512]), ("d_GGp", [128, 1024])):
            dbg[nm] = dout(nm, shp)
    dbf = {}

    def dump(nm, view, bf=False):
        if not DBG:
            return
        if bf:
            P.dma("pool", dbg[nm], View(view.ap if len(view.ap.shape) == 2 else view.ap.rearrange("p a b -> p (a b)"), view.keys))
        else:
            P.dma("sp", dbg[nm], View(view.ap if len(view.ap.shape) == 2 else view.ap.rearrange("p a b -> p (a b)"), view.keys))

    P.make_psum()
    identf = P.sbt("identf", [128, 128], F32)
    identb = P.sbt("identb", [128, 128], BF16)
    tri = P.sbt("tri", [128, 128], F32)
    sut = P.sbt("sut", [128, 128], F32)
    sutb = P.sbt("sutb", [128, 128], BF16)
    onesf = P.sbt("onesf", [128, 128], F32)
    onesb = P.sbt("onesb", [128, 128], BF16)
    prmT = P.sbt("prmT", [128, 488], F32)
    modT = P.sbt("modT", [128, 24, 17], F32)
    G_all = P.sbt("G_all", [128, 8, 17], F32)
    dtb_bc = P.sbt("dtb_bc", [128, 32], F32)
    a_bc = P.sbt("a_bc", [128, 32], F32)
    D_fm = P.sbt("D_fm", [128, 16], F32)
    wdt = P.sbt("wdt", [128, 8, 32], BF16)
    GGp = P.sbt("GGp", [128, D], F32)
    ss = P.sbt("ss", [128, 8], F32)
    epsb = P.sbt("epsb", [128, 1], F32)

    tailst = [P.sbt("tailst%d" % i, [32, 512], F32) for i in range(2)]
    tailu = P.sbt("tailu", [32, 1024], F32)
    NSLOT = 92
    P.make_arena(NSLOT)
    hT = P.at(0, 4, BF16, "p (a b) -> p a b", a=8)
    gbs = P.at(4, 4, BF16, "p (a b) -> p a b", a=8)
    hT2 = P.at(8, 4, BF16, "p (a b) -> p a b", a=8)
    HT = P.at(12, 4, F32, "p (a b) -> p a b", a=8)
    uhalo = P.at(16, 1, BF16)
    xhalo = P.at(17, 1, BF16)
    dt_t = P.at(18, 1, F32)
    dt2_t = P.at(19, 1, F32)
    R0 = 20

    def prm(c0, n):
        return prmT[:, c0:c0 + n]

    P.dma("sp", identf.all(), c_ident)
    P.dma("pool", identb.all(), c_ident)
    P.dma("sp", tri.all(), c_tri)
    P.dma("sp", sut.all(), c_sut)
    P.dma("pool", sutb.all(), c_sut)
    P.memset("pool", onesf.all(), 1.0)
    P.memset("pool", onesb.all(), 1.0)
    P.memset("pool", epsb.all(), EPS)
    P.memset("pool", HT.all(), 0.0)
    P.memset("pool", uhalo.all(), 0.0)
    P.memset("pool", xhalo.all(), 0.0)
    P.dma("sp", dtb_bc.all(), ssm_dt_bias.rearrange("(o n) -> o n", o=1).partition_broadcast(128)[:, 0, :])
    P.dma("sp", a_bc.all(), ssm_a_log.rearrange("(o n) -> o n", o=1).partition_broadcast(128)[:, 0, :])
    P.dma("sp", D_fm[0:64, :], bass.AP(tensor=ssm_d.tensor, offset=0, ap=[[0, 64], [2, 16]]), slow=True, grp="dfm")
    P.dma("sp", D_fm[64:128, :], bass.AP(tensor=ssm_d.tensor, offset=1, ap=[[0, 64], [2, 16]]), slow=True, grp="dfm")
    P.dma("pool", wdt.all(), win_v[:, :, OFF_DT:OFF_DT + 32])
    P.act(a_bc.all(), a_bc.all(), AF.Exp)
    P.ts("dve", a_bc.all(), a_bc.all(), -1.0, None, ALU.mult)

    stg = P.at(R0, 1, F32)
    cw = conf_dw_w.rearrange("j (c p) -> (j c) p", p=128)
    sw = ssm_dw_w.rearrange("j (c p) -> (j c) p", p=128)

    def v128(x):
        return x.rearrange("(c p) -> c p", p=128)
    groups = [
        (0, 128, [(cw[0:128, :], 128)]),
        (128, 120, [(cw[128:248, :], 120)]),
        (248, 128, [(sw, 128)]),
        (376, 112, [(v128(g_pre), 8), (v128(g_post), 8), (v128(b_ada), 24), (v128(conf_dw_b), 8),
                    (v128(conf_ln_g), 8), (v128(conf_ln_b), 8), (v128(ssm_dw_b), 32), (v128(ssm_norm_g), 16)]),
    ]
    for gi, (c0, nrow, srcs) in enumerate(groups):
        r0 = 0
        for (src, n) in srcs:
            P.dma("sp", stg[r0:r0 + n, 0:128], src, grp=("stg", gi))
            r0 += n
        pt = P.ps(gi % 2)
        P.tr(pt[:, 0:nrow], stg[0:nrow, 0:128], identf[0:nrow, 0:nrow])
        P.copy("dve", prmT[:, c0:c0 + nrow], pt[:, 0:nrow])
    PC_CDW, PC_SDW, PC_GPRE, PC_GPOST, PC_BADA, PC_CDWB, PC_LNG, PC_LNB, PC_SDWB, PC_NG = (
        0, 248, 376, 384, 392, 416, 424, 432, 440, 472)

    c_sb = P.at(R0 + 1, 2, F32)
    csl = P.at(R0 + 3, 2, F32)
    scT = P.at(R0 + 5, 1, F32)
    scT3 = scT.v(scT.ap[:, 0:256].rearrange("p (a b) -> p a b", a=8))
    P.dma("sp", c_sb[0:17, :], call)
    P.act(csl[0:17, :], c_sb[0:17, :], AF.Silu)
    pt = P.ps(2)
    for kc in range(8):
        P.tr(pt[:, kc * 32:kc * 32 + 17], csl[0:17, kc * 128:(kc + 1) * 128], identf[0:17, 0:17])
        P.copy("dve", scT[:, kc * 32:kc * 32 + 17], pt[:, kc * 32:kc * 32 + 17])
    wada_v = w_ada.rearrange("(kc kp) n -> kp kc n", kp=128)
    pm = P.ps(3)
    for c6 in range(6):
        wa = P.at(R0 + 24 + 8 * (c6 % 2), 8, F32, "p (a b) -> p a b", a=8)
        P.dma("sp", wa.all(), wada_v[:, :, c6 * 512:(c6 + 1) * 512])
        for q in range(4):
            dc = c6 * 4 + q
            for kc in range(8):
                P.mm(pm[:, dc * 17:(dc + 1) * 17], wa[:, kc, q * 128:(q + 1) * 128],
                     View(scT3.ap[:, kc, 0:17], scT.keys), start=(kc == 0), stop=(kc == 7))
    pm3 = pm.v(pm.ap[:, 0:408].rearrange("p (a b) -> p a b", a=24))
    P.tt("dve", modT.all(), pm3, prmT.v(prmT.ap[:, PC_BADA:PC_BADA + 24].unsqueeze(2).broadcast_to([128, 24, 17])),
         ALU.add)
    P.stt("dve", G_all.all(), modT[:, 8:16, :], 1.0,
          prmT.v(prmT.ap[:, PC_GPRE:PC_GPRE + 8].unsqueeze(2).broadcast_to([128, 8, 17])), ALU.add, ALU.mult)
    ggrep = P.at(R0 + 13, 8, F32, "p (a b) -> p a b", a=8)
    P.tt("dve", ggrep[:, :, 0:128], modT.v(modT.ap[:, 16:24, 0:1].broadcast_to([128, 8, 128])),
         prmT.v(prmT.ap[:, PC_GPOST:PC_GPOST + 8].unsqueeze(2).broadcast_to([128, 8, 128])), ALU.mult)
    for half in range(2):
        pg = P.ps(4 + half)
        for q in range(4):
            dc = half * 4 + q
            P.tr(pg[:, q * 128:(q + 1) * 128], ggrep[:, dc, 0:128], identf.all())
        P.copy("act", GGp[:, half * 512:(half + 1) * 512], pg.all())

    dump("d_prmT", prmT.all())
    dump("d_modT", modT.all())
    dump("d_GGp", GGp.all())
    if STOP <= 0:
        P.emit()
        return nc, P, stack
    def A(off, n, dtype, pattern=None, **kw):
        return P.at(R0 + off, n, dtype, pattern, **kw)

    RB = [A(6 * i, 6, BF16) for i in range(4)]
    S0 = 24
    yn = A(S0 + 0, 8, BF16, "p (a b) -> p a b", a=16)
    xpad = [A(S0 + 8 + 3 * i, 3, BF16) for i in range(2)]
    xact = [A(S0 + 14 + 2 * i, 2, BF16, "p (a b) -> p a b", a=4) for i in range(2)]
    zs = [A(S0 + 18 + i, 1, BF16, "p (a b) -> p a b", a=2) for i in range(2)]
    Dg = [A(S0 + 20 + 2 * i, 2, BF16, "p (a b c) -> p a b c", a=4, b=4) for i in range(2)]
    xdt = A(S0 + 24, 1, BF16, "p (a b) -> p a b", a=4)
    xdtd = A(S0 + 25, 1, BF16, "p (a b) -> p a b", a=4)
    Btok = A(S0 + 26, 1, BF16)
    cbm = A(S0 + 27, 1, F32, "p (a b) -> p a b", a=4)
    R1 = [A(S0 + 28 + i, 1, F32, "p (a b) -> p a b", a=4) for i in range(2)]
    Et = [A(S0 + 30 + i, 1, F32, "p (a b) -> p a b", a=4) for i in range(2)]
    MT = [A(S0 + 32 + i, 1, BF16) for i in range(2)]
    ytmp = A(S0 + 34, 1, F32)
    ytok = A(S0 + 35, 1, BF16, "p (a b) -> p a b", a=4)
    HTb = A(S0 + 36, 1, BF16)
    t1 = A(S0 + 37, 2, F32, "p (a b) -> p a b", a=2)
    yg = A(S0 + 39, 2, F32, "p (a b) -> p a b", a=2)
    sq = A(S0 + 41, 1, BF16, "p (a b) -> p a b", a=2)
    rstd = A(S0 + 42, 1, F32)
    sgm = A(S0 + 43, 1, F32)
    xdt2 = A(S0 + 44, 1, BF16, "p (a b) -> p a b", a=4)
    cbm2 = A(S0 + 45, 1, F32, "p (a b) -> p a b", a=4)
    ytok2 = A(S0 + 46, 1, BF16, "p (a b) -> p a b", a=4)
    xdtd2 = A(S0 + 47, 1, BF16, "p (a b) -> p a b", a=4)
    upad = [A(S0 + i, 1, BF16) for i in range(2)]
    sgl = A(S0 + 2, 1, F32)
    ycv = A(S0 + 3, 8, F32, "p (a b) -> p a b", a=8)
    sgt = A(S0 + 11, 4, BF16, "p (a b) -> p a b", a=8)
    Dc = [A(S0 + 15 + 4 * i, 4, BF16) for i in range(2)]
    ysq = A(S0 + 23, 1, BF16)
    mean_t = A(S0 + 24, 1, F32)
    var_t = A(S0 + 25, 1, F32)
    vt = A(S0 + 26, 4, BF16, "p (a b) -> p a b", a=8)
    sgc = A(S0 + 30, 1, F32)
    merged = A(S0 + 31, 4, BF16, "p (a b) -> p a b", a=8)
    xt = [A(S0 + 35 + 2 * i, 2, F32) for i in range(2)]
    ot = [A(S0 + 39 + 2 * i, 2, F32) for i in range(2)]
    junk = A(S0 + 43, 2, BF16)
    xn = A(S0 + 45, 1, BF16)
    tk = A(S0 + 46, 2, F32)

    wso_v = w_ssm_out.rearrange("(cc cp) n -> cp cc n", cp=128)
    wco_v = w_conf_out.rearrange("(cc cp) n -> cp cc n", cp=128)
    wo_v = w_o.rearrange("(kc kp) n -> kp kc n", kp=128)

    def rbv(rb, c0, a, n):
        return View(rb.ap[:, c0:c0 + a * n].rearrange("p (a b) -> p a b", a=a), rb.keys)

    NJ_SB = 34
    scr = nc.dram_tensor("scr", [NJ_SB, 128, 6144], BF16, kind="Internal").ap()

    def job_wg(g):
        def f(rb):
            w = rbv(rb, 0, 8, 768)
            P.dma("pool", w[:, :, 0:256], win_v[:, :, OFF_XS + 256 * g:OFF_XS + 256 * g + 256])
            P.dma("pool", w[:, :, 256:384], win_v[:, :, OFF_B + 128 * g:OFF_B + 128 * g + 128])
            P.dma("pool", w[:, :, 384:512], win_v[:, :, OFF_C + 128 * g:OFF_C + 128 * g + 128])
            P.dma("pool", w[:, :, 512:768], win_v[:, :, OFF_Z + 256 * g:OFF_Z + 256 * g + 256])
        return f, 6144

    def job_s3(dc):
        def f(rb):
            P.dma("pool", rbv(rb, 0, 16, 128), wso_v[:, :, dc * 128:(dc + 1) * 128])
            P.dma("pool", rbv(rb, 2048, 8, 128), win_v[:, :, OFF_MS + dc * 128:OFF_MS + (dc + 1) * 128])
        return f, 3072

    def job_wc(cc):
        def f(rb):
            w = rbv(rb, 0, 8, 384)
            for q, off in enumerate((OFF_VAL, OFF_GLU, OFF_GATE)):
                P.dma("pool", w[:, :, q * 128:(q + 1) * 128], win_v[:, :, off + cc * 128:off + (cc + 1) * 128])
        return f, 3072

    def job_s5(dc):
        def f(rb):
            P.dma("pool", rbv(rb, 0, 8, 128), wco_v[:, :, dc * 128:(dc + 1) * 128])
            P.dma("pool", rbv(rb, 1024, 8, 128), win_v[:, :, OFF_MC + dc * 128:OFF_MC + (dc + 1) * 128])
        return f, 2048

    def job_wo(half):
        def f(rb):
            P.dma("pool", rbv(rb, 0, 8, 512), wo_v[:, :, half * 512:(half + 1) * 512])
        return f, 4096

    def job_win(col0):
        def f(rb):
            P.dma("pool", rbv(rb, 0, 8, 512), win_v[:, :, col0:col0 + 512])
        return f, 4096

    def cached(jf, slot, first):
        f, ncol = jf
        sv = View(scr[slot, :, 0:ncol], [("scr", slot)])

        def g_(rb):
            if first:
                f(rb)
                P.dma("sp", sv, rb[:, 0:ncol])
            else:
                P.dma("sp", rb[:, 0:ncol], sv)
        return g_

    jobs = []
    jid = {}
    for sb_ in range(NSB):
        sl = 0
        for kind, n_, mk in (("wg", 8, job_wg), ("s3", 8, job_s3), ("wc", 8, job_wc), ("s5", 8, job_s5), ("wo", 2, job_wo)):
            for i_ in range(n_):
                jid[(sb_, kind, i_)] = len(jobs)
                jobs.append(cached(mk(i_), sl, sb_ == 0))
                sl += 1
    job_issued = [0]
    PF = 3

    def w_get(k, hold=None):
        lim = k + 1 + PF if hold is None else min(k + 1 + PF, hold + 4)
        while job_issued[0] < min(len(jobs), lim):
            j = job_issued[0]
            P.cur_grp = ("job", j)
            jobs[j](RB[j % 4])
            P.cur_grp = None
            job_issued[0] += 1
        return RB[k % 4]

    dtv = dt_t.v(dt_t.ap[:, 0:128].rearrange("p (a b) -> p a b", a=4))
    dtav = dt_t.v(dt_t.ap[:, 128:256].rearrange("p (a b) -> p a b", a=4))
    ecv = dt_t.v(dt_t.ap[:, 256:384].rearrange("p (a b) -> p a b", a=4))
    cdv = dt_t.v(dt_t.ap[:, 384:512].rearrange("p (a b) -> p a b", a=4))
    dtmp = dt2_t.v(dt2_t.ap[:, 0:128].rearrange("p (a b) -> p a b", a=4))
    cqs = dt2_t.v(dt2_t.ap[:, 128:256])
    dendv = dt2_t.v(dt2_t.ap[:, 256:384].rearrange("p (a b) -> p a b", a=4))
    dtalv = dt2_t.v(dt2_t.ap[:, 384:512].rearrange("p (a b) -> p a b", a=4))

    def bc_heads(v3, c, g, width):
        ap = v3.ap[:, c, 4 * g:4 * g + 4].unsqueeze(2).broadcast_to([128, 4, width])
        return View(ap, v3.keys)

    pp_ctr = [0]

    def next_pp():
        b = pp_ctr[0] % 2
        pp_ctr[0] += 1
        return b

    def load_x_tile(row0, buf):
        P.dma("sp", xt[buf].all(), xp[row0:row0 + 128, :])

    hT_bufs = [hT, hT2]
    xt1 = [A(S0 + 0, 2, F32), A(S0 + 2, 2, F32)]
    xn1 = A(S0 + 4, 1, BF16)
    junk1 = A(S0 + 5, 2, BF16)
    ss1 = P.sbt("ss1", [128, 4], F32)
    for sb in range(NSB):
        hT = hT_bufs[sb % 2]
        P.tag = "sb%d.st1" % sb
        for i in range(4):
            row0 = sb * TB + i * 128
            buf = i % 2
            P.dma("sp", xt1[buf].all(), xp[row0:row0 + 128, :])
            P.stt("dve", junk1[:, 0:1024], xt1[buf].all(), 1.0, xt1[buf].all(), ALU.mult, ALU.mult,
                  accum=ss1[:, 0:1])
            P.act(ss1[:, 1:2], ss1[:, 0:1], AF.Ln, bias=epsb[:, 0:1], scale=1.0 / D)
            P.act(ss1[:, 2:3], ss1[:, 1:2], AF.Exp, scale=-0.5)
            P.ts("dve", xn1.all(), xt1[buf].all(), ss1[:, 2:3], None, ALU.mult)
            pt = P.psb(2 + (i % 2))
            pt3 = pt.v(pt.ap.rearrange("p (a b) -> p a b", a=8))
            for kc in range(8):
                P.tr(View(pt3.ap[:, kc, :], pt.keys), xn1[:, kc * 128:(kc + 1) * 128], identb.all())
            for kc in range(8):
                P.act(hT[:, kc, i * 128:(i + 1) * 128], View(pt3.ap[:, kc, :], pt.keys), AF.Identity,
                      bias=modT[:, kc, 0:1], scale=G_all[:, kc, 0:1])

        if sb == 0:
            dump("d_hT", hT.all(), bf=True)
        if STOP <= 1:
            break
        P.tag = "sb%d.st2" % sb
        pdt = P.ps(3, 0)
        for i in range(4):
            for kc in range(8):
                P.mm(pdt[:, i * 32:(i + 1) * 32], hT[:, kc, i * 128:(i + 1) * 128], wdt[:, kc, :],
                     start=(kc == 0), stop=(kc == 7))
        P.tt("dve", dtmp, pdt.v(pdt.ap[:, 0:128].rearrange("p (a b) -> p a b", a=4)),
             dtb_bc.v(dtb_bc.ap.unsqueeze(1).broadcast_to([128, 4, 32])), ALU.add)
        P.act(dtmp, dtmp, AF.Exp)
        P.act(dtv, dtmp, AF.Ln, bias=1.0)
        P.tt("dve", dtav, dtv, a_bc.v(a_bc.ap.unsqueeze(1).broadcast_to([128, 4, 32])), ALU.mult)
        pcq = P.ps(4, 0)
        pct = P.ps(4, 1)
        dta2 = dt_t[:, 128:256]
        P.mm(pcq[:, 0:128], tri.all(), dta2)
        P.mm(pct[:, 0:128], onesf.all(), dta2)
        P.act(dt_t[:, 256:384], pcq[:, 0:128], AF.Exp)
        P.act(dt_t[:, 384:512], pct[:, 0:128], AF.Exp)
        P.copy("dve", cqs, pcq[:, 0:128])
        P.tt("dve", dt2_t[:, 256:384], pct[:, 0:128], cqs, ALU.subtract)
        P.act(dt2_t[:, 256:384], dt2_t[:, 256:384], AF.Exp)
        P.copy("dve", xn1[:, 0:128], dta2)
        P.copy("dve", dt2_t[:, 0:128], xn1[:, 0:128])
        P.tt("dve", dt2_t[:, 384:512], dta2, dt2_t[:, 0:128], ALU.subtract)

        if sb == 0:
            dump("d_dt", dt_t.all())
            dump("d_dt2", dt2_t.all())
        if STOP <= 2:
            break
        xh3 = xhalo.v(xhalo.ap[:, 0:96].rearrange("p (a b) -> p a b", a=32))
        CH = lambda g, j: (2 * g, 2 * g + 1, 16 + g, 24 + g)[j]

        def phaseA(g):
            w = rbv(w_get(jid[(sb, "wg", g)]), 0, 8, 768)
            xp_t = xpad[g % 2]
            xp3 = xp_t.v(xp_t.ap[:, 0:2060].rearrange("p (a b) -> p a b", a=4))
            xa = xact[g % 2]
            zz = zs[g % 2]
            dg = Dg[g % 2]
            P.copy("pool", View(xp3.ap[:, :, 0:3], xp_t.keys), View(xh3.ap[:, 4 * g:4 * g + 4, :], xhalo.keys))
            for j in range(4):
                c0 = PC_SDW + CH(g, j)
                P.tt("dve" if sb == 0 else "pool", dg[:, j, :, :], identb.v(identb.ap.unsqueeze(1).broadcast_to([128, 4, 128])),
                     prmT.v(prmT.ap[:, c0:c0 + 97:32].unsqueeze(2).broadcast_to([128, 4, 128])), ALU.mult)
            yield
            if sb == NSB - 1:
                pq = P.ps(next_pp())
                for kc in range(8):
                    P.mm(pq[0:32, :], hT[:, kc, 480:512], w[:, kc, 0:512], start=(kc == 0), stop=(kc == 7))
                tst = tailst[g % 2]
                P.copy("act", tst[0:32, :], pq[0:32, :])
                P.dma("sp", o_ssmc_p[:, 256 * g:256 * g + 256], tst[29:32, 0:256])
                P.dma("sp", o_ssmc_p[:, 2048 + 128 * g:2048 + 128 * g + 128], tst[29:32, 256:384])
                P.dma("sp", o_ssmc_p[:, 3072 + 128 * g:3072 + 128 * g + 128], tst[29:32, 384:512])
            for j in range(4):
                pj = P.ps(next_pp())
                for kc in range(8):
                    P.mm(pj.all(), w[:, kc, j * 128:(j + 1) * 128], hT[:, kc, :], start=(kc == 0), stop=(kc == 7))
                P.copy("act", View(xp3.ap[:, j, 3:515], xp_t.keys), pj.all())
                yield
            P.copy("pool", View(xh3.ap[:, 4 * g:4 * g + 4, :], xhalo.keys), View(xp3.ap[:, :, 512:515], xp_t.keys))
            for j in range(4):
                pj = P.ps(next_pp())
                for tap in range(4):
                    P.mm(pj.all(), dg[:, j, tap, :], View(xp3.ap[:, j, tap:tap + 512], xp_t.keys),
                         start=(tap == 0), stop=(tap == 3))
                col = PC_SDWB + CH(g, j)
                P.act(xa[:, j, :], pj.all(), AF.Silu, bias=prmT[:, col:col + 1])
                yield
            for j in range(2):
                pj = P.ps(next_pp())
                for kc in range(8):
                    P.mm(pj.all(), w[:, kc, 512 + j * 128:512 + (j + 1) * 128], hT[:, kc, :],
                         start=(kc == 0), stop=(kc == 7))
                P.act(zz[:, j, :], pj.all(), AF.Silu)
                yield

        aux = "dve" if sb == 0 else "pool"
        xdt_b = [xdt, xdt2]
        xdtd_b = [xdtd, xdtd2]
        cbm_b = [cbm, cbm2]
        ytok_b = [ytok, ytok2]

        def pre(g):
            xa = xact[g % 2]
            xdt_ = xdt_b[g % 2]
            xdtd_ = xdtd_b[g % 2]
            cbm_ = cbm_b[g % 2]
            bo = (g % 2) * 512
            ptx = P.psb(2)
            ptx3 = ptx.v(ptx.ap.rearrange("p (a b) -> p a b", a=4))
            for c in range(4):
                for j in range(2):
                    P.tr(View(ptx3.ap[:, c, j * 128:(j + 1) * 128], ptx.keys), xa[:, j, c * 128:(c + 1) * 128],
                         identb.all())
            dt_b = View(dtv.ap[:, :, 4 * g:4 * g + 4].unsqueeze(3).broadcast_to([128, 4, 4, 64]), dt_t.keys)
            P.tt("dve", xdt_.v(xdt_.ap.rearrange("p a (h q) -> p a h q", h=4)),
                 ptx.v(ptx.ap.rearrange("p (a h q) -> p a h q", a=4, h=4)), dt_b, ALU.mult)
            yield
            ptb = P.psb(4)
            for c in range(4):
                P.tr(ptb[:, c * 128:(c + 1) * 128], xa[:, 2, c * 128:(c + 1) * 128], identb.all())
            P.copy("act", Btok[:, bo:bo + 512], ptb[:, 0:512])
            pcb = P.ps(4)
            for c in range(4):
                P.mm(pcb[:, c * 128:(c + 1) * 128], xa[:, 2, c * 128:(c + 1) * 128], xa[:, 3, c * 128:(c + 1) * 128])
            P.tt("dve", cbm_.all(), pcb.v(pcb.ap.rearrange("p (a b) -> p a b", a=4)),
                 tri.v(tri.ap.unsqueeze(1).broadcast_to([128, 4, 128])), ALU.mult)
            dend_b = View(dendv.ap[:, :, 4 * g:4 * g + 4].unsqueeze(3).broadcast_to([128, 4, 4, 64]), dt2_t.keys)
            P.tt(aux, xdtd_.v(xdtd_.ap.rearrange("p a (h q) -> p a h q", h=4)),
                 xdt_.v(xdt_.ap.rearrange("p a (h q) -> p a h q", h=4)), dend_b, ALU.mult)
            yield

        def chunks(g):
            xa = xact[g % 2]
            xdt_ = xdt_b[g % 2]
            xdtd_ = xdtd_b[g % 2]
            cbm_ = cbm_b[g % 2]
            ytok_ = ytok_b[g % 2]
            bo = (g % 2) * 512
            P.copy("act", HTb[:, 0:256], HT[:, g, :])

            def front(c):
                r1 = R1[c % 2]
                et = Et[c % 2]
                mt = MT[c % 2]
                mt3 = mt.v(mt.ap[:, 0:512].rearrange("p (a b) -> p a b", a=4))
                r1b = r1.ap.rearrange("p a b -> p (a b)").bitcast(BF16)
                r1h = View(r1b[:, 0:512].rearrange("p (a b) -> p a b", a=4), r1.keys)
                r1l = View(r1b[:, 512:1024].rearrange("p (a b) -> p a b", a=4), r1.keys)
                P.tt("dve", r1h, tri.v(tri.ap.unsqueeze(1).broadcast_to([128, 4, 128])),
                     View(dtav.ap[:, c, 4 * g:4 * g + 4].unsqueeze(2).broadcast_to([128, 4, 128]), dt_t.keys),
                     ALU.mult)
                P.tt(aux, r1l, tri.v(tri.ap.unsqueeze(1).broadcast_to([128, 4, 128])),
                     View(dtalv.ap[:, c, 4 * g:4 * g + 4].unsqueeze(2).broadcast_to([128, 4, 128]), dt2_t.keys),
                     ALU.mult)
                pseg = P.ps(5 + (c % 2))
                P.mm(pseg.all(), sutb.all(), View(r1b[:, 0:512], r1.keys), start=True, stop=False)
                P.mm(pseg.all(), sutb.all(), View(r1b[:, 512:1024], r1.keys), start=False, stop=True)
                P.act(et.v(et.ap.rearrange("p a b -> p (a b)")), pseg.all(), AF.Exp)
                P.tt("dve", mt3, et.all(), View(cbm_.ap[:, c:c + 1, :].broadcast_to([128, 4, 128]), cbm_.keys), ALU.mult)

            def back(c):
                mt = MT[c % 2]
                mt3 = mt.v(mt.ap[:, 0:512].rearrange("p (a b) -> p a b", a=4))
                py1 = P.ps(7, 0)
                py2 = P.ps(7, 1)
                for h in range(4):
                    P.mm(py1[:, h * 64:(h + 1) * 64], View(mt3.ap[:, h, :], mt.keys), xdt_[:, c, h * 64:(h + 1) * 64])
                P.mm(py2.all(), xa[:, 3, c * 128:(c + 1) * 128], HTb[:, 0:256])
                P.tt("dve", ytmp.v(ytmp.ap[:, 0:256].rearrange("p (h q) -> p h q", h=4)),
                     py2.v(py2.ap.rearrange("p (h q) -> p h q", h=4)), bc_heads(ecv, c, g, 64), ALU.mult)
                P.tt("dve", ytok_[:, c, :], ytmp[:, 0:256], py1.all(), ALU.add)
                pst = P.ps(3, 0)
                P.mm(pst.all(), Btok[:, bo + c * 128:bo + (c + 1) * 128], xdtd_[:, c, :])
                hg = View(HT.ap[:, g, :].rearrange("p (h q) -> p h q", h=4), HT.keys)
                P.tt("dve", hg, hg, bc_heads(cdv, c, g, 64), ALU.mult)
                P.tt("dve", HT[:, g, :], HT[:, g, :], pst.all(), ALU.add)
                if c < 3:
                    P.copy("act", HTb[:, 0:256], HT[:, g, :])

            front(0)
            for c in range(4):
                if c + 1 < 4:
                    front(c + 1)
                yield
                back(c)
                yield
            if sb == 0 and g == 0:
                dump("d_xact", xa.all(), bf=True)
                dump("d_ytok", ytok_.all(), bf=True)

        def epi(g):
            xa = xact[g % 2]
            zz = zs[g % 2]
            ytok_ = ytok_b[g % 2]
            pyt = P.psb(next_pp())
            pyt3 = pyt.v(pyt.ap.rearrange("p (a b) -> p a b", a=2))
            for j in range(2):
                for c in range(4):
                    P.tr(View(pyt3.ap[:, j, c * 128:(c + 1) * 128], pyt.keys), ytok_[:, c, j * 128:(j + 1) * 128],
                         identb.all())
            for j in range(2):
                P.stt("dve", t1[:, j, :], xa[:, j, :], D_fm[:, 2 * g + j:2 * g + j + 1],
                      View(pyt3.ap[:, j, :], pyt.keys), ALU.mult, ALU.add)
            yield
            for j in range(2):
                P.tt(aux, yg[:, j, :], t1[:, j, :], zz[:, j, :], ALU.mult)
                P.act(sq[:, j, :], yg[:, j, :], AF.Square)
            yield
            pms = P.ps(next_pp())
            for j in range(2):
                P.mm(pms.all(), onesb.all(), sq[:, j, :], start=(j == 0), stop=(j == 1))
            P.act(rstd.all(), pms.all(), AF.Ln, bias=epsb[:, 0:1], scale=1.0 / 256.0)
            yield
            P.act(rstd.all(), rstd.all(), AF.Exp, scale=-0.5)
            for j in range(2):
                col = PC_NG + 2 * g + j
                P.stt("dve", yn[:, 2 * g + j, :], yg[:, j, :], prmT[:, col:col + 1], rstd.all(), ALU.mult, ALU.mult)
            yield

        def chain(*gens):
            for g_ in gens:
                yield from g_

        def interleave(*gens):
            live = [g_ for g_ in gens if g_ is not None]
            while live:
                for g_ in list(live):
                    try:
                        next(g_)
                    except StopIteration:
                        live.remove(g_)

        interleave(chain(phaseA(0), pre(0)))
        for g in range(8):
            interleave(chunks(g),
                       chain(phaseA(g + 1), pre(g + 1)) if g + 1 < 8 else None,
                       epi(g - 1) if g >= 1 else None)
        interleave(epi(7))

        if sb == 0:
            dump("d_yn", yn.all(), bf=True)
        if STOP <= 3:
            break
        P.tag = "sb%d.st3" % sb
        for dc in range(8):
            rb = w_get(jid[(sb, "s3", dc)])
            wso_t = rbv(rb, 0, 16, 128)
            wm_t = rbv(rb, 2048, 8, 128)
            pa = P.ps(next_pp())
            for cc in range(16):
                P.mm(pa.all(), wso_t[:, cc, :], yn[:, cc, :], start=(cc == 0), stop=(cc == 15))
            pb = P.ps(2 + (dc % 2))
            for kc in range(8):
                P.mm(pb.all(), wm_t[:, kc, :], hT[:, kc, :], start=(kc == 0), stop=(kc == 7))
            P.act(sgm.all(), pb.all(), AF.Sigmoid)
            P.tt("dve", gbs[:, dc, :], pa.all(), sgm.all(), ALU.mult)

        if sb == 0:
            dump("d_gbs", gbs.all(), bf=True)
        if STOP <= 4:
            break
        P.tag = "sb%d.st4" % sb
        uh3 = uhalo.v(uhalo.ap[:, 0:240].rearrange("p (a b) -> p a b", a=8))

        def conv_proj(cc):
            w = rbv(w_get(jid[(sb, "wc", cc)]), 0, 8, 384)
            up = upad[cc % 2]
            dcw = Dc[cc % 2]
            dcw3 = dcw.v(dcw.ap[:, 0:3968].rearrange("p (a b) -> p a b", a=31))
            P.copy("pool", up[:, 0:30], View(uh3.ap[:, cc, :], uhalo.keys))
            c0 = PC_CDW + cc
            P.tt("dve", dcw3, identb.v(identb.ap.unsqueeze(1).broadcast_to([128, 31, 128])),
                 prmT.v(prmT.ap[:, c0:c0 + 241:8].unsqueeze(2).broadcast_to([128, 31, 128])), ALU.mult)
            banks = (0, 1, 2) if cc % 2 == 0 else (5, 6, 7)
            pv, pgl, pgt = (P.ps(bk) for bk in banks)
            for q, pq in enumerate((pv, pgl, pgt)):
                for kc in range(8):
                    P.mm(pq.all(), w[:, kc, q * 128:(q + 1) * 128], hT[:, kc, :], start=(kc == 0), stop=(kc == 7))
            P.act(sgl.all(), pgl.all(), AF.Sigmoid)
            P.act(sgt[:, cc, :], pgt.all(), AF.Silu)
            P.tt("dve", up[:, 30:542], pv.all(), sgl.all(), ALU.mult)
            P.copy("pool", View(uh3.ap[:, cc, :], uhalo.keys), up[:, 512:542])
            if sb == NSB - 1:
                pq = P.ps(3 + (cc % 2))
                for kc in range(8):
                    P.mm(pq[0:32, 0:256], hT[:, kc, 480:512], w[:, kc, 0:256], start=(kc == 0), stop=(kc == 7))
                tst = tailst[cc % 2]
                P.act(tst[0:32, 0:128], pq[0:32, 128:256], AF.Sigmoid)
                P.tt("dve", tailu[0:32, cc * 128:(cc + 1) * 128], pq[0:32, 0:128], tst[0:32, 0:128], ALU.mult)

        def conv_taps(cc):
            up = upad[cc % 2]
            dcw = Dc[cc % 2]
            dcw3 = dcw.v(dcw.ap[:, 0:3968].rearrange("p (a b) -> p a b", a=31))
            pcv = P.ps(3 + (cc % 2))
            for tap in range(31):
                P.mm(pcv.all(), View(dcw3.ap[:, tap, :], dcw.keys), up[:, tap:tap + 512],
                     start=(tap == 0), stop=(tap == 30))
            col = PC_CDWB + cc
            P.act(ycv[:, cc, :], pcv.all(), AF.Identity, bias=prmT[:, col:col + 1])

        conv_proj(0)
        for cc in range(8):
            if cc + 1 < 8:
                conv_proj(cc + 1)
            conv_taps(cc)
        if sb == 0:
            dump("d_ycv", ycv.all())
        pmean = P.ps(5)
        pmsq = P.ps(6)
        for cc in range(8):
            P.copy("act", ysq[:, 0:512], ycv[:, cc, :])
            P.mm(pmean.all(), onesb.all(), ysq[:, 0:512], start=(cc == 0), stop=(cc == 7))
            P.act(ysq[:, 512:1024], ycv[:, cc, :], AF.Square)
            P.mm(pmsq.all(), onesb.all(), ysq[:, 512:1024], start=(cc == 0), stop=(cc == 7))
        P.ts("dve", mean_t.all(), pmean.all(), 1.0 / D, None, ALU.mult)
        P.tt("dve", var_t.all(), mean_t.all(), mean_t.all(), ALU.mult)
        P.stt("dve", var_t.all(), pmsq.all(), 1.0 / D, var_t.all(), ALU.mult, ALU.subtract)
        P.act(var_t.all(), var_t.all(), AF.Ln, bias=epsb[:, 0:1])
        P.act(var_t.all(), var_t.all(), AF.Exp, scale=-0.5)
        for cc in range(8):
            P.tt("dve", ycv[:, cc, :], ycv[:, cc, :], mean_t.all(), ALU.subtract)
            P.tt("dve", ycv[:, cc, :], ycv[:, cc, :], var_t.all(), ALU.mult)
            P.act(ycv[:, cc, :], ycv[:, cc, :], AF.Silu, bias=prmT[:, PC_LNB + cc:PC_LNB + cc + 1],
                  scale=prmT[:, PC_LNG + cc:PC_LNG + cc + 1])
            P.tt("dve", vt[:, cc, :], ycv[:, cc, :], sgt[:, cc, :], ALU.mult)

        if sb == 0:
            dump("d_vt", vt.all(), bf=True)
        if STOP <= 5:
            break
        P.tag = "sb%d.st5" % sb
        for dc in range(8):
            rb = w_get(jid[(sb, "s5", dc)])
            wco_t = rbv(rb, 0, 8, 128)
            wmc_t = rbv(rb, 1024, 8, 128)
            pa = P.ps(next_pp())
            for cc in range(8):
                P.mm(pa.all(), wco_t[:, cc, :], vt[:, cc, :], start=(cc == 0), stop=(cc == 7))
            pb = P.ps(2 + (dc % 2))
            for kc in range(8):
                P.mm(pb.all(), wmc_t[:, kc, :], hT[:, kc, :], start=(kc == 0), stop=(kc == 7))
            P.act(sgc.all(), pb.all(), AF.Sigmoid)
            P.tt("dve", sgc.all(), pa.all(), sgc.all(), ALU.mult)
            P.tt("dve", merged[:, dc, :], sgc.all(), gbs[:, dc, :], ALU.add)

        if sb == 0:
            dump("d_merged", merged.all(), bf=True)
        if STOP <= 6:
            break
        P.tag = "sb%d.st6" % sb
        wo_t = [rbv(w_get(jid[(sb, "wo", half)], hold=jid[(sb, "wo", 0)]), 0, 8, 512) for half in range(2)]
        for i in range(4):
            row0 = sb * TB + i * 128
            buf = i % 2
            load_x_tile(row0, buf)
            po = [P.ps(next_pp()), None]
            po[1] = P.ps(2 + (i % 2))
            for half in range(2):
                for kc in range(8):
                    P.mm(po[half].all(), merged[:, kc, i * 128:(i + 1) * 128], wo_t[half][:, kc, :],
                         start=(kc == 0), stop=(kc == 7))
            o_t = ot[buf]
            for half in range(2):
                P.copy("act", o_t[:, half * 512:(half + 1) * 512], po[half].all())
            P.stt("dve", junk[:, 1024:2048], o_t.all(), 1.0, o_t.all(), ALU.mult, ALU.mult, accum=ss[:, 4:5])
            P.act(ss[:, 6:7], ss[:, 4:5], AF.Ln, bias=epsb[:, 0:1], scale=1.0 / D)
            P.act(ss[:, 7:8], ss[:, 6:7], AF.Exp, scale=-0.5)
            P.tt("dve", o_t.all(), o_t.all(), GGp.all(), ALU.mult)
            P.stt("dve", o_t.all(), o_t.all(), ss[:, 7:8], xt[buf].all(), ALU.mult, ALU.add)
            P.dma("sp", y_prompt[row0:row0 + 128, :], o_t.all())

        P.tag = "tail"
        if sb == NSB - 1:
            P.dma("sp", o_conf_p, tailu[2:32, 0:1024])
            for g in range(8):
                pq = P.ps(g % 2)
                for j in range(2):
                    P.tr(pq[:, j * 128:(j + 1) * 128], HT[:, g, j * 128:(j + 1) * 128], identf.all())
                P.copy("act", xt[g % 2][:, 0:256], pq[:, 0:256])
                for j in range(2):
                    r0 = (2 * g + j) * 128
                    P.dma("sp", o_ssm_p[r0:r0 + 128, :], xt[g % 2][:, j * 128:(j + 1) * 128])

    if with_sample and STOP > 6:
        smp = P.sbt("smp", [128, 144], F32)
        P.dma("sp", smp.all(), c_smp[:, 0:144])
        bmk = P.sbt("bmk", [128, 1024], BF16)
        P.dma("pool", bmk.all(), c_smp[:, 144:1168])
        bdtri = smp[0:64, 0:64]
        samem = smp[0:64, 64:128]
        seqm = smp[0:64, 128:144]
        regs = {"a": [0, 20], "t": [20, 32], "b": [32, 44], "e": [44, 69], "c": [69, 92]}

        def SA(n, dtype, pattern=None, reg="c", **kw):
            lo, hi = regs[reg]
            assert lo + n <= hi, (reg, lo, n, hi)
            t_ = P.at(lo, n, dtype, pattern, **kw)
            regs[reg][0] = lo + n
            return t_
        xts = SA(2, F32, reg="a")
        xns = SA(1, BF16, reg="a")
        sm1 = SA(1, F32, reg="a")
        hTs = SA(1, BF16, reg="e")
        hTs3 = hTs.v(hTs.ap[:, 0:512].rearrange("p (a b) -> p a b", a=8))
        projS = SA(12, F32, reg="e")
        prS = projS.v(projS.ap[:, 0:5696].rearrange("p (a b) -> p a b", a=89))
        tokS_slot0 = regs["t"][0]
        tokS = SA(12, F32, reg="t")
        wts = [SA(6, BF16, reg="e") for _ in range(2)]
        rowsA = SA(8, F32, reg="a")
        rowsB_slot0 = regs["a"][0]
        rowsB = SA(8, F32, reg="a")
        upad_s = SA(1, BF16, reg="b")
        ups3 = upad_s.v(upad_s.ap[:, 0:544].rearrange("p (a b) -> p a b", a=16))
        xpad_s = SA(4, BF16, reg="c")
        xps4 = xpad_s.v(xpad_s.ap[:, 0:3584].rearrange("p (c a b) -> p c a b", c=32, a=16))
        xact_s = SA(2, BF16, "p (a b) -> p a b", reg="b", a=32)
        zs_s = SA(1, BF16, "p (a b) -> p a b", reg="b", a=16)
        ycv_s = SA(1, F32, "p (a b) -> p a b", reg="b", a=8)
        sgt_s = SA(1, BF16, "p (a b) -> p a b", reg="b", a=16)
        vt_s = SA(1, BF16, "p (a b) -> p a b", reg="b", a=16)
        yn_s = SA(1, BF16, "p (a b) -> p a b", reg="b", a=16)
        gbs_s = SA(1, F32, "p (a b) -> p a b", reg="b", a=8)
        mrg_s = SA(1, BF16, "p (a b) -> p a b", reg="b", a=16)
        dgs = SA(1, BF16, "p (a b) -> p a b", reg="b", a=8)
        Dcs = SA(4, BF16)
        Dcs3 = Dcs.v(Dcs.ap[:, 0:3968].rearrange("p (a b) -> p a b", a=31))
        GGs = SA(2, F32)
        sm2 = SA(1, F32)
        sm3 = SA(1, F32)
        sm4 = SA(1, F32)
        sb1 = SA(1, BF16)
        sb2 = SA(1, BF16)
        sb3 = SA(1, BF16)
        sb4 = SA(1, BF16)
        sb5 = SA(1, BF16)
        ots = SA(2, F32)
        dts = SA(1, F32)

        P.tag = "S1"
        P.dma("sp", xts[0:64, :], xs_in)
        P.stt("dve", wts[1][0:64, 0:1024], xts[0:64, :], 1.0, xts[0:64, :], ALU.mult, ALU.mult, accum=ss1[0:64, 0:1])
        P.act(ss1[0:64, 1:2], ss1[0:64, 0:1], AF.Ln, bias=epsb[0:64, 0:1], scale=1.0 / D)
        P.act(ss1[0:64, 2:3], ss1[0:64, 1:2], AF.Exp, scale=-0.5)
        P.ts("dve", xns[0:64, :], xts[0:64, :], ss1[0:64, 2:3], None, ALU.mult)
        pt = P.psb(2)
        for kc in range(8):
            P.tr(pt[:, kc * 64:(kc + 1) * 64], xns[0:64, kc * 128:(kc + 1) * 128], identb[0:64, 0:64])
        pt4 = pt.v(pt.ap[:, 0:512].rearrange("p (a b c) -> p a b c", a=8, b=16))
        Gs = View(G_all.ap[:, :, 1:17].unsqueeze(3).broadcast_to([128, 8, 16, 4]), G_all.keys)
        Ss = View(modT.ap[:, 0:8, 1:17].unsqueeze(3).broadcast_to([128, 8, 16, 4]), modT.keys)
        t4 = sm1.v(sm1.ap[:, 0:512].rearrange("p (a b c) -> p a b c", a=8, b=16))
        P.tt("dve", t4, pt4, Gs, ALU.mult)
        P.tt("dve", hTs.v(hTs.ap[:, 0:512].rearrange("p (a b c) -> p a b c", a=8, b=16)), t4, Ss, ALU.add)
        ggr = sm2.v(sm2.ap[:, 0:512].rearrange("p (a b c) -> p a b c", a=8, b=16))
        P.tt("dve", ggr, View(modT.ap[:, 16:24, 1:17].unsqueeze(3).broadcast_to([128, 8, 16, 4]), modT.keys),
             View(prmT.ap[:, PC_GPOST:PC_GPOST + 8].unsqueeze(2).unsqueeze(3).broadcast_to([128, 8, 16, 4]), prmT.keys),
             ALU.mult)
        for half in range(2):
            pg = P.ps(4 + half)
            for q in range(4):
                dc = half * 4 + q
                P.tr(pg[0:64, q * 128:(q + 1) * 128], sm2[:, dc * 64:(dc + 1) * 64], identf.all())
            P.copy("act", GGs[0:64, half * 512:(half + 1) * 512], pg[0:64, :])

        P.tag = "S2"
        tile_ctr = [0]

        def s2_load(slot, c_lo, c_hi):
            w = wts[tile_ctr[0] % 2]
            tile_ctr[0] += 1
            P.dma("sp", w[:, c_lo:c_hi], View(scr[slot, :, c_lo:c_hi], [("scr", slot)]))
            return w

        def fm_block(w, base, W, c0, pslot):
            wv = w.ap[:, base:base + 8 * W].rearrange("p (a b) -> p a b", a=8)
            pj = P.ps(next_pp())
            for kc in range(8):
                P.mm(pj[:, 0:64], View(wv[:, kc, c0:c0 + 128], w.keys), View(hTs3.ap[:, kc, :], hTs.keys),
                     start=(kc == 0), stop=(kc == 7))
            P.copy("act", View(prS.ap[:, pslot, :], projS.keys), pj[:, 0:64])

        def tm_block(w, base, W, c0, wid, tok_off):
            wv = w.ap[:, base:base + 8 * W].rearrange("p (a b) -> p a b", a=8)
            pj = P.ps(2 + (tile_ctr[0] % 2))
            for kc in range(8):
                P.mm(pj[0:64, 0:wid], View(hTs3.ap[:, kc, :], hTs.keys), View(wv[:, kc, c0:c0 + wid], w.keys),
                     start=(kc == 0), stop=(kc == 7))
            P.copy("act", tokS[0:64, tok_off:tok_off + wid], pj[0:64, 0:wid])

        for g in range(8):
            w = s2_load(g, 0, 6144)
            for c0, ps_ in ((0, 40 + 2 * g), (128, 41 + 2 * g), (256, 56 + g), (384, 64 + g),
                            (512, 24 + 2 * g), (640, 25 + 2 * g)):
                fm_block(w, 0, 768, c0, ps_)
            tm_block(w, 0, 768, 0, 256, 2048 + 256 * g)
            tm_block(w, 0, 768, 256, 128, 2048 + 2048 + 128 * g)
            tm_block(w, 0, 768, 384, 128, 2048 + 3072 + 128 * g)
        for dc in range(8):
            w = s2_load(8 + dc, 2048, 3072)
            fm_block(w, 2048, 128, 0, 81 + dc)
        for cc in range(8):
            w = s2_load(16 + cc, 0, 3072)
            fm_block(w, 0, 384, 0, cc)
            fm_block(w, 0, 384, 128, 8 + cc)
            fm_block(w, 0, 384, 256, 16 + cc)
            tm_block(w, 0, 384, 0, 128, cc * 128)
            tm_block(w, 0, 384, 128, 128, 1024 + cc * 128)
        for dc in range(8):
            w = s2_load(24 + dc, 1024, 2048)
            fm_block(w, 1024, 128, 0, 73 + dc)
        pdt = P.ps(3, 0)
        for kc in range(8):
            P.mm(pdt[0:64, 0:32], View(hTs3.ap[:, kc, :], hTs.keys), wdt[:, kc, :], start=(kc == 0), stop=(kc == 7))
        dt_s = dts[0:64, 0:32]
        dta_s = dts[0:64, 32:64]
        e_s = dts[0:64, 64:96]
        dend_s = dts[0:64, 96:128]
        P.tt("dve", dt_s, pdt[0:64, 0:32], dtb_bc[0:64, :], ALU.add)
        P.act(dt_s, dt_s, AF.Exp)
        P.act(dt_s, dt_s, AF.Ln, bias=1.0)
        P.tt("dve", dta_s, dt_s, a_bc[0:64, :], ALU.mult)
        pcq = P.ps(4, 0)
        pct = P.ps(5, 0)
        P.mm(pcq[0:64, 0:32], bdtri, dta_s)
        P.mm(pct[0:64, 0:32], samem, dta_s)
        P.act(e_s, pcq[0:64, 0:32], AF.Exp)
        P.copy("dve", sm3[0:64, 0:32], pcq[0:64, 0:32])
        P.tt("dve", dend_s, pct[0:64, 0:32], sm3[0:64, 0:32], ALU.subtract)
        P.act(dend_s, dend_s, AF.Exp)

        P.tag = "S3"
        P.act(sm4[0:64, :], tokS[0:64, 1024:1536], AF.Sigmoid)
        P.tt("dve", ots[0:64, 0:512], tokS[0:64, 0:512], sm4[0:64, :], ALU.mult)
        P.act(sm4[0:64, :], tokS[0:64, 1536:2048], AF.Sigmoid)
        P.tt("dve", ots[0:64, 512:1024], tokS[0:64, 512:1024], sm4[0:64, :], ALU.mult)
        for b in range(16):
            P.dma("sp", o_conf_s[b, 0:26, :], st_conf[b, 4:30, :])
            P.dma("sp", o_conf_s[b, 26:30, :], ots[4 * b:4 * b + 4, :])
            P.dma("sp", o_ssmc_s[b, :, :], tokS[4 * b + 1:4 * b + 4, 2048:6144])
        stc = st_conf.rearrange("b j d -> (b j) d")
        rA = rowsA.v(rowsA.ap.rearrange("p (a b) -> p a b", a=4))
        for rt in range(4):
            nr = 128 if rt < 3 else 96
            P.dma("sp", View(rA.ap[0:nr, rt, :], rowsA.keys), stc[rt * 128:rt * 128 + nr, :], grp="rowsA")
        for cc in range(8):
            ph = P.ps(6)
            for rt in range(4):
                nr = 128 if rt < 3 else 96
                P.tr(ph[:, rt * 128:rt * 128 + nr], View(rA.ap[0:nr, rt, cc * 128:(cc + 1) * 128], rowsA.keys),
                     identf[0:nr, 0:nr])
            P.copy("act", View(ups3.ap[:, :, 0:30], upad_s.keys),
                   ph.v(ph.ap[:, 0:480].rearrange("p (a b) -> p a b", a=16)))
            P.act(sm4[:, 0:64], View(prS.ap[:, 8 + cc, :], projS.keys), AF.Sigmoid)
            P.tt("dve", View(ups3.ap[:, :, 30:34], upad_s.keys),
                 View(prS.ap[:, cc, :].rearrange("p (a b) -> p a b", a=16), projS.keys),
                 sm4.v(sm4.ap[:, 0:64].rearrange("p (a b) -> p a b", a=16)), ALU.mult)
            P.act(sgt_s[:, cc, :], View(prS.ap[:, 16 + cc, :], projS.keys), AF.Silu)
            for tap in range(31):
                col = PC_CDW + tap * 8 + cc
                P.ts("dve", View(Dcs3.ap[:, tap, :], Dcs.keys), identb.all(), prmT[:, col:col + 1], None, ALU.mult)
            pcv = P.ps(7)
            pcv3 = pcv.v(pcv.ap[:, 0:64].rearrange("p (a b) -> p a b", a=16))
            for tap in range(31):
                P.mm(pcv3, View(Dcs3.ap[:, tap, :], Dcs.keys), View(ups3.ap[:, :, tap:tap + 4], upad_s.keys),
                     start=(tap == 0), stop=(tap == 30))
            P.act(ycv_s[:, cc, :], pcv[:, 0:64], AF.Identity, bias=prmT[:, PC_CDWB + cc:PC_CDWB + cc + 1])
        pmean = P.ps(5)
        pmsq = P.ps(6)
        for cc in range(8):
            P.copy("act", sb1[:, 0:64], ycv_s[:, cc, :])
            P.mm(pmean[:, 0:64], onesb.all(), sb1[:, 0:64], start=(cc == 0), stop=(cc == 7))
            P.act(sb1[:, 64:128], ycv_s[:, cc, :], AF.Square)
            P.mm(pmsq[:, 0:64], onesb.all(), sb1[:, 64:128], start=(cc == 0), stop=(cc == 7))
        mean_s = sm3[:, 64:128]
        var_s = sm3[:, 128:192]
        P.ts("dve", mean_s, pmean[:, 0:64], 1.0 / D, None, ALU.mult)
        P.tt("dve", var_s, mean_s, mean_s, ALU.mult)
        P.stt("dve", var_s, pmsq[:, 0:64], 1.0 / D, var_s, ALU.mult, ALU.subtract)
        P.act(var_s, var_s, AF.Ln, bias=epsb[:, 0:1])
        P.act(var_s, var_s, AF.Exp, scale=-0.5)
        for cc in range(8):
            P.tt("dve", ycv_s[:, cc, :], ycv_s[:, cc, :], mean_s, ALU.subtract)
            P.tt("dve", ycv_s[:, cc, :], ycv_s[:, cc, :], var_s, ALU.mult)
            P.act(ycv_s[:, cc, :], ycv_s[:, cc, :], AF.Silu, bias=prmT[:, PC_LNB + cc:PC_LNB + cc + 1],
                  scale=prmT[:, PC_LNG + cc:PC_LNG + cc + 1])
            P.tt("dve", vt_s[:, cc, :], ycv_s[:, cc, :], sgt_s[:, cc, :], ALU.mult)

        P.tag = "S4"
        stm = st_ssmc.rearrange("b j d -> (b j) d")
        rB = rowsB.v(rowsB.ap)
        P.dma("sp", View(rB.ap[0:48, :], rowsB.keys), stm)
        for ch in range(32):
            ph = P.ps(6)
            P.tr(ph[:, 0:48], View(rB.ap[0:48, ch * 128:(ch + 1) * 128], rowsB.keys), identf[0:48, 0:48])
            P.copy("act", View(xps4.ap[:, ch, :, 0:3], xpad_s.keys), ph.v(ph.ap[:, 0:48].rearrange("p (a b) -> p a b", a=16)))
            P.copy("dve", View(xps4.ap[:, ch, :, 3:7], xpad_s.keys),
                   View(prS.ap[:, 40 + ch, :].rearrange("p (a b) -> p a b", a=16), projS.keys))
            for tap in range(4):
                col = PC_SDW + tap * 32 + ch
                P.ts("dve", dgs[:, tap, :], identb.all(), prmT[:, col:col + 1], None, ALU.mult)
            pj = P.ps(7)
            pj3 = pj.v(pj.ap[:, 0:64].rearrange("p (a b) -> p a b", a=16))
            for tap in range(4):
                P.mm(pj3, dgs[:, tap, :], View(xps4.ap[:, ch, :, tap:tap + 4], xpad_s.keys),
                     start=(tap == 0), stop=(tap == 3))
            P.act(xact_s[:, ch, :], pj[:, 0:64], AF.Silu, bias=prmT[:, PC_SDWB + ch:PC_SDWB + ch + 1])
        for zc in range(16):
            P.act(zs_s[:, zc, :], View(prS.ap[:, 24 + zc, :], projS.keys), AF.Silu)

        P.tag = "S5"
        tokS8 = P.at(tokS_slot0, 8, F32)
        h0gs = [rowsA.v(rowsA.ap.rearrange("p (b r n) -> p b r n", b=16, r=2)),
                tokS8.v(tokS8.ap.rearrange("p (b r n) -> p b r n", b=16, r=2))]
        h0keys = [rowsA.keys, tokS8.keys]
        newhs = []
        for i in range(8):
            t_ = P.at(rowsB_slot0 + i, 1, F32)
            newhs.append(Tile(t_.ap[:, 0:256].rearrange("p (r n) -> p r n", r=2), t_.keys))
        nh_ctr = [0]

        def load_h0(g_):
            for r_ in range(2):
                src = st_v[:, 256 * g_ + 128 * r_:256 * g_ + 128 * r_ + 128, :].rearrange("b p n -> p b n")
                P.dma(("sp", "act")[r_], View(h0gs[g_ % 2].ap[:, :, r_, :], h0keys[g_ % 2]), src, grp=("h0", g_))
        h0T = wts[0].v(wts[0].ap[:, 0:4096].rearrange("p (b c) -> p b c", b=16))
        xdm = wts[1].v(wts[1].ap[:, 0:4096].rearrange("p (b c) -> p b c", b=16))
        cmask = sb1.v(sb1.ap.rearrange("p (a b) -> p a b", a=16))
        bmask = bmk.v(bmk.ap.rearrange("p (a b) -> p a b", a=16))
        xdt_s = sb2[0:64, 0:256]
        xdtd_s = sb3[0:64, 0:256]
        Btok_s = sb4[0:64, 0:128]
        MT_s = sb5.v(sb5.ap[0:64, 0:256].rearrange("p (a b) -> p a b", a=4))
        cbm_s = sm1[0:64, 0:64]
        R1_s = sm2.v(sm2.ap[0:64, 0:256].rearrange("p (a b) -> p a b", a=4))
        E_s = sm4.v(sm4.ap[0:64, 0:256].rearrange("p (a b) -> p a b", a=4))
        ytmp_s = sm3[0:64, 256:512]
        ytok_s = Dcs[0:64, 0:256]
        dtaB = ots[0:64, 0:128]
        cdcol = dts[:, 128:160]
        st_v = st_ssm.rearrange("b h p n -> b (h p) n")
        load_h0(0)
        for g in range(8):
            if g + 1 < 8:
                load_h0(g + 1)
            h0g = h0gs[g % 2]
            h0k = h0keys[g % 2]
            for b in range(16):
                ph = P.ps(next_pp())
                for r_ in range(2):
                    P.tr(ph[:, r_ * 128:(r_ + 1) * 128], View(h0g.ap[:, b, r_, :], h0k), identf.all())
                P.copy("act", View(h0T.ap[:, b, :], wts[0].keys), ph[:, 0:256])
            ptx = P.psb(2)
            for j in range(2):
                P.tr(ptx[0:64, j * 128:(j + 1) * 128], xact_s[:, 2 * g + j, :], identb.all())
            P.tt("dve", View(xdt_s.ap.rearrange("p (h q) -> p h q", h=4), sb2.keys),
                 ptx.v(ptx.ap[0:64, 0:256].rearrange("p (h q) -> p h q", h=4)),
                 View(dt_s.ap[:, 4 * g:4 * g + 4].unsqueeze(2).broadcast_to([64, 4, 64]), dts.keys), ALU.mult)
            P.tt("dve", View(xdtd_s.ap.rearrange("p (h q) -> p h q", h=4), sb3.keys),
                 View(xdt_s.ap.rearrange("p (h q) -> p h q", h=4), sb2.keys),
                 View(dend_s.ap[:, 4 * g:4 * g + 4].unsqueeze(2).broadcast_to([64, 4, 64]), dts.keys), ALU.mult)
            ptb = P.psb(3)
            P.tr(ptb[0:64, 0:128], xact_s[:, 16 + g, :], identb.all())
            P.copy("act", Btok_s, ptb[0:64, 0:128])
            pcb = P.ps(4)
            P.mm(pcb[0:64, 0:64], xact_s[:, 16 + g, :], xact_s[:, 24 + g, :])
            P.tt("dve", cbm_s, pcb[0:64, 0:64], bdtri, ALU.mult)
            P.tt("pool", R1_s, View(tri.ap[0:64, 0:64].unsqueeze(1).broadcast_to([64, 4, 64]), tri.keys),
                 View(dta_s.ap[:, 4 * g:4 * g + 4].unsqueeze(2).broadcast_to([64, 4, 64]), dts.keys), ALU.mult)
            pseg = P.ps(5)
            P.mm(pseg[0:64, 0:256], sut[0:64, 0:64], View(R1_s.ap.rearrange("p a b -> p (a b)"), sm2.keys))
            P.act(View(E_s.ap.rearrange("p a b -> p (a b)"), sm4.keys), pseg[0:64, 0:256], AF.Exp)
            P.tt("dve", MT_s, E_s, View(cbm_s.ap.unsqueeze(1).broadcast_to([64, 4, 64]), sm1.keys), ALU.mult)
            py1 = P.ps(7, 0)
            py2 = P.ps(7, 1)
            for h in range(4):
                P.mm(py1[0:64, h * 64:(h + 1) * 64], View(MT_s.ap[:, h, :], sb5.keys),
                     View(xdt_s.ap[:, h * 64:(h + 1) * 64], sb2.keys))
            P.tt("dve", cmask, View(xact_s.ap[:, 24 + g, :].unsqueeze(1).broadcast_to([128, 16, 64]), xact_s.keys),
                 bmask, ALU.mult)
            for b in range(16):
                P.mm(py2[0:64, :], View(cmask.ap[:, b, :], sb1.keys), View(h0T.ap[:, b, :], wts[0].keys),
                     start=(b == 0), stop=(b == 15))
            P.tt("dve", View(ytmp_s.ap.rearrange("p (h q) -> p h q", h=4), sm3.keys),
                 py2.v(py2.ap[0:64, :].rearrange("p (h q) -> p h q", h=4)),
                 View(e_s.ap[:, 4 * g:4 * g + 4].unsqueeze(2).broadcast_to([64, 4, 64]), dts.keys), ALU.mult)
            P.tt("dve", ytok_s, ytmp_s, py1[0:64, :], ALU.add)
            pyt = P.psb(2)
            for j in range(2):
                P.tr(pyt[:, j * 64:(j + 1) * 64], View(ytok_s.ap[:, j * 128:(j + 1) * 128], Dcs.keys), identb[0:64, 0:64])
            t1s = sm2.v(sm2.ap[:, 256:384].rearrange("p (a b) -> p a b", a=2))
            ygs = sm2.v(sm2.ap[:, 384:512].rearrange("p (a b) -> p a b", a=2))
            sqs = sb5.v(sb5.ap[:, 512:640].rearrange("p (a b) -> p a b", a=2))
            for j in range(2):
                P.stt("dve", View(t1s.ap[:, j, :], sm2.keys), xact_s[:, 2 * g + j, :], D_fm[:, 2 * g + j:2 * g + j + 1],
                      pyt[:, j * 64:(j + 1) * 64], ALU.mult, ALU.add)
                P.tt("dve", View(ygs.ap[:, j, :], sm2.keys), View(t1s.ap[:, j, :], sm2.keys), zs_s[:, 2 * g + j, :], ALU.mult)
                P.act(View(sqs.ap[:, j, :], sb5.keys), View(ygs.ap[:, j, :], sm2.keys), AF.Square)
            pms = P.ps(4)
            for j in range(2):
                P.mm(pms[:, 0:64], onesb.all(), View(sqs.ap[:, j, :], sb5.keys), start=(j == 0), stop=(j == 1))
            rs_s = sm1[:, 64:128]
            P.act(rs_s, pms[:, 0:64], AF.Ln, bias=epsb[:, 0:1], scale=1.0 / 256.0)
            P.act(rs_s, rs_s, AF.Exp, scale=-0.5)
            for j in range(2):
                col = PC_NG + 2 * g + j
                P.stt("dve", yn_s[:, 2 * g + j, :], View(ygs.ap[:, j, :], sm2.keys), prmT[:, col:col + 1], rs_s,
                      ALU.mult, ALU.mult)
            P.tt("dve", xdm.v(xdm.ap[0:64, :, :]), View(xdtd_s.ap.unsqueeze(1).broadcast_to([64, 16, 256]), sb3.keys),
                 View(seqm.ap.unsqueeze(2).broadcast_to([64, 16, 256]), smp.keys), ALU.mult)
            pcd = P.ps(6)
            for j in range(2):
                P.copy("dve", View(dtaB.ap.rearrange("p (h q) -> p h q", h=2), ots.keys),
                       View(dta_s.ap[:, 4 * g + 2 * j:4 * g + 2 * j + 2].unsqueeze(2).broadcast_to([64, 2, 64]), dts.keys))
                P.mm(pcd[:, j * 16:(j + 1) * 16], dtaB, seqm)
            P.act(cdcol, pcd[:, 0:32], AF.Exp)
            for b in range(16):
                nh = newhs[nh_ctr[0] % 8]
                nh_ctr[0] += 1
                for j in range(2):
                    pst = P.ps(next_pp())
                    P.mm(pst[:, 0:128], View(xdm.ap[0:64, b, j * 128:(j + 1) * 128], wts[1].keys), Btok_s)
                    P.stt("dve", nh[:, j, :], View(h0g.ap[:, b, j, :], h0k),
                          View(cdcol.ap[:, j * 16 + b:j * 16 + b + 1], dts.keys), pst[:, 0:128], ALU.mult, ALU.add)
                P.dma("sp", o_ssm_s[b, 256 * g:256 * g + 256, :].rearrange("(r p) n -> p r n", p=128), nh.all())

        P.tag = "S6"
        wso_v2 = w_ssm_out.rearrange("(cc cp) n -> cp cc n", cp=128)
        wco_v2 = w_conf_out.rearrange("(cc cp) n -> cp cc n", cp=128)
        s6b = [wts[0], wts[1], rowsA.v(rowsA.ap.bitcast(BF16)), rowsB.v(rowsB.ap.bitcast(BF16))]
        s6k = [wts[0].keys, wts[1].keys, rowsA.keys, rowsB.keys]
        for dc in range(8):
            wb_ = s6b[dc % 4]
            wk_ = s6k[dc % 4]
            wsoS = View(wb_.ap[:, 0:2048].rearrange("p (a b) -> p a b", a=16), wk_)
            wcoS = View(wb_.ap[:, 2048:3072].rearrange("p (a b) -> p a b", a=8), wk_)
            P.dma("sp", View(wb_.ap[:, 0:2048], wk_), View(scr[8 + dc, :, 0:2048], [("scr", 8 + dc)]))
            P.dma("sp", View(wb_.ap[:, 2048:3072], wk_), View(scr[24 + dc, :, 0:1024], [("scr", 24 + dc)]))
            pa = P.ps(next_pp())
            for cc in range(16):
                P.mm(pa[:, 0:64], View(wsoS.ap[:, cc, :], wk_), yn_s[:, cc, :], start=(cc == 0), stop=(cc == 15))
            P.act(sm1[:, 0:64], View(prS.ap[:, 81 + dc, :], projS.keys), AF.Sigmoid)
            P.tt("dve", gbs_s[:, dc, :], pa[:, 0:64], sm1[:, 0:64], ALU.mult)
            pb = P.ps(next_pp())
            for cc in range(8):
                P.mm(pb[:, 0:64], View(wcoS.ap[:, cc, :], wk_), vt_s[:, cc, :], start=(cc == 0), stop=(cc == 7))
            P.act(sm1[:, 64:128], View(prS.ap[:, 73 + dc, :], projS.keys), AF.Sigmoid)
            P.tt("dve", sm1[:, 128:192], pb[:, 0:64], sm1[:, 64:128], ALU.mult)
            P.tt("dve", mrg_s[:, dc, :], sm1[:, 128:192], gbs_s[:, dc, :], ALU.add)
        wo_v2 = w_o.rearrange("(kc kp) n -> kp kc n", kp=128)
        for half in range(2):
            P.dma("sp", wts[half][:, 0:4096], View(scr[32 + half, :, 0:4096], [("scr", 32 + half)]))
        for half in range(2):
            wov = wts[half].ap[:, 0:4096].rearrange("p (a b) -> p a b", a=8)
            po_ = P.ps(next_pp())
            for kc in range(8):
                P.mm(po_[0:64, :], mrg_s[:, kc, :], View(wov[:, kc, :], wts[half].keys), start=(kc == 0), stop=(kc == 7))
            P.copy("act", ots[0:64, half * 512:(half + 1) * 512], po_[0:64, :])
        P.stt("dve", sb1[0:64, 0:1024], ots[0:64, :], 1.0, ots[0:64, :], ALU.mult, ALU.mult, accum=ss[0:64, 4:5])
        P.act(ss[0:64, 6:7], ss[0:64, 4:5], AF.Ln, bias=epsb[0:64, 0:1], scale=1.0 / D)
        P.act(ss[0:64, 7:8], ss[0:64, 6:7], AF.Exp, scale=-0.5)
        P.stt("dve", ots[0:64, :], ots[0:64, :], ss[0:64, 7:8], GGs[0:64, :], ALU.mult, ALU.mult)
        P.tt("pool", ots[0:64, :], ots[0:64, :], xts[0:64, :], ALU.add)
        P.dma("sp", y_sample, ots[0:64, :])

    P.emit()
    return nc, P, stack


_CONSTS = None


def _consts():
    global _CONSTS
    if _CONSTS is None:
        i = np.arange(128)
        ident = np.eye(128, dtype=np.float32)
        tri = (i[:, None] <= i[None, :]).astype(np.float32)
        sut = (i[:, None] > i[None, :]).astype(np.float32)
        smp = np.zeros((128, 1168), np.float32)
        t = np.arange(64)
        same = (t[:, None] // 4 == t[None, :] // 4)
        smp[:64, 0:64] = (same & (t[:, None] <= t[None, :])).astype(np.float32)
        smp[:64, 64:128] = same.astype(np.float32)
        smp[:64, 128:144] = (t[:, None] // 4 == np.arange(16)[None, :]).astype(np.float32)
        bm = (np.arange(16)[:, None] == t[None, :] // 4).astype(np.float32).reshape(1, 1024)
        smp[:, 144:1168] = bm
        _CONSTS = dict(c_ident=ident, c_tri=tri, c_sut=sut, c_smp=smp)
    return _CONSTS


def make_in_maps(inp, ncores=NCORE, NSB=4):
    f = lambda a: np.ascontiguousarray(np.asarray(a, dtype=np.float32))
    L = NSB * TB
    shared = dict(
        w_ada=f(inp["w_ada"][0]), b_ada=f(inp["b_ada"][0]), g_pre=f(inp["g_pre"][0]), g_post=f(inp["g_post"][0]),
        w_in=f(inp["w_in"][0]), conf_dw_w=f(inp["conf_dw_w"][0]), conf_dw_b=f(inp["conf_dw_b"][0]),
        conf_ln_g=f(inp["conf_ln_g"][0]), conf_ln_b=f(inp["conf_ln_b"][0]), w_conf_out=f(inp["w_conf_out"][0]),
        ssm_dw_w=f(inp["ssm_dw_w"][0]), ssm_dw_b=f(inp["ssm_dw_b"][0]), ssm_dt_bias=f(inp["ssm_dt_bias"][0]),
        ssm_a_log=f(inp["ssm_a_log"][0]), ssm_d=f(inp["ssm_d"][0]), ssm_norm_g=f(inp["ssm_norm_g"][0]),
        w_ssm_out=f(inp["w_ssm_out"][0]), w_o=f(inp["w_o"][0]), **_consts())
    maps = []
    for c in range(ncores):
        s = slice(16 * c, 16 * c + 16)
        m = dict(shared)
        m["xp"] = f(inp["x_prompt"][c][:L])
        m["xs"] = f(inp["x_sample"][s]).reshape(64, D)
        m["call"] = f(np.concatenate([np.asarray(inp["c_prompt"])[c:c + 1], np.asarray(inp["c_sample"])[s]], axis=0))
        m["st_conf"] = f(inp["state_conf_conv"][0, s])
        m["st_ssmc"] = f(inp["state_ssm_conv"][0, s])
        m["st_ssm"] = f(inp["state_ssm"][0, s])
        maps.append(m)
    return maps


_NC_CACHE = {}


def kernel(**inputs):
    if "full" not in _NC_CACHE:
        _NC_CACHE["full"] = build_nc(4, True)
    nc = _NC_CACHE["full"][0]
    maps = make_in_maps(inputs)
    res = run_bass_kernel_spmd(nc, maps, core_ids=list(range(NCORE)))
    R = res.results
    y_prompt = np.stack([R[c]["y_prompt"] for c in range(NCORE)], axis=0)
    y_sample = np.concatenate([R[c]["y_sample"].reshape(16, 4, D) for c in range(NCORE)], axis=0)
    conf_p = np.stack([R[c]["o_conf_p"] for c in range(NCORE)], axis=0)[None]
    ssmc_p = np.stack([R[c]["o_ssmc_p"] for c in range(NCORE)], axis=0)[None]
    ssm_p = np.stack([R[c]["o_ssm_p"].reshape(32, 64, 128) for c in range(NCORE)], axis=0)[None]
    conf_s = np.concatenate([R[c]["o_conf_s"] for c in range(NCORE)], axis=0)[None]
    ssmc_s = np.concatenate([R[c]["o_ssmc_s"] for c in range(NCORE)], axis=0)[None]
    ssm_s = np.concatenate([R[c]["o_ssm_s"].reshape(16, 32, 64, 128) for c in range(NCORE)], axis=0)[None]
    return (y_prompt, y_sample, conf_p, ssmc_p, ssm_p, conf_s, ssmc_s, ssm_s)
```
